# Optimizing a Trainium2 kernel written in Bass

```python
import jax, jax.numpy as jnp
from jax import lax
import numpy as np

D_MODEL = 2048
BATCH = 2
SEQ = 4096
DEPTH = 1

N_META = 16
CONV_DIM = D_MODEL // 2
CONV_GROUPS = 16
CONV_K = 3
ATTN_HEADS = 16
HEAD_DIM = (D_MODEL - CONV_DIM) // ATTN_HEADS
ATTN_DIM = ATTN_HEADS * HEAD_DIM
PROJ_DIM = 3 * CONV_DIM + 3 * ATTN_DIM
D_FF = 4 * D_MODEL
GRID_W = 64
WIN_ROWS = 8
WIN_COLS = 16
COL_BLOCK = WIN_COLS
KEY_COL_BLOCK = 2 * WIN_COLS
RMS_EPS = 1e-6
NEG_INF = -1e30

kernel_name = "hybrid_shortconv_natten2d_block"


def rms_norm(x, g):
    xf = x.astype(jnp.float32)
    y = xf * lax.rsqrt(jnp.mean(xf * xf, axis=-1, keepdims=True) + RMS_EPS)
    return (y * g.astype(jnp.float32)).astype(x.dtype)


def short_conv_mixer(b_gate, c_gate, u, conv_w, conv_b):
    v = c_gate * u
    kern = conv_w.astype(v.dtype)[:, None, :]
    y = lax.conv_general_dilated(
        v, kern, window_strides=(1,), padding=[(CONV_K // 2, CONV_K // 2)],
        dimension_numbers=("NWC", "WIO", "NWC"), feature_group_count=v.shape[-1])
    return b_gate * (y + conv_b.astype(v.dtype))


def neighbourhood_attention(q, k, v, rpb):
    B, L, H, dh = q.shape
    S = L - N_META
    rows = S // GRID_W
    kh = min(WIN_ROWS, rows)
    nqb = GRID_W // COL_BLOCK
    scale = dh ** -0.5

    qm, km, vm = q[:, :N_META], k[:, :N_META], v[:, :N_META]
    qg = q[:, N_META:].reshape(B, rows, nqb, COL_BLOCK, H, dh)
    kg = k[:, N_META:].reshape(B, rows, GRID_W, H, dh)
    vg = v[:, N_META:].reshape(B, rows, GRID_W, H, dh)

    r = np.arange(rows)
    row_start = np.clip(r - kh // 2, 0, rows - kh)
    row_idx = row_start[:, None] + np.arange(kh)
    jb = np.arange(nqb)
    blk_start = np.clip(jb * COL_BLOCK - WIN_COLS // 2, 0, GRID_W - KEY_COL_BLOCK)
    col_idx = blk_start[:, None] + np.arange(KEY_COL_BLOCK)
    qc = jb[:, None] * COL_BLOCK + np.arange(COL_BLOCK)
    c0 = np.clip(qc - WIN_COLS // 2, 0, GRID_W - WIN_COLS)
    col_valid = (col_idx[:, None, :] >= c0[:, :, None]) & (col_idx[:, None, :] < c0[:, :, None] + WIN_COLS)
    dr_idx = row_idx - r[:, None] + (WIN_ROWS - 1)
    dc_idx = np.clip(col_idx[:, None, :] - qc[:, :, None] + (WIN_COLS - 1), 0, 2 * WIN_COLS - 2)

    gr = row_idx[:, :, None, None]
    gc = col_idx[None, None, :, :]
    k_win = kg[:, gr, gc]
    v_win = vg[:, gr, gc]

    bias = rpb[:, dr_idx[:, None, None, :, None], dc_idx[None, :, :, None, :]].astype(jnp.float32)
    bias = jnp.where(col_valid[None, None, :, :, None, :], bias, NEG_INF)

    s_win = jnp.einsum('brjihd,brkjmhd->bhrjikm', qg, k_win,
                       preferred_element_type=jnp.float32) * scale + bias[None]
    s_meta = jnp.einsum('brjihd,bnhd->bhrjin', qg, km,
                        preferred_element_type=jnp.float32) * scale
    n_win = kh * KEY_COL_BLOCK
    s = jnp.concatenate([s_meta, s_win.reshape(B, H, rows, nqb, COL_BLOCK, n_win)], axis=-1)
    p = jax.nn.softmax(s, axis=-1).astype(v.dtype)
    p_meta = p[..., :N_META]
    p_win = p[..., N_META:].reshape(B, H, rows, nqb, COL_BLOCK, kh, KEY_COL_BLOCK)
    o_real = (jnp.einsum('bhrjikm,brkjmhd->brjihd', p_win, v_win)
              + jnp.einsum('bhrjin,bnhd->brjihd', p_meta, vm)).reshape(B, S, H, dh)

    s_mm = jnp.einsum('bnhd,bmhd->bhnm', qm, km, preferred_element_type=jnp.float32) * scale
    p_mm = jax.nn.softmax(s_mm, axis=-1).astype(v.dtype)
    o_meta = jnp.einsum('bhnm,bmhd->bnhd', p_mm, vm)
    return jnp.concatenate([o_meta, o_real], axis=1).reshape(B, L, H * dh)


def setup_inputs(seed: int = 0) -> dict:
    key = jax.random.key(seed)
    ks = jax.random.split(key, 16)
    f32 = jnp.float32
    n = lambda k, shape, s: jax.random.normal(k, shape, f32) * s
    return {
        "x": n(ks[0], (BATCH, SEQ, D_MODEL), 1.0),
        "meta_tokens": n(ks[1], (N_META, D_MODEL), 1.0),
        "norm1_g": 1.0 + n(ks[2], (DEPTH, D_MODEL), 0.01),
        "w_in": n(ks[3], (DEPTH, D_MODEL, PROJ_DIM), D_MODEL ** -0.5),
        "conv_w": n(ks[4], (DEPTH, CONV_K, CONV_DIM), CONV_K ** -0.5),
        "conv_b": n(ks[5], (DEPTH, CONV_DIM), 0.01),
        "conv_norm_g": 1.0 + n(ks[6], (DEPTH, CONV_DIM), 0.01),
        "attn_rpb": n(ks[7], (DEPTH, ATTN_HEADS, 2 * WIN_ROWS - 1, 2 * WIN_COLS - 1), 0.02),
        "attn_norm_g": 1.0 + n(ks[8], (DEPTH, ATTN_DIM), 0.01),
        "w_out": n(ks[9], (DEPTH, D_MODEL, D_MODEL), D_MODEL ** -0.5),
        "norm2_g": 1.0 + n(ks[10], (DEPTH, D_MODEL), 0.01),
        "w_up": n(ks[11], (DEPTH, D_MODEL, D_FF), D_MODEL ** -0.5),
        "w_down": n(ks[12], (DEPTH, D_FF, D_MODEL), D_FF ** -0.5),
        "final_norm_g": 1.0 + n(ks[13], (D_MODEL,), 0.01),
    }


def reference(x, meta_tokens, norm1_g, w_in, conv_w, conv_b, conv_norm_g, attn_rpb,
              attn_norm_g, w_out, norm2_g, w_up, w_down, final_norm_g):
    B = x.shape[0]
    meta = jnp.broadcast_to(meta_tokens[None].astype(x.dtype), (B, N_META, D_MODEL))
    h_res = jnp.concatenate([meta, x], axis=1)
    L = h_res.shape[1]
    splits = [CONV_DIM, 2 * CONV_DIM, 3 * CONV_DIM,
              3 * CONV_DIM + ATTN_DIM, 3 * CONV_DIM + 2 * ATTN_DIM]
    for l in range(DEPTH):
        h = rms_norm(h_res, norm1_g[l])
        proj = h @ w_in[l].astype(h.dtype)
        b_gate, c_gate, u, q, k, v = jnp.split(proj, splits, axis=-1)
        y_conv = short_conv_mixer(b_gate, c_gate, u, conv_w[l], conv_b[l])
        y_attn = neighbourhood_attention(
            q.reshape(B, L, ATTN_HEADS, HEAD_DIM), k.reshape(B, L, ATTN_HEADS, HEAD_DIM),
            v.reshape(B, L, ATTN_HEADS, HEAD_DIM), attn_rpb[l])
        mix = jnp.concatenate([rms_norm(y_conv, conv_norm_g[l]),
                               rms_norm(y_attn, attn_norm_g[l])], axis=-1)
        h_res = h_res + mix @ w_out[l].astype(mix.dtype)
        h = rms_norm(h_res, norm2_g[l])
        h_res = h_res + jnp.square(jax.nn.relu(h @ w_up[l].astype(h.dtype))) @ w_down[l].astype(h.dtype)
    out = rms_norm(h_res, final_norm_g)
    return out[:, N_META:]
```

```python
import contextlib
import numpy as np
import concourse.bass as bass
import concourse.mybir as mybir
from concourse.bass_utils import run_bass_kernel_spmd

F32 = mybir.dt.float32
BF16 = mybir.dt.bfloat16
ALU = mybir.AluOpType
AF = mybir.ActivationFunctionType

D_MODEL = 2048
SEQ = 4096
N_META = 16
D_FF = 8192
EPS = 1e-6
NEG = -1e30
NT = 13
TOK = NT * 128
OWN0 = 256
NOWN = 1024


class Op:
    __slots__ = ("eng", "fn", "dma", "key", "sig", "sigidx", "waits", "cnt", "deps")

    def __init__(self, eng, fn, dma, key):
        self.eng = eng
        self.fn = fn
        self.dma = dma
        self.key = key
        self.sig = False
        self.sigidx = 0
        self.waits = {}
        self.cnt = 0
        self.deps = []


class Sched:
    ENGS = ("pe", "act", "dve", "pool", "sp")

    def __init__(self):
        self.ops = {e: [] for e in self.ENGS}
        self.state = {}
        self.dma_cnt = {}
        self.allops = []
        self.final_keys = []

    @staticmethod
    def _norm(r):
        if isinstance(r, str):
            return (r, 0, 1 << 40)
        if len(r) == 2:
            return (r[0], r[1], r[1] + 1)
        return r

    def _add(self, eng, fn, reads, writes, xs, dma, key):
        op = Op(eng, fn, dma, key)
        deps = op.deps
        for mode, regs in (("r", reads), ("x", xs), ("w", writes)):
            for r in regs:
                n, lo, hi = self._norm(r)
                st = self.state.setdefault(n, [])
                for (a, b, o, pm) in st:
                    if not (a < hi and lo < b) or o is op:
                        continue
                    if pm == "r" and mode == "r":
                        continue
                    if pm == "x" and mode == "x" and o.eng == eng and not o.dma and not dma:
                        continue
                    deps.append(o)
                if mode != "r":
                    st[:] = [(a, b, o, pm) for (a, b, o, pm) in st if not (lo <= a and b <= hi) or o is op]
                elif not dma:
                    st[:] = [(a, b, o, pm) for (a, b, o, pm) in st
                             if not (a == lo and b == hi and pm == "r" and o.eng == eng and not o.dma)]
                st.append((lo, hi, op, mode))
        if dma:
            c = self.dma_cnt.get(key, 0) + 16
            self.dma_cnt[key] = c
            op.cnt = c
        self.ops[eng].append(op)
        self.allops.append(op)
        return op

    def op(self, eng, fn, reads=(), writes=(), xs=()):
        return self._add(eng, fn, reads, writes, xs, False, None)

    def dma(self, eng, fn, reads=(), writes=(), key=None):
        return self._add(eng, fn, reads, writes, (), True, key)

    def resolve(self):
        for op in self.allops:
            for d in op.deps:
                if not d.dma:
                    if d.eng == "pe" and op.eng == "pe" and not op.dma:
                        continue
                    d.sig = True
        for e in self.ENGS:
            c = 0
            for op in self.ops[e]:
                if op.sig and not op.dma:
                    c += 1
                    op.sigidx = c
        waited = {e: {} for e in self.ENGS}
        for e in self.ENGS:
            for op in self.ops[e]:
                need = {}
                for d in op.deps:
                    if d.dma:
                        k, v = ("dma", d.key), d.cnt
                    else:
                        if d.eng == "pe" and e == "pe" and not op.dma:
                            continue
                        k, v = ("eng", d.eng), d.sigidx
                    if v > need.get(k, 0):
                        need[k] = v
                for k, v in need.items():
                    if waited[e].get(k, 0) >= v:
                        continue
                    waited[e][k] = v
                    op.waits[k] = v

    def emit(self, nc):
        self.resolve()
        with contextlib.ExitStack() as es:
            sems = {}
            for e in self.ENGS:
                sems[("eng", e)] = es.enter_context(nc.semaphore("s_" + e))
            for k in self.dma_cnt:
                sems[("dma", k)] = es.enter_context(nc.semaphore("d_" + str(k)))
            block = es.enter_context(nc.Block())

            def run(engname, eng):
                for op in self.ops[engname]:
                    for k, v in op.waits.items():
                        eng.wait_ge(sems[k], v)
                    ins = op.fn(eng)
                    if op.dma:
                        ins.then_inc(sems[("dma", op.key)], 16)
                    elif op.sig:
                        ins.then_inc(sems[("eng", engname)], 1)

            @block.sync
            def _(eng):
                run("sp", eng)
                for k in self.final_keys:
                    eng.wait_ge(sems[("dma", k)], self.dma_cnt[k])

            @block.scalar
            def _(eng):
                run("act", eng)

            @block.vector
            def _(eng):
                run("dve", eng)

            @block.gpsimd
            def _(eng):
                run("pool", eng)

            @block.tensor
            def _(eng):
                run("pe", eng)


OFF_RESID = 0
OFF_H1T = 0
OFF_QT = 53248
OFF_KT = 69632
OFF_V = 96256
OFF_YCT = 123392
OFF_YAT = 139776
OFF_W = 156160
ARENA = 205312
OFF_H2T = 65536
OFF_HID = 98304
OFF_XS = 123392
OFF_XN = 139776
OFF_G1B = 147968
OFF_CB = 139776
OFF_YB = 139776 + 8256
OFF_YSQ = OFF_YB + 4096
OFF_TB = 0
OFF_SB = 12288
OFF_PT = 24576
OFF_PTM = 30720
OFF_YATOK = 31744
OFF_YAN = 39936
OFF_GAB = 44032
OFF_XN2 = 98304
OFF_G2B = 106496
OFF_JUNK2 = 114688
OFF_JUNKG = 139264
OFF_RT = 131072
OFF_GFB = 147456


OPTS = "ACEGP"


def build_nc(dbg=False, opts=None):
    import os
    opts = os.environ.get("KOPTS", OPTS) if opts is None else opts
    nc = bass.Bass("TRN2", target_bir_lowering=False)

    def din(name, shape, dt=F32):
        return nc.dram_tensor(name, shape, dt, kind="ExternalInput").ap()

    xl = din("xl", [TOK, D_MODEL])
    w_in = din("w_in", [2048, 6144])
    w_out = din("w_out", [2048, 2048])
    w_up = din("w_up", [2048, 8192])
    w_down = din("w_down", [8192, 2048])
    tab = din("tab", [4, 16, 128, 1536])
    g1b_d = din("g1b", [128, 2048])
    g2b_d = din("g2b", [128, 2048])
    gfb_d = din("gfb", [128, 2048])
    gab_d = din("gab", [128, 1024])
    cols_d = din("cols", [128, 48])
    ident_d = din("ident", [128, 128], BF16)
    out_d = nc.dram_tensor("out", [NOWN, D_MODEL], F32, kind="ExternalOutput").ap()
    dbg_out = {}
    if dbg:
        for nm, n, dt in (("d_h1T", 16 * TOK, BF16), ("d_qT", 8 * 1024, BF16), ("d_kT", 8 * TOK, BF16),
                          ("d_V", NT * 1040, BF16), ("d_ycT", 8 * 1024, BF16), ("d_yaT", 8 * 1024, BF16),
                          ("d_res", 8 * 2048, F32), ("d_stat", 128, F32), ("d_h2T", 16 * 1024, BF16)):
            dbg_out[nm] = nc.dram_tensor(nm, [128, n], dt, kind="ExternalOutput").ap()

    w_in_v = w_in.rearrange("(kc p) n -> p kc n", p=128)
    w_out_v = w_out.rearrange("(kc p) n -> p kc n", p=128)
    w_up_v = w_up.rearrange("(kc p) n -> p kc n", p=128)
    w_down_v = w_down.rearrange("(kc p) n -> p kc n", p=128)

    S = Sched()
    with contextlib.ExitStack() as es:
        arena = es.enter_context(nc.sbuf_tensor("arena", [128, ARENA // 2], BF16))
        ps = es.enter_context(nc.psum_tensor("ps", [128, 8, 512], F32))

        def small(name, shape, dt=F32):
            return es.enter_context(nc.sbuf_tensor(name, shape, dt))

        ident = small("identsb", [128, 128], BF16)
        cols = small("colsb", [128, 48])
        stat = small("stat", [128, 128])
        epsb = small("epsb", [128, 1])
        ones = small("ones", [128, 2], BF16)
        rec = small("rec", [128, 2])
        C_SSQ1, C_RS1, C_SSQC, C_RSC, C_SSQA, C_RSA, C_SSQ2, C_RS2, C_SSQF, C_RSF, C_TMP = 0, 13, 26, 34, 42, 50, 80, 88, 96, 104, 64

        def view(off, dt, dims):
            es_ = 4 if dt == F32 else 2
            n = int(np.prod(dims))
            a = arena[:, off // 2:(off + n * es_) // 2]
            if dt == F32:
                a = a.bitcast(F32)
            if len(dims) == 2:
                a = a.rearrange("p (a b) -> p a b", a=dims[0])
            elif len(dims) == 3:
                a = a.rearrange("p (a b c) -> p a b c", a=dims[0], b=dims[1])
            return a

        def AR(off, nbytes):
            return ("A", off, off + nbytes)

        def PS(b0, b1=None):
            return ("ps", b0, (b0 + 1) if b1 is None else b1)

        h1T = view(OFF_H1T, BF16, [16, TOK])
        R_H1T = AR(OFF_H1T, 16 * TOK * 2)
        qT = view(OFF_QT, BF16, [8, 1024])
        R_QT = AR(OFF_QT, 16384)
        kT = view(OFF_KT, BF16, [8, TOK])
        R_KT = AR(OFF_KT, 8 * TOK * 2)
        V = view(OFF_V, BF16, [NT, 1040])
        R_V = AR(OFF_V, NT * 1040 * 2)
        ycT = view(OFF_YCT, BF16, [8, 1024])
        R_YCT = AR(OFF_YCT, 16384)
        yaT = view(OFF_YAT, BF16, [8, 1024])
        R_YAT = AR(OFF_YAT, 16384)
        resid = view(OFF_RESID, F32, [8, 2048])

        def R_RES(tt):
            return AR(OFF_RESID + tt * 8192, 8192)

        h2T = view(OFF_H2T, BF16, [16, 1024])
        R_H2T = AR(OFF_H2T, 32768)
        hid = [view(OFF_HID + i * 16384, BF16, [8, 1024]) for i in range(2)]
        R_HID = [AR(OFF_HID + i * 16384, 16384) for i in range(2)]

        def psbank(b):
            return ps[:, b, :]

        def psbank_bf(b):
            return ps[:, b, :].bitcast(BF16)

        wstate = {"p": 0}

        def alloc_slots(ns):
            p = wstate["p"]
            if ns == 2 and p % 2 == 1:
                p = (p + 1) % 6
            if p + ns > 6:
                p = 0
            wstate["p"] = (p + ns) % 6
            return p

        def w_load(src, kc0, KC, c0, N):
            nbytes = KC * N * 2
            ns = nbytes // 8192
            assert ns in (1, 2) and nbytes == ns * 8192
            s0 = alloc_slots(ns)
            off = OFF_W + s0 * 8192
            wv = view(off, BF16, [KC, N])
            reg = AR(off, nbytes)
            step = max(1, (1 << 20) // (128 * N * 4))
            for k0 in range(0, KC, step):
                k1 = min(KC, k0 + step)
                S.dma("pool", (lambda e, k0=k0, k1=k1: e.dma_start(out=wv[:, k0:k1, :], in_=src[:, kc0 + k0:kc0 + k1, c0:c0 + N])),
                      writes=[reg], key=f"w{s0}")
            return wv, reg

        def run_tasks(tasks, look=2):
            handles = {}
            n = len(tasks)
            for j in range(min(look, n)):
                handles[j] = tasks[j][0]()
            for k in range(n):
                if k + look < n:
                    handles[k + look] = tasks[k + look][0]()
                tasks[k][1](handles.pop(k))

        bank_rr = {"i": 0}

        def next_bank(nb=6):
            b = bank_rr["i"] % nb
            bank_rr["i"] += 1
            return b

        evac_rr = {"i": 0}

        def evac_eng():
            evac_rr["i"] += 1
            return "act" if evac_rr["i"] % 2 == 0 else "dve"

        def copy_op(eng, out, in_, reads, writes, xs):
            if eng == "act":
                S.op("act", lambda e: e.copy(out=out, in_=in_), reads=reads, writes=writes, xs=xs)
            else:
                S.op("dve", lambda e: e.tensor_copy(out=out, in_=in_), reads=reads, writes=writes, xs=xs)

        def rstd_ops(c_ssq, c_rs, n, scale, C_TMP=112):
            S.op("act", lambda e: e.activation(out=stat[:, C_TMP:C_TMP + n], in_=stat[:, c_ssq:c_ssq + n], func=AF.Sqrt,
                                               scale=scale, bias=epsb[:, 0:1]),
                 reads=[("stat", c_ssq, c_ssq + n), "epsb"], writes=[("stat", C_TMP, C_TMP + n)])
            S.op("dve", lambda e: e.reciprocal(out=stat[:, c_rs:c_rs + n], in_=stat[:, C_TMP:C_TMP + n]),
                 reads=[("stat", C_TMP, C_TMP + n)], writes=[("stat", c_rs, c_rs + n)])

        S.dma("sp", lambda e: e.dma_start(out=ident[:], in_=ident_d[:, :]), writes=["ident"], key="ident")
        S.dma("sp", lambda e: e.dma_start(out=cols[:], in_=cols_d[:, :]), writes=["cols"], key="cols")
        S.op("dve", lambda e: e.memset(stat[:], 0.0), writes=["stat"])
        S.op("dve", lambda e: e.memset(epsb[:], EPS), writes=["epsb"])
        S.op("dve", lambda e: e.memset(ones[:], 1.0), writes=["ones"])

        g1b = view(OFF_G1B, F32, [2048])
        R_G1B = AR(OFF_G1B, 8192)
        S.dma("sp", lambda e: e.dma_start(out=g1b, in_=g1b_d[:, :]), writes=[R_G1B], key="g1b")
        xs_offs = [OFF_XS, OFF_XS + 8192, OFF_V, OFF_V + 8192]
        xs_b = [view(o, F32, [2048]) for o in xs_offs]
        R_XS = [AR(o, 8192) for o in xs_offs]
        xn_b = [view(OFF_XN + i * 4096, BF16, [2048]) for i in range(2)]
        R_XN = [AR(OFF_XN + i * 4096, 4096) for i in range(2)]

        def norm_a(src_ap, src_reg, junk, junk_reg, c_ssq, c_rs, t):
            S.op("act", lambda e: e.activation(out=junk, in_=src_ap, func=AF.Square, accum_out=stat[:, c_ssq + t:c_ssq + t + 1]),
                 reads=[src_reg], writes=[junk_reg, ("stat", c_ssq + t)])
            S.op("act", lambda e: e.activation(out=stat[:, C_TMP + t:C_TMP + t + 1], in_=stat[:, c_ssq + t:c_ssq + t + 1], func=AF.Sqrt,
                                               scale=1.0 / D_MODEL, bias=epsb[:, 0:1]),
                 reads=[("stat", c_ssq + t), "epsb"], writes=[("stat", C_TMP + t)])

        def norm_b(src_ap, src_reg, out_ap, out_reg, gb, gb_reg, c_rs, t):
            S.op("dve", lambda e: e.reciprocal(out=stat[:, c_rs + t:c_rs + t + 1], in_=stat[:, C_TMP + t:C_TMP + t + 1]),
                 reads=[("stat", C_TMP + t)], writes=[("stat", c_rs + t)])
            S.op("dve", lambda e: e.scalar_tensor_tensor(out=out_ap, in0=src_ap, scalar=stat[:, c_rs + t:c_rs + t + 1], in1=gb,
                                                         op0=ALU.mult, op1=ALU.mult),
                 reads=[src_reg, gb_reg, ("stat", c_rs + t)], writes=[out_reg])

        def norm_stage(src_ap, src_reg, xn, xn_reg, gb, gb_reg, c_ssq, c_rs, t):
            norm_a(src_ap, src_reg, xn, xn_reg, c_ssq, c_rs, t)
            norm_b(src_ap, src_reg, xn, xn_reg, gb, gb_reg, c_rs, t)

        def transpose_stage(xn, xn_reg, dstT, dst_regs, tok0):
            for half in range(2):
                b = next_bank(8)
                pb = psbank_bf(b)

                def tr(e, half=half, pb=pb):
                    ins = None
                    for c in range(8):
                        cc = half * 8 + c
                        ins = e.transpose(out=pb[:, c * 128:(c + 1) * 128], in_=xn[:, cc * 128:(cc + 1) * 128], identity=ident[:])
                    return ins
                S.op("pe", tr, reads=[xn_reg, "ident"], writes=[PS(b)])
                copy_op(evac_eng(), dstT[:, half * 8:half * 8 + 8, tok0:tok0 + 128],
                        pb.rearrange("p (a b) -> p a b", a=8), [], dst_regs, [PS(b)])

        a_order = [12] + list(range(12))

        def a_load(idx):
            t = a_order[idx]
            i = idx % 4
            S.dma("sp", (lambda e, t=t, i=i: e.dma_start(out=xs_b[i], in_=xl[t * 128:(t + 1) * 128, :])),
                  writes=[R_XS[i]], key=f"xs{i}")

        junka = view(OFF_V + 16384, BF16, [2048])
        R_JUNKA = AR(OFF_V + 16384, 4096)

        def a_norm_a(idx):
            norm_a(xs_b[idx % 4], R_XS[idx % 4], junka, R_JUNKA, C_SSQ1, C_RS1, a_order[idx])

        def a_norm_b(idx):
            norm_b(xs_b[idx % 4], R_XS[idx % 4], xn_b[idx % 2], R_XN[idx % 2], g1b, R_G1B, C_RS1, a_order[idx])

        def a_tr(idx):
            t = a_order[idx]
            transpose_stage(xn_b[idx % 2], R_XN[idx % 2], h1T, [("h1t", t)], t * 128)

        for idx in range(4):
            a_load(idx)
        a_norm_a(0)
        a_norm_a(1)
        a_norm_b(0)
        a_load(4)
        for idx in range(NT):
            if idx + 2 < NT:
                a_norm_a(idx + 2)
            if idx + 1 < NT:
                a_norm_b(idx + 1)
            if idx + 5 < NT:
                a_load(idx + 5)
            a_tr(idx)

        if dbg:
            S.dma("sp", lambda e: e.dma_start(out=dbg_out["d_h1T"][:, :], in_=arena[:, OFF_H1T // 2:OFF_H1T // 2 + 16 * TOK]),
                  reads=[R_H1T, "h1t"], writes=["dbg0"], key="dbg0")
            S.final_keys.append("dbg0")

        S.op("dve", lambda e: e.memset(arena[:, OFF_V // 2:OFF_V // 2 + NT * 1040], 1.0), writes=[R_V])
        Vh = V.rearrange("p t (h d) -> p t h d", h=16)

        def fm_group(wv, wreg, oc_local, rhs_ap, n, t0):
            b = next_bank(6)
            h1r = ("h1t", t0 // 128, (t0 + n - 1) // 128 + 1)

            def mm(e, b=b):
                ins = None
                for kc in range(16):
                    ins = e.matmul(ps[:, b, 0:n], lhsT=wv[:, kc, oc_local * 128:(oc_local + 1) * 128], rhs=rhs_ap(kc),
                                   start=(kc == 0), stop=(kc == 15))
                return ins
            S.op("pe", mm, reads=[wreg, R_H1T, h1r], writes=[PS(b)])
            return b

        tasks = []
        for pj in range(2):
            def ld(pj=pj):
                return w_load(w_in_v, 0, 16, 4096 + pj * 512, 512)

            def cp(h, pj=pj):
                wv, wreg = h
                for blk in (3, 0, 1, 2):
                    for ol in range(4):
                        oc = pj * 4 + ol
                        t0 = blk * 512
                        n = 512 if blk < 3 else 128
                        b = fm_group(wv, wreg, ol, lambda kc, t0=t0, n=n: h1T[:, kc, t0:t0 + n], n, t0)
                        copy_op(evac_eng(), kT[:, oc, t0:t0 + n], ps[:, b, 0:n], [], [R_KT], [PS(b)])
            tasks.append((ld, cp))
        for pj in range(2):
            def ld(pj=pj):
                return w_load(w_in_v, 0, 16, 3072 + pj * 512, 512)

            def cp(h, pj=pj):
                wv, wreg = h
                for blk in range(2):
                    for ol in range(4):
                        oc = pj * 4 + ol
                        t0 = OWN0 + blk * 512
                        b = fm_group(wv, wreg, ol, lambda kc, t0=t0: h1T[:, kc, t0:t0 + 512], 512, t0)
                        S.op("act", (lambda e, b=b, oc=oc, blk=blk: e.activation(out=qT[:, oc, blk * 512:(blk + 1) * 512], in_=ps[:, b, :],
                                                                               func=AF.Copy, scale=0.125)),
                             writes=[R_QT], xs=[PS(b)])
            tasks.append((ld, cp))
        for pj in range(2):
            def ld(pj=pj):
                return w_load(w_in_v, 0, 16, 5120 + pj * 512, 512)

            def cp(h, pj=pj):
                wv, wreg = h
                for t in range(NT):
                    b = next_bank(6)

                    def mm(e, b=b, t=t):
                        ins = None
                        for kc in range(16):
                            ins = e.matmul(ps[:, b, :], lhsT=h1T[:, kc, t * 128:(t + 1) * 128], rhs=wv[:, kc, :],
                                           start=(kc == 0), stop=(kc == 15))
                        return ins
                    S.op("pe", mm, reads=[wreg, R_H1T, ("h1t", t)], writes=[PS(b)])
                    copy_op(evac_eng(), Vh[:, t, pj * 8:pj * 8 + 8, 0:64], ps[:, b, :].rearrange("p (h d) -> p h d", h=8),
                            [], [R_V], [PS(b)])
            tasks.append((ld, cp))

        cb = [view(OFF_CB + i * 4128, F32, [1026]) for i in range(2)]
        R_CB = [AR(OFF_CB + i * 4128, 4104) for i in range(2)]
        yb = view(OFF_YB, F32, [1024])
        R_YB = AR(OFF_YB, 4096)
        ysq = view(OFF_YSQ, BF16, [1024])
        R_YSQ = AR(OFF_YSQ, 2048)

        def halo_group(wv, wreg, ol, col0):
            def mm(e):
                ins = None
                for kc in range(16):
                    ins = e.matmul(ps[:, 6, col0:col0 + 2], lhsT=wv[:, kc, ol * 128:(ol + 1) * 128],
                                   rhs=h1T[:, kc, 255:1281:1025], start=(kc == 0), stop=(kc == 15))
                return ins
            S.op("pe", mm, reads=[wreg, R_H1T, ("h1t", 1), ("h1t", 10)], writes=[PS(6)])

        for pj in range(4):
            def ld_c(pj=pj):
                return w_load(w_in_v, 0, 16, 1024 + pj * 256, 256)

            def cp_c(h, pj=pj):
                wv, wreg = h
                for ol in range(2):
                    for blk in range(2):
                        t0 = OWN0 + blk * 512
                        b = fm_group(wv, wreg, ol, lambda kc, t0=t0: h1T[:, kc, t0:t0 + 512], 512, t0)
                        S.op("act", (lambda e, b=b, ol=ol, blk=blk: e.copy(out=cb[ol][:, 1 + blk * 512:1 + (blk + 1) * 512], in_=ps[:, b, :])),
                             writes=[R_CB[ol]], xs=[PS(b)])
                    halo_group(wv, wreg, ol, 0)
                    S.op("act", (lambda e, ol=ol: e.copy(out=cb[ol][:, 0:1026:1025], in_=ps[:, 6, 0:2])),
                         writes=[R_CB[ol]], xs=[PS(6)])

            def ld_u(pj=pj):
                return w_load(w_in_v, 0, 16, 2048 + pj * 256, 256)

            def cp_u(h, pj=pj):
                wv, wreg = h
                for ol in range(2):
                    for blk in range(2):
                        t0 = OWN0 + blk * 512
                        b = fm_group(wv, wreg, ol, lambda kc, t0=t0: h1T[:, kc, t0:t0 + 512], 512, t0)
                        S.op("dve", (lambda e, b=b, ol=ol, blk=blk: e.tensor_tensor(
                            out=cb[ol][:, 1 + blk * 512:1 + (blk + 1) * 512], in0=cb[ol][:, 1 + blk * 512:1 + (blk + 1) * 512],
                            in1=ps[:, b, :], op=ALU.mult)), writes=[R_CB[ol]], xs=[PS(b)])
                    halo_group(wv, wreg, ol, 8)
                    S.op("dve", (lambda e, ol=ol: e.tensor_tensor(out=cb[ol][:, 0:1026:1025], in0=cb[ol][:, 0:1026:1025],
                                                                 in1=ps[:, 6, 8:10], op=ALU.mult)),
                         writes=[R_CB[ol]], xs=[PS(6)])

            def ld_b(pj=pj):
                return w_load(w_in_v, 0, 16, 0 + pj * 256, 256)

            def cp_b(h, pj=pj):
                wv, wreg = h
                for ol in range(2):
                    ci = pj * 2 + ol
                    c_w0, c_w1, c_w2, c_bias, c_g = 8 + 3 * ci, 9 + 3 * ci, 10 + 3 * ci, 32 + ci, ci
                    S.op("dve", (lambda e, ol=ol, c_w1=c_w1, c_bias=c_bias: e.tensor_scalar(
                        out=yb, in0=cb[ol][:, 1:1025], scalar1=cols[:, c_w1:c_w1 + 1], scalar2=cols[:, c_bias:c_bias + 1],
                        op0=ALU.mult, op1=ALU.add)), reads=[R_CB[ol], "cols"], writes=[R_YB])
                    S.op("dve", (lambda e, ol=ol, c_w0=c_w0: e.scalar_tensor_tensor(
                        out=yb, in0=cb[ol][:, 0:1024], scalar=cols[:, c_w0:c_w0 + 1], in1=yb, op0=ALU.mult, op1=ALU.add)),
                        reads=[R_CB[ol], "cols"], writes=[R_YB])
                    S.op("dve", (lambda e, ol=ol, c_w2=c_w2: e.scalar_tensor_tensor(
                        out=yb, in0=cb[ol][:, 2:1026], scalar=cols[:, c_w2:c_w2 + 1], in1=yb, op0=ALU.mult, op1=ALU.add)),
                        reads=[R_CB[ol], "cols"], writes=[R_YB])
                    for blk in range(2):
                        t0 = OWN0 + blk * 512
                        b = fm_group(wv, wreg, ol, lambda kc, t0=t0: h1T[:, kc, t0:t0 + 512], 512, t0)
                        S.op("dve", (lambda e, b=b, blk=blk: e.tensor_tensor(
                            out=yb[:, blk * 512:(blk + 1) * 512], in0=yb[:, blk * 512:(blk + 1) * 512], in1=ps[:, b, :], op=ALU.mult)),
                            writes=[R_YB], xs=[PS(b)])
                    S.op("act", (lambda e, ci=ci, c_g=c_g: e.activation(out=ycT[:, ci, :], in_=yb, func=AF.Copy,
                                                                         scale=cols[:, c_g:c_g + 1])),
                         reads=[R_YB, "cols"], writes=[R_YCT])
                    S.op("act", lambda e: e.activation(out=ysq, in_=yb, func=AF.Square), reads=[R_YB], writes=[R_YSQ])

                    def mm(e):
                        ins = None
                        for tt in range(8):
                            ins = e.matmul(ps[:, 7, tt:tt + 1], lhsT=ysq[:, tt * 128:(tt + 1) * 128], rhs=ones[:, 0:1],
                                           start=True, stop=True)
                        return ins
                    S.op("pe", mm, reads=[R_YSQ, "ones"], writes=[PS(7)])
                    S.op("dve", lambda e: e.tensor_tensor(out=stat[:, C_SSQC:C_SSQC + 8], in0=stat[:, C_SSQC:C_SSQC + 8],
                                                          in1=ps[:, 7, 0:8], op=ALU.add),
                         writes=[("stat", C_SSQC, C_SSQC + 8)], xs=[PS(7)])
            tasks.append((ld_c, cp_c))
            tasks.append((ld_u, cp_u))
            tasks.append((ld_b, cp_b))

        run_tasks(tasks)
        rstd_ops(C_SSQC, C_RSC, 8, 1.0 / 1024)

        if dbg:
            for i, (nm, off, n) in enumerate((("d_qT", OFF_QT, 8192), ("d_kT", OFF_KT, 8 * TOK), ("d_V", OFF_V, NT * 1040),
                                              ("d_ycT", OFF_YCT, 8192))):
                S.dma("sp", (lambda e, nm=nm, off=off, n=n: e.dma_start(out=dbg_out[nm][:, :], in_=arena[:, off // 2:off // 2 + n])),
                      reads=[AR(off, n * 2)], writes=[f"dbg{i + 1}"], key=f"dbg{i + 1}")
                S.final_keys.append(f"dbg{i + 1}")

        tb = [view(OFF_TB + i * 6144, F32, [1536]) for i in range(4)]
        R_TB = [AR(OFF_TB + i * 6144, 6144) for i in range(4)]
        sb = [view(OFF_SB + i * 6144, F32, [1536]) for i in range(2)]
        R_SB = [AR(OFF_SB + i * 6144, 6144) for i in range(2)]
        PT = [view(OFF_PT + i * 3072, BF16, [1536]) for i in range(2)]
        R_PT = [AR(OFF_PT + i * 3072, 3072) for i in range(2)]
        ptm_offs = [OFF_PTM, OFF_PTM + 512, OFF_GAB + 4096]
        PTm = [view(o, BF16, [256]) for o in ptm_offs]
        R_PTM = [AR(o, 512) for o in ptm_offs]
        yatok = view(OFF_YATOK, F32, [2, 1024])
        R_YATOK = AR(OFF_YATOK, 8192)
        yan = view(OFF_YAN, BF16, [2, 1024])
        R_YAN = AR(OFF_YAN, 4096)
        gab = view(OFF_GAB, F32, [1024])
        R_GAB = AR(OFF_GAB, 4096)
        S.dma("sp", lambda e: e.dma_start(out=gab, in_=gab_d[:, :]), writes=[R_GAB], key="gab")

        def c_tab(n_):
            m, h = n_ // 16, n_ % 16
            S.dma("sp", (lambda e: e.dma_start(out=tb[n_ % 4], in_=tab[m, h])), writes=[R_TB[n_ % 4]], key=f"tb{n_ % 4}")

        def c_stage1(n_):
            m, h = n_ // 16, n_ % 16
            par = n_ % 2
            p3 = n_ % 3
            t4 = n_ % 4
            hc, hb = h // 2, 64 * (h % 2)

            def qk(e):
                ins = None
                rhs = qT[hb:hb + 64, hc, m * 256:(m + 1) * 256]
                ins = e.matmul(ps[0:16, 6, 0:256], lhsT=kT[hb:hb + 64, hc, 1536:1552], rhs=rhs, start=True, stop=True)
                for po in range(6):
                    kt = 2 * m + po
                    ins = e.matmul(ps[:, 3 * par + po // 2, (po % 2) * 256:(po % 2) * 256 + 256],
                                   lhsT=kT[hb:hb + 64, hc, kt * 128:(kt + 1) * 128], rhs=rhs, start=True, stop=True)
                return ins
            S.op("pe", qk, reads=[R_KT, R_QT], writes=[PS(3 * par, 3 * par + 3), PS(6)])
            S.op("act", (lambda e: e.activation(out=PTm[p3][0:16, :], in_=ps[0:16, 6, 0:256], func=AF.Exp)),
                 writes=[R_PTM[p3]], xs=[PS(6)])
            S.op("dve", (lambda e: e.tensor_tensor(out=ps[:, 3 * par:3 * par + 3, :], in0=ps[:, 3 * par:3 * par + 3, :],
                                                   in1=tb[t4].rearrange("p (a b) -> p a b", a=3), op=ALU.add)),
                 reads=[R_TB[t4]], xs=[PS(3 * par, 3 * par + 3)])
            if "P" not in opts:
                c_exp(n_)

        def c_exp(n_):
            par = n_ % 2
            S.op("act", (lambda e: e.activation(out=PT[par].rearrange("p (a b) -> p a b", a=3), in_=ps[:, 3 * par:3 * par + 3, :],
                                                func=AF.Exp)),
                 writes=[R_PT[par]], xs=[PS(3 * par, 3 * par + 3)])

        def c_stage1b(n_):
            par = n_ % 2
            S.op("act", (lambda e: e.activation(out=PT[par], in_=sb[par], func=AF.Exp)),
                 reads=[R_SB[par]], writes=[R_PT[par]])

        epi_pending = []

        def c_stage2(n_):
            m, h = n_ // 16, n_ % 16
            par = n_ % 2
            p3 = n_ % 3
            if h == 3 and epi_pending:
                epi_pending.pop(0)()

            def pv(e):
                ins = None
                for half in range(2):
                    o = ps[:, 7, half * 65:half * 65 + 65]
                    for po in range(6):
                        ins = e.matmul(o, lhsT=PT[par][:, po * 256 + half * 128:po * 256 + half * 128 + 128],
                                       rhs=V[:, 2 * m + po, h * 65:(h + 1) * 65], start=(po == 0), stop=False)
                    ins = e.matmul(o, lhsT=PTm[p3][0:16, half * 128:(half + 1) * 128], rhs=V[0:16, 12, h * 65:(h + 1) * 65],
                                   start=False, stop=True)
                return ins
            S.op("pe", pv, reads=[R_PT[par], R_PTM[p3], R_V], writes=[PS(7)])
            S.op("dve", lambda e: e.reciprocal(out=rec[:, 0:2], in_=ps[:, 7, 64:130:65]), writes=["rec"], xs=[PS(7)])
            S.op("dve", (lambda e: e.tensor_tensor(
                out=yatok[:, :, h * 64:(h + 1) * 64],
                in0=ps[:, 7, 0:130].rearrange("p (a b) -> p a b", a=2)[:, :, 0:64],
                in1=rec[:, 0:2].unsqueeze(2).to_broadcast([128, 2, 64]), op=ALU.mult)),
                reads=["rec"], writes=[R_YATOK], xs=[PS(7)])
            if h != 15:
                return
            for half in range(2):
                tt = 2 * m + half
                S.op("act", (lambda e, half=half, tt=tt: e.activation(out=yan[:, half, :], in_=yatok[:, half, :], func=AF.Square,
                                                                     accum_out=stat[:, C_SSQA + tt:C_SSQA + tt + 1])),
                     reads=[R_YATOK], writes=[R_YAN, ("stat", C_SSQA + tt)])
            rstd_ops(C_SSQA + 2 * m, C_RSA + 2 * m, 2, 1.0 / 1024)
            for half in range(2):
                tt = 2 * m + half
                S.op("dve", (lambda e, half=half, tt=tt: e.scalar_tensor_tensor(
                    out=yan[:, half, :], in0=yatok[:, half, :], scalar=stat[:, C_RSA + tt:C_RSA + tt + 1], in1=gab,
                    op0=ALU.mult, op1=ALU.mult)), reads=[R_YATOK, R_GAB, ("stat", C_RSA + tt)], writes=[R_YAN])
            def epi2(m=m):
                for half in range(2):
                    tt = 2 * m + half
                    b = 6 + half
                    pb = psbank_bf(b)

                    def tr(e, half=half, pb=pb):
                        ins = None
                        for c in range(8):
                            ins = e.transpose(out=pb[:, c * 128:(c + 1) * 128], in_=yan[:, half, c * 128:(c + 1) * 128],
                                              identity=ident[:])
                        return ins
                    S.op("pe", tr, reads=[R_YAN, "ident"], writes=[PS(b)])
                    S.op("act", (lambda e, tt=tt, pb=pb: e.copy(out=yaT[:, 0:8, tt * 128:(tt + 1) * 128],
                                                               in_=pb.rearrange("p (a b) -> p a b", a=8))),
                         writes=[R_YAT], xs=[PS(b)])
            epi_pending.append(epi2)

        if "Q" in opts:
            c_stage1(0)
            c_stage1(1)
            c_stage1b(0)
            for n_ in range(64):
                if n_ + 2 < 64:
                    c_stage1(n_ + 2)
                if n_ + 1 < 64:
                    c_stage1b(n_ + 1)
                c_stage2(n_)
        elif "P" in opts:
            for n_ in range(4):
                c_tab(n_)
            c_stage1(0)
            c_exp(0)
            c_stage1(1)
            c_exp(1)
            for n_ in range(64):
                if n_ + 4 < 64:
                    c_tab(n_ + 4)
                if n_ + 2 < 64:
                    c_stage1(n_ + 2)
                c_stage2(n_)
                if n_ + 2 < 64:
                    c_exp(n_ + 2)
            while epi_pending:
                epi_pending.pop(0)()
        else:
            for n_ in range(3):
                c_tab(n_)
            c_stage1(0)
            for n_ in range(64):
                if n_ + 3 < 64:
                    c_tab(n_ + 3)
                if n_ + 1 < 64:
                    c_stage1(n_ + 1)
                c_stage2(n_)

        if dbg:
            S.dma("sp", lambda e: e.dma_start(out=dbg_out["d_yaT"][:, :], in_=arena[:, OFF_YAT // 2:OFF_YAT // 2 + 8192]),
                  reads=[R_YAT], writes=["dbg5"], key="dbg5")
            S.final_keys.append("dbg5")

        for tt in range(8):
            S.dma("sp", (lambda e, tt=tt: e.dma_start(out=resid[:, tt, :], in_=xl[OWN0 + tt * 128:OWN0 + (tt + 1) * 128, :])),
                  writes=[R_RES(tt)], key=f"res{tt}")
        g2b = view(OFF_G2B, F32, [2048])
        R_G2B = AR(OFF_G2B, 8192)
        S.dma("sp", lambda e: e.dma_start(out=g2b, in_=g2b_d[:, :]), writes=[R_G2B], key="g2b")
        xn2 = [view(OFF_XN2 + i * 4096, BF16, [2048]) for i in range(2)]
        R_XN2 = [AR(OFF_XN2 + i * 4096, 4096) for i in range(2)]
        junk2 = view(OFF_JUNK2, BF16, [2048])
        R_JUNK2 = AR(OFF_JUNK2, 4096)

        def e_norm_a(tt):
            norm_a(resid[:, tt, :], R_RES(tt), junk2, R_JUNK2, C_SSQ2, C_RS2, tt)

        def e_norm_b(tt):
            norm_b(resid[:, tt, :], R_RES(tt), xn2[tt % 2], R_XN2[tt % 2], g2b, R_G2B, C_RS2, tt)

        def e_transpose(tt):
            regs = [("h2t", tt)] + [AR(OFF_H2T + (c * 1024 + tt * 128) * 2, 256) for c in range(16)]
            transpose_stage(xn2[tt % 2], R_XN2[tt % 2], h2T, regs, tt * 128)

        tasks = []
        for cbk in range(4):
            def ld(cbk=cbk):
                return w_load(w_out_v, 0, 16, cbk * 512, 512)

            def cp(h, cbk=cbk):
                wv, wreg = h
                for tt in range(8):
                    ba = next_bank(8)
                    bb = next_bank(8)

                    def mm(e, tt=tt, ba=ba, bb=bb):
                        ins = None
                        for kc in range(8):
                            ins = e.matmul(ps[:, ba, :], lhsT=ycT[:, kc, tt * 128:(tt + 1) * 128], rhs=wv[:, kc, :],
                                           start=(kc == 0), stop=(kc == 7))
                        for kc in range(8):
                            ins = e.matmul(ps[:, bb, :], lhsT=yaT[:, kc, tt * 128:(tt + 1) * 128], rhs=wv[:, 8 + kc, :],
                                           start=(kc == 0), stop=(kc == 7))
                        return ins
                    S.op("pe", mm, reads=[wreg, R_YCT, R_YAT], writes=[PS(ba), PS(bb)])
                    rs = resid[:, tt, cbk * 512:(cbk + 1) * 512]
                    S.op("dve", (lambda e, rs=rs, tt=tt, ba=ba: e.scalar_tensor_tensor(
                        out=rs, in0=ps[:, ba, :], scalar=stat[:, C_RSC + tt:C_RSC + tt + 1], in1=rs, op0=ALU.mult, op1=ALU.add)),
                        reads=[("stat", C_RSC + tt)], writes=[R_RES(tt)], xs=[PS(ba)])
                    S.op("dve", (lambda e, rs=rs, bb=bb: e.tensor_tensor(out=rs, in0=rs, in1=ps[:, bb, :], op=ALU.add)),
                         writes=[R_RES(tt)], xs=[PS(bb)])
                    if cbk == 3 and "E" in opts:
                        e_norm_a(tt)
                        if tt >= 1:
                            e_norm_b(tt - 1)
                        if tt >= 2:
                            e_transpose(tt - 2)
                if cbk == 3 and "E" in opts:
                    e_norm_b(7)
                    e_transpose(6)
                    e_transpose(7)
            tasks.append((ld, cp))
        run_tasks(tasks)
        if "E" not in opts:
            for tt in range(8):
                e_norm_a(tt)
                e_norm_b(tt)
                e_transpose(tt)

        gfb = view(OFF_GFB, F32, [2048])
        R_GFB = AR(OFF_GFB, 8192)
        S.dma("sp", lambda e: e.dma_start(out=gfb, in_=gfb_d[:, :]), writes=[R_GFB], key="gfb")

        if dbg:
            S.dma("sp", lambda e: e.dma_start(out=dbg_out["d_res"][:, :], in_=arena[:, 0:32768].bitcast(F32)),
                  reads=[AR(0, 65536)], writes=["dbg6"], key="dbg6")
            S.final_keys.append("dbg6")
            S.dma("sp", lambda e: e.dma_start(out=dbg_out["d_h2T"][:, :], in_=arena[:, OFF_H2T // 2:OFF_H2T // 2 + 16384]),
                  reads=[R_H2T, "h2t"], writes=["dbg7"], key="dbg7")
            S.final_keys.append("dbg7")

        rt = [view(OFF_RT + i * 2048, F32, [512]) for i in range(4)]
        R_RT = [AR(OFF_RT + i * 2048, 2048) for i in range(4)]
        rt_rr = {"i": 0}
        tasks = []
        for p in range(8):
            pp = p % 2
            for j in range(2):
                def ld(p=p, j=j):
                    return w_load(w_up_v, 0, 16, p * 1024 + j * 512, 512)

                def cp(h, p=p, j=j, pp=pp):
                    wv, wreg = h
                    for fl in range(4):
                        f = j * 4 + fl
                        for blk in range(2):
                            b = next_bank(8)

                            def mm(e, b=b, fl=fl, blk=blk):
                                ins = None
                                for kc in range(16):
                                    ins = e.matmul(ps[:, b, :], lhsT=wv[:, kc, fl * 128:(fl + 1) * 128],
                                                   rhs=h2T[:, kc, blk * 512:(blk + 1) * 512], start=(kc == 0), stop=(kc == 15))
                                return ins
                            S.op("pe", mm, reads=[wreg, ("h2t", 4 * blk, 4 * blk + 4)], writes=[PS(b)])
                            ri = rt_rr["i"] % 4
                            rt_rr["i"] += 1
                            S.op("act", (lambda e, b=b, ri=ri: e.activation(out=rt[ri], in_=ps[:, b, :], func=AF.Relu)),
                                 writes=[R_RT[ri]], xs=[PS(b)])
                            S.op("dve", (lambda e, ri=ri, f=f, blk=blk, pp=pp: e.tensor_tensor(
                                out=hid[pp][:, f, blk * 512:(blk + 1) * 512], in0=rt[ri], in1=rt[ri], op=ALU.mult)),
                                reads=[R_RT[ri]], writes=[R_HID[pp]])
                tasks.append((ld, cp))
            for j in range(2):
                def ld(p=p, j=j):
                    return w_load(w_down_v, p * 8, 8, j * 1024, 1024)

                def cp(h, p=p, j=j, pp=pp):
                    wv, wreg = h
                    for tt in range(8):
                        for cbl in range(2):
                            b = next_bank(8)

                            def mm(e, b=b, tt=tt, cbl=cbl):
                                ins = None
                                for kc in range(8):
                                    ins = e.matmul(ps[:, b, :], lhsT=hid[pp][:, kc, tt * 128:(tt + 1) * 128],
                                                   rhs=wv[:, kc, cbl * 512:(cbl + 1) * 512], start=(kc == 0), stop=(kc == 7))
                                return ins
                            S.op("pe", mm, reads=[wreg, R_HID[pp]], writes=[PS(b)])
                            c0 = j * 1024 + cbl * 512
                            rs = resid[:, tt, c0:c0 + 512]
                            S.op("dve", (lambda e, rs=rs, b=b: e.tensor_tensor(out=rs, in0=rs, in1=ps[:, b, :], op=ALU.add)),
                                 writes=[R_RES(tt)], xs=[PS(b)])
                        if p == 7 and j == 1 and "G" in opts:
                            g_norm_a(tt)
                            if tt >= 1:
                                g_norm_b(tt - 1)
                    if p == 7 and j == 1 and "G" in opts:
                        g_norm_b(7)
                tasks.append((ld, cp))

        junkg = view(OFF_JUNKG, BF16, [2048])
        R_JUNKG = AR(OFF_JUNKG, 4096)

        def g_norm_a(tt):
            norm_a(resid[:, tt, :], R_RES(tt), junkg, R_JUNKG, C_SSQF, C_RSF, tt)

        def g_norm_b(tt):
            norm_b(resid[:, tt, :], R_RES(tt), resid[:, tt, :], R_RES(tt), gfb, R_GFB, C_RSF, tt)
            S.dma("sp", (lambda e: e.dma_start(out=out_d[tt * 128:(tt + 1) * 128, :], in_=resid[:, tt, :])),
                  reads=[R_RES(tt)], writes=[("outd", tt)], key=f"out{tt % 2}")

        run_tasks(tasks)
        if "G" not in opts:
            for tt in range(8):
                g_norm_a(tt)
                g_norm_b(tt)
        S.final_keys += ["out0", "out1"]
        if dbg:
            S.dma("sp", lambda e: e.dma_start(out=dbg_out["d_stat"][:, :], in_=stat[:]), reads=["stat"], writes=["dbg8"], key="dbg8")
            S.final_keys.append("dbg8")
        S.emit(nc)
    return nc


def _bias_table(rpb, i):
    R0 = 16 * i
    m = np.arange(4)[:, None, None, None, None]
    kp = np.arange(128)[None, :, None, None, None]
    po = np.arange(6)[None, None, :, None, None]
    a = np.arange(4)[None, None, None, :, None]
    qc = np.arange(64)[None, None, None, None, :]
    kb, kc = kp // 64, kp % 64
    gk = R0 - 4 + 4 * m + 2 * po + kb
    r = R0 + 4 * m + a
    rs = np.clip(r - 4, 0, 56)
    c0 = np.clip(qc - 8, 0, 48)
    valid = (gk >= 0) & (gk < 64) & (gk >= rs) & (gk < rs + 8) & (kc >= c0) & (kc < c0 + 16)
    dr = np.clip(gk - r + 7, 0, 14)
    dc = np.clip(kc - qc + 15, 0, 30)
    shape = np.broadcast_shapes(valid.shape, dr.shape, dc.shape)
    valid = np.broadcast_to(valid, shape)
    dr = np.broadcast_to(dr, shape)
    dc = np.broadcast_to(dc, shape)
    t = rpb[:, dr, dc]
    t = np.where(valid[None], t, np.float32(NEG)).astype(np.float32)
    t = np.ascontiguousarray(t.transpose(1, 0, 2, 3, 4, 5)).reshape(4, 16, 128, 1536)
    return t


def _prep_inputs(x, meta_tokens, norm1_g, w_in, conv_w, conv_b, conv_norm_g, attn_rpb, attn_norm_g, w_out, norm2_g,
                 w_up, w_down, final_norm_g):
    import ml_dtypes
    f32 = np.float32
    x = np.asarray(x, f32)
    meta = np.asarray(meta_tokens, f32)
    shared = {
        "w_in": np.ascontiguousarray(np.asarray(w_in, f32)[0]),
        "w_out": np.ascontiguousarray(np.asarray(w_out, f32)[0]),
        "w_up": np.ascontiguousarray(np.asarray(w_up, f32)[0]),
        "w_down": np.ascontiguousarray(np.asarray(w_down, f32)[0]),
        "g1b": np.ascontiguousarray(np.broadcast_to(np.asarray(norm1_g, f32)[0][None, :], (128, 2048))),
        "g2b": np.ascontiguousarray(np.broadcast_to(np.asarray(norm2_g, f32)[0][None, :], (128, 2048))),
        "gfb": np.ascontiguousarray(np.broadcast_to(np.asarray(final_norm_g, f32)[None, :], (128, 2048))),
        "gab": np.ascontiguousarray(np.broadcast_to(np.asarray(attn_norm_g, f32)[0][None, :], (128, 1024))),
        "ident": np.eye(128, dtype=f32).astype(ml_dtypes.bfloat16),
    }
    cols = np.zeros((128, 48), f32)
    cols[:, 0:8] = np.asarray(conv_norm_g, f32)[0].reshape(8, 128).T
    cw = np.asarray(conv_w, f32)[0]
    cols[:, 8:32] = cw.reshape(3, 8, 128).transpose(2, 1, 0).reshape(128, 24)
    cols[:, 32:40] = np.asarray(conv_b, f32)[0].reshape(8, 128).T
    shared["cols"] = cols
    rpb = np.asarray(attn_rpb, f32)[0]
    tabs = [_bias_table(rpb, i) for i in range(4)]
    in_maps = []
    for c in range(8):
        b, i = c // 4, c % 4
        xl = np.zeros((TOK, D_MODEL), f32)
        g0 = 16 * i - 4
        lo, hi = max(g0, 0), min(g0 + 24, 64)
        xl[(lo - g0) * 64:(hi - g0) * 64] = x[b, lo * 64:hi * 64]
        if i == 0:
            xl[255] = meta[15]
        xl[1536:1552] = meta
        mp = dict(shared)
        mp["xl"] = xl
        mp["tab"] = tabs[i]
        in_maps.append(mp)
    return in_maps


_NC = {}


def kernel(**inputs):
    in_maps = _prep_inputs(**inputs)
    if "nc" not in _NC:
        _NC["nc"] = build_nc()
    res = run_bass_kernel_spmd(_NC["nc"], in_maps, core_ids=list(range(8)))
    out = np.empty((2, SEQ, D_MODEL), np.float32)
    for c in range(8):
        b, i = c // 4, c % 4
        out[b, i * 1024:(i + 1) * 1024] = np.asarray(res.results[c]["out"], np.float32)
    return out
```

```python
import contextlib
import numpy as np
import concourse.bass as bass
import concourse.mybir as mybir
from concourse.bass_utils import run_bass_kernel_spmd

F32 = mybir.dt.float32
BF16 = mybir.dt.bfloat16
ALU = mybir.AluOpType
AF = mybir.ActivationFunctionType

D_MODEL = 2048
SEQ = 4096
N_META = 16
D_FF = 8192
EPS = 1e-6
NEG = -1e30
NT = 13
TOK = NT * 128
OWN0 = 256
NOWN = 1024


class Op:
    __slots__ = ("eng", "fn", "dma", "key", "sig", "sigidx", "waits", "cnt", "deps")

    def __init__(self, eng, fn, dma, key):
        self.eng = eng
        self.fn = fn
        self.dma = dma
        self.key = key
        self.sig = False
        self.sigidx = 0
        self.waits = {}
        self.cnt = 0
        self.deps = []


class Sched:
    ENGS = ("pe", "act", "dve", "pool", "sp")

    def __init__(self):
        self.ops = {e: [] for e in self.ENGS}
        self.state = {}
        self.dma_cnt = {}
        self.allops = []
        self.final_keys = []

    @staticmethod
    def _norm(r):
        if isinstance(r, str):
            return (r, 0, 1 << 40)
        if len(r) == 2:
            return (r[0], r[1], r[1] + 1)
        return r

    def _add(self, eng, fn, reads, writes, xs, dma, key):
        op = Op(eng, fn, dma, key)
        deps = op.deps
        for mode, regs in (("r", reads), ("x", xs), ("w", writes)):
            for r in regs:
                n, lo, hi = self._norm(r)
                st = self.state.setdefault(n, [])
                for (a, b, o, pm) in st:
                    if not (a < hi and lo < b) or o is op:
                        continue
                    if pm == "r" and mode == "r":
                        continue
                    if pm == "x" and mode == "x" and o.eng == eng and not o.dma and not dma:
                        continue
                    deps.append(o)
                if mode != "r":
                    st[:] = [(a, b, o, pm) for (a, b, o, pm) in st if not (lo <= a and b <= hi) or o is op]
                elif not dma:
                    st[:] = [(a, b, o, pm) for (a, b, o, pm) in st
                             if not (a == lo and b == hi and pm == "r" and o.eng == eng and not o.dma)]
                st.append((lo, hi, op, mode))
        if dma:
            c = self.dma_cnt.get(key, 0) + 16
            self.dma_cnt[key] = c
            op.cnt = c
        self.ops[eng].append(op)
        self.allops.append(op)
        return op

    def op(self, eng, fn, reads=(), writes=(), xs=()):
        return self._add(eng, fn, reads, writes, xs, False, None)

    def dma(self, eng, fn, reads=(), writes=(), key=None):
        return self._add(eng, fn, reads, writes, (), True, key)

    def resolve(self):
        for op in self.allops:
            for d in op.deps:
                if not d.dma:
                    if d.eng == "pe" and op.eng == "pe" and not op.dma:
                        continue
                    d.sig = True
        for e in self.ENGS:
            c = 0
            for op in self.ops[e]:
                if op.sig and not op.dma:
                    c += 1
                    op.sigidx = c
        waited = {e: {} for e in self.ENGS}
        for e in self.ENGS:
            for op in self.ops[e]:
                need = {}
                for d in op.deps:
                    if d.dma:
                        k, v = ("dma", d.key), d.cnt
                    else:
                        if d.eng == "pe" and e == "pe" and not op.dma:
                            continue
                        k, v = ("eng", d.eng), d.sigidx
                    if v > need.get(k, 0):
                        need[k] = v
                for k, v in need.items():
                    if waited[e].get(k, 0) >= v:
                        continue
                    waited[e][k] = v
                    op.waits[k] = v

    def emit(self, nc):
        self.resolve()
        with contextlib.ExitStack() as es:
            sems = {}
            for e in self.ENGS:
                sems[("eng", e)] = es.enter_context(nc.semaphore("s_" + e))
            for k in self.dma_cnt:
                sems[("dma", k)] = es.enter_context(nc.semaphore("d_" + str(k)))
            block = es.enter_context(nc.Block())

            def run(engname, eng):
                for op in self.ops[engname]:
                    for k, v in op.waits.items():
                        eng.wait_ge(sems[k], v)
                    ins = op.fn(eng)
                    if op.dma:
                        ins.then_inc(sems[("dma", op.key)], 16)
                    elif op.sig:
                        ins.then_inc(sems[("eng", engname)], 1)

            @block.sync
            def _(eng):
                run("sp", eng)
                for k in self.final_keys:
                    eng.wait_ge(sems[("dma", k)], self.dma_cnt[k])

            @block.scalar
            def _(eng):
                run("act", eng)

            @block.vector
            def _(eng):
                run("dve", eng)

            @block.gpsimd
            def _(eng):
                run("pool", eng)

            @block.tensor
            def _(eng):
                run("pe", eng)


OFF_RESID = 0
OFF_H1T = 0
OFF_QT = 53248
OFF_KT = 69632
OFF_V = 96256
OFF_YCT = 123392
OFF_YAT = 139776
OFF_W = 156160
ARENA = 205312
OFF_H2T = 65536
OFF_HID = 98304
OFF_XS = 123392
OFF_XN = 139776
OFF_G1B = 147968
OFF_CB = 139776
OFF_YB = 139776 + 8256
OFF_YSQ = OFF_YB + 4096
OFF_TB = 0
OFF_SB = 12288
OFF_PT = 24576
OFF_PTM = 30720
OFF_YATOK = 31744
OFF_YAN = 39936
OFF_GAB = 44032
OFF_XN2 = 98304
OFF_G2B = 106496
OFF_JUNK2 = 114688
OFF_JUNKG = 139264
OFF_RT = 131072
OFF_GFB = 147456


OPTS = "ACEGP"


def build_nc(dbg=False, opts=None):
    import os
    opts = os.environ.get("KOPTS", OPTS) if opts is None else opts
    nc = bass.Bass("TRN2", target_bir_lowering=False)

    def din(name, shape, dt=F32):
        return nc.dram_tensor(name, shape, dt, kind="ExternalInput").ap()

    xl = din("xl", [TOK, D_MODEL])
    w_in = din("w_in", [2048, 6144])
    w_out = din("w_out", [2048, 2048])
    w_up = din("w_up", [2048, 8192])
    w_down = din("w_down", [8192, 2048])
    tab = din("tab", [4, 16, 128, 1536])
    g1b_d = din("g1b", [128, 2048])
    g2b_d = din("g2b", [128, 2048])
    gfb_d = din("gfb", [128, 2048])
    gab_d = din("gab", [128, 1024])
    cols_d = din("cols", [128, 48])
    ident_d = din("ident", [128, 128], BF16)
    out_d = nc.dram_tensor("out", [NOWN, D_MODEL], F32, kind="ExternalOutput").ap()
    dbg_out = {}
    if dbg:
        for nm, n, dt in (("d_h1T", 16 * TOK, BF16), ("d_qT", 8 * 1024, BF16), ("d_kT", 8 * TOK, BF16),
                          ("d_V", NT * 1040, BF16), ("d_ycT", 8 * 1024, BF16), ("d_yaT", 8 * 1024, BF16),
                          ("d_res", 8 * 2048, F32), ("d_stat", 128, F32), ("d_h2T", 16 * 1024, BF16)):
            dbg_out[nm] = nc.dram_tensor(nm, [128, n], dt, kind="ExternalOutput").ap()

    w_in_v = w_in.rearrange("(kc p) n -> p kc n", p=128)
    w_out_v = w_out.rearrange("(kc p) n -> p kc n", p=128)
    w_up_v = w_up.rearrange("(kc p) n -> p kc n", p=128)
    w_down_v = w_down.rearrange("(kc p) n -> p kc n", p=128)

    S = Sched()
    with contextlib.ExitStack() as es:
        arena = es.enter_context(nc.sbuf_tensor("arena", [128, ARENA // 2], BF16))
        ps = es.enter_context(nc.psum_tensor("ps", [128, 8, 512], F32))

        def small(name, shape, dt=F32):
            return es.enter_context(nc.sbuf_tensor(name, shape, dt))

        ident = small("identsb", [128, 128], BF16)
        cols = small("colsb", [128, 48])
        stat = small("stat", [128, 128])
        epsb = small("epsb", [128, 1])
        ones = small("ones", [128, 2], BF16)
        rec = small("rec", [128, 2])
        C_SSQ1, C_RS1, C_SSQC, C_RSC, C_SSQA, C_RSA, C_SSQ2, C_RS2, C_SSQF, C_RSF, C_TMP = 0, 13, 26, 34, 42, 50, 80, 88, 96, 104, 64

        def view(off, dt, dims):
            es_ = 4 if dt == F32 else 2
            n = int(np.prod(dims))
            a = arena[:, off // 2:(off + n * es_) // 2]
            if dt == F32:
                a = a.bitcast(F32)
            if len(dims) == 2:
                a = a.rearrange("p (a b) -> p a b", a=dims[0])
            elif len(dims) == 3:
                a = a.rearrange("p (a b c) -> p a b c", a=dims[0], b=dims[1])
            return a

        def AR(off, nbytes):
            return ("A", off, off + nbytes)

        def PS(b0, b1=None):
            return ("ps", b0, (b0 + 1) if b1 is None else b1)

        h1T = view(OFF_H1T, BF16, [16, TOK])
        R_H1T = AR(OFF_H1T, 16 * TOK * 2)
        qT = view(OFF_QT, BF16, [8, 1024])
        R_QT = AR(OFF_QT, 16384)
        kT = view(OFF_KT, BF16, [8, TOK])
        R_KT = AR(OFF_KT, 8 * TOK * 2)
        V = view(OFF_V, BF16, [NT, 1040])
        R_V = AR(OFF_V, NT * 1040 * 2)
        ycT = view(OFF_YCT, BF16, [8, 1024])
        R_YCT = AR(OFF_YCT, 16384)
        yaT = view(OFF_YAT, BF16, [8, 1024])
        R_YAT = AR(OFF_YAT, 16384)
        resid = view(OFF_RESID, F32, [8, 2048])

        def R_RES(tt):
            return AR(OFF_RESID + tt * 8192, 8192)

        h2T = view(OFF_H2T, BF16, [16, 1024])
        R_H2T = AR(OFF_H2T, 32768)
        hid = [view(OFF_HID + i * 16384, BF16, [8, 1024]) for i in range(2)]
        R_HID = [AR(OFF_HID + i * 16384, 16384) for i in range(2)]

        def psbank(b):
            return ps[:, b, :]

        def psbank_bf(b):
            return ps[:, b, :].bitcast(BF16)

        wstate = {"p": 0}

        def alloc_slots(ns):
            p = wstate["p"]
            if ns == 2 and p % 2 == 1:
                p = (p + 1) % 6
            if p + ns > 6:
                p = 0
            wstate["p"] = (p + ns) % 6
            return p

        def w_load(src, kc0, KC, c0, N):
            nbytes = KC * N * 2
            ns = nbytes // 8192
            assert ns in (1, 2) and nbytes == ns * 8192
            s0 = alloc_slots(ns)
            off = OFF_W + s0 * 8192
            wv = view(off, BF16, [KC, N])
            reg = AR(off, nbytes)
            step = max(1, (1 << 20) // (128 * N * 4))
            for k0 in range(0, KC, step):
                k1 = min(KC, k0 + step)
                S.dma("pool", (lambda e, k0=k0, k1=k1: e.dma_start(out=wv[:, k0:k1, :], in_=src[:, kc0 + k0:kc0 + k1, c0:c0 + N])),
                      writes=[reg], key=f"w{s0}")
            return wv, reg

        def run_tasks(tasks, look=2):
            handles = {}
            n = len(tasks)
            for j in range(min(look, n)):
                handles[j] = tasks[j][0]()
            for k in range(n):
                if k + look < n:
                    handles[k + look] = tasks[k + look][0]()
                tasks[k][1](handles.pop(k))

        bank_rr = {"i": 0}

        def next_bank(nb=6):
            b = bank_rr["i"] % nb
            bank_rr["i"] += 1
            return b

        evac_rr = {"i": 0}

        def evac_eng():
            evac_rr["i"] += 1
            return "act" if evac_rr["i"] % 2 == 0 else "dve"

        def copy_op(eng, out, in_, reads, writes, xs):
            if eng == "act":
                S.op("act", lambda e: e.copy(out=out, in_=in_), reads=reads, writes=writes, xs=xs)
            else:
                S.op("dve", lambda e: e.tensor_copy(out=out, in_=in_), reads=reads, writes=writes, xs=xs)

        def rstd_ops(c_ssq, c_rs, n, scale, C_TMP=112):
            S.op("act", lambda e: e.activation(out=stat[:, C_TMP:C_TMP + n], in_=stat[:, c_ssq:c_ssq + n], func=AF.Sqrt,
                                               scale=scale, bias=epsb[:, 0:1]),
                 reads=[("stat", c_ssq, c_ssq + n), "epsb"], writes=[("stat", C_TMP, C_TMP + n)])
            S.op("dve", lambda e: e.reciprocal(out=stat[:, c_rs:c_rs + n], in_=stat[:, C_TMP:C_TMP + n]),
                 reads=[("stat", C_TMP, C_TMP + n)], writes=[("stat", c_rs, c_rs + n)])

        S.dma("sp", lambda e: e.dma_start(out=ident[:], in_=ident_d[:, :]), writes=["ident"], key="ident")
        S.dma("sp", lambda e: e.dma_start(out=cols[:], in_=cols_d[:, :]), writes=["cols"], key="cols")
        S.op("dve", lambda e: e.memset(stat[:], 0.0), writes=["stat"])
        S.op("dve", lambda e: e.memset(epsb[:], EPS), writes=["epsb"])
        S.op("dve", lambda e: e.memset(ones[:], 1.0), writes=["ones"])

        g1b = view(OFF_G1B, F32, [2048])
        R_G1B = AR(OFF_G1B, 8192)
        S.dma("sp", lambda e: e.dma_start(out=g1b, in_=g1b_d[:, :]), writes=[R_G1B], key="g1b")
        xs_offs = [OFF_XS, OFF_XS + 8192, OFF_V, OFF_V + 8192]
        xs_b = [view(o, F32, [2048]) for o in xs_offs]
        R_XS = [AR(o, 8192) for o in xs_offs]
        xn_b = [view(OFF_XN + i * 4096, BF16, [2048]) for i in range(2)]
        R_XN = [AR(OFF_XN + i * 4096, 4096) for i in range(2)]

        def norm_a(src_ap, src_reg, junk, junk_reg, c_ssq, c_rs, t):
            S.op("act", lambda e: e.activation(out=junk, in_=src_ap, func=AF.Square, accum_out=stat[:, c_ssq + t:c_ssq + t + 1]),
                 reads=[src_reg], writes=[junk_reg, ("stat", c_ssq + t)])
            S.op("act", lambda e: e.activation(out=stat[:, C_TMP + t:C_TMP + t + 1], in_=stat[:, c_ssq + t:c_ssq + t + 1], func=AF.Sqrt,
                                               scale=1.0 / D_MODEL, bias=epsb[:, 0:1]),
                 reads=[("stat", c_ssq + t), "epsb"], writes=[("stat", C_TMP + t)])

        def norm_b(src_ap, src_reg, out_ap, out_reg, gb, gb_reg, c_rs, t):
            S.op("dve", lambda e: e.reciprocal(out=stat[:, c_rs + t:c_rs + t + 1], in_=stat[:, C_TMP + t:C_TMP + t + 1]),
                 reads=[("stat", C_TMP + t)], writes=[("stat", c_rs + t)])
            S.op("dve", lambda e: e.scalar_tensor_tensor(out=out_ap, in0=src_ap, scalar=stat[:, c_rs + t:c_rs + t + 1], in1=gb,
                                                         op0=ALU.mult, op1=ALU.mult),
                 reads=[src_reg, gb_reg, ("stat", c_rs + t)], writes=[out_reg])

        def norm_stage(src_ap, src_reg, xn, xn_reg, gb, gb_reg, c_ssq, c_rs, t):
            norm_a(src_ap, src_reg, xn, xn_reg, c_ssq, c_rs, t)
            norm_b(src_ap, src_reg, xn, xn_reg, gb, gb_reg, c_rs, t)

        def transpose_stage(xn, xn_reg, dstT, dst_regs, tok0):
            for half in range(2):
                b = next_bank(8)
                pb = psbank_bf(b)

                def tr(e, half=half, pb=pb):
                    ins = None
                    for c in range(8):
                        cc = half * 8 + c
                        ins = e.transpose(out=pb[:, c * 128:(c + 1) * 128], in_=xn[:, cc * 128:(cc + 1) * 128], identity=ident[:])
                    return ins
                S.op("pe", tr, reads=[xn_reg, "ident"], writes=[PS(b)])
                copy_op(evac_eng(), dstT[:, half * 8:half * 8 + 8, tok0:tok0 + 128],
                        pb.rearrange("p (a b) -> p a b", a=8), [], dst_regs, [PS(b)])

        a_order = [12] + list(range(12))

        def a_load(idx):
            t = a_order[idx]
            i = idx % 4
            S.dma("sp", (lambda e, t=t, i=i: e.dma_start(out=xs_b[i], in_=xl[t * 128:(t + 1) * 128, :])),
                  writes=[R_XS[i]], key=f"xs{i}")

        junka = view(OFF_V + 16384, BF16, [2048])
        R_JUNKA = AR(OFF_V + 16384, 4096)

        def a_norm_a(idx):
            norm_a(xs_b[idx % 4], R_XS[idx % 4], junka, R_JUNKA, C_SSQ1, C_RS1, a_order[idx])

        def a_norm_b(idx):
            norm_b(xs_b[idx % 4], R_XS[idx % 4], xn_b[idx % 2], R_XN[idx % 2], g1b, R_G1B, C_RS1, a_order[idx])

        def a_tr(idx):
            t = a_order[idx]
            transpose_stage(xn_b[idx % 2], R_XN[idx % 2], h1T, [("h1t", t)], t * 128)

        for idx in range(4):
            a_load(idx)
        a_norm_a(0)
        a_norm_a(1)
        a_norm_b(0)
        a_load(4)
        for idx in range(NT):
            if idx + 2 < NT:
                a_norm_a(idx + 2)
            if idx + 1 < NT:
                a_norm_b(idx + 1)
            if idx + 5 < NT:
                a_load(idx + 5)
            a_tr(idx)

        if dbg:
            S.dma("sp", lambda e: e.dma_start(out=dbg_out["d_h1T"][:, :], in_=arena[:, OFF_H1T // 2:OFF_H1T // 2 + 16 * TOK]),
                  reads=[R_H1T, "h1t"], writes=["dbg0"], key="dbg0")
            S.final_keys.append("dbg0")

        S.op("dve", lambda e: e.memset(arena[:, OFF_V // 2:OFF_V // 2 + NT * 1040], 1.0), writes=[R_V])
        Vh = V.rearrange("p t (h d) -> p t h d", h=16)

        def fm_group(wv, wreg, oc_local, rhs_ap, n, t0):
            b = next_bank(6)
            h1r = ("h1t", t0 // 128, (t0 + n - 1) // 128 + 1)

            def mm(e, b=b):
                ins = None
                for kc in range(16):
                    ins = e.matmul(ps[:, b, 0:n], lhsT=wv[:, kc, oc_local * 128:(oc_local + 1) * 128], rhs=rhs_ap(kc),
                                   start=(kc == 0), stop=(kc == 15))
                return ins
            S.op("pe", mm, reads=[wreg, R_H1T, h1r], writes=[PS(b)])
            return b

        tasks = []
        for pj in range(2):
            def ld(pj=pj):
                return w_load(w_in_v, 0, 16, 4096 + pj * 512, 512)

            def cp(h, pj=pj):
                wv, wreg = h
                for blk in (3, 0, 1, 2):
                    for ol in range(4):
                        oc = pj * 4 + ol
                        t0 = blk * 512
                        n = 512 if blk < 3 else 128
                        b = fm_group(wv, wreg, ol, lambda kc, t0=t0, n=n: h1T[:, kc, t0:t0 + n], n, t0)
                        copy_op(evac_eng(), kT[:, oc, t0:t0 + n], ps[:, b, 0:n], [], [R_KT], [PS(b)])
            tasks.append((ld, cp))
        for pj in range(2):
            def ld(pj=pj):
                return w_load(w_in_v, 0, 16, 3072 + pj * 512, 512)

            def cp(h, pj=pj):
                wv, wreg = h
                for blk in range(2):
                    for ol in range(4):
                        oc = pj * 4 + ol
                        t0 = OWN0 + blk * 512
                        b = fm_group(wv, wreg, ol, lambda kc, t0=t0: h1T[:, kc, t0:t0 + 512], 512, t0)
                        S.op("act", (lambda e, b=b, oc=oc, blk=blk: e.activation(out=qT[:, oc, blk * 512:(blk + 1) * 512], in_=ps[:, b, :],
                                                                               func=AF.Copy, scale=0.125)),
                             writes=[R_QT], xs=[PS(b)])
            tasks.append((ld, cp))
        for pj in range(2):
            def ld(pj=pj):
                return w_load(w_in_v, 0, 16, 5120 + pj * 512, 512)

            def cp(h, pj=pj):
                wv, wreg = h
                for t in range(NT):
                    b = next_bank(6)

                    def mm(e, b=b, t=t):
                        ins = None
                        for kc in range(16):
                            ins = e.matmul(ps[:, b, :], lhsT=h1T[:, kc, t * 128:(t + 1) * 128], rhs=wv[:, kc, :],
                                           start=(kc == 0), stop=(kc == 15))
                        return ins
                    S.op("pe", mm, reads=[wreg, R_H1T, ("h1t", t)], writes=[PS(b)])
                    copy_op(evac_eng(), Vh[:, t, pj * 8:pj * 8 + 8, 0:64], ps[:, b, :].rearrange("p (h d) -> p h d", h=8),
                            [], [R_V], [PS(b)])
            tasks.append((ld, cp))

        cb = [view(OFF_CB + i * 4128, F32, [1026]) for i in range(2)]
        R_CB = [AR(OFF_CB + i * 4128, 4104) for i in range(2)]
        yb = view(OFF_YB, F32, [1024])
        R_YB = AR(OFF_YB, 4096)
        ysq = view(OFF_YSQ, BF16, [1024])
        R_YSQ = AR(OFF_YSQ, 2048)

        ssq_pending = []

        def halo_group(wv, wreg, ol, col0):
            def mm(e):
                ins = None
                for kc in range(16):
                    ins = e.matmul(ps[:, 6, col0:col0 + 2], lhsT=wv[:, kc, ol * 128:(ol + 1) * 128],
                                   rhs=h1T[:, kc, 255:1281:1025], start=(kc == 0), stop=(kc == 15))
                return ins
            S.op("pe", mm, reads=[wreg, R_H1T, ("h1t", 1), ("h1t", 10)], writes=[PS(6)])

        for pj in range(4):
            def ld_c(pj=pj):
                return w_load(w_in_v, 0, 16, 1024 + pj * 256, 256)

            def cp_c(h, pj=pj):
                wv, wreg = h
                for ol in range(2):
                    for blk in range(2):
                        t0 = OWN0 + blk * 512
                        b = fm_group(wv, wreg, ol, lambda kc, t0=t0: h1T[:, kc, t0:t0 + 512], 512, t0)
                        S.op("act", (lambda e, b=b, ol=ol, blk=blk: e.copy(out=cb[ol][:, 1 + blk * 512:1 + (blk + 1) * 512], in_=ps[:, b, :])),
                             writes=[R_CB[ol]], xs=[PS(b)])
                    halo_group(wv, wreg, ol, 0)
                    S.op("act", (lambda e, ol=ol: e.copy(out=cb[ol][:, 0:1026:1025], in_=ps[:, 6, 0:2])),
                         writes=[R_CB[ol]], xs=[PS(6)])
                    while ssq_pending:
                        ssq_pending.pop(0)()

            def ld_u(pj=pj):
                return w_load(w_in_v, 0, 16, 2048 + pj * 256, 256)

            def cp_u(h, pj=pj):
                wv, wreg = h
                for ol in range(2):
                    for blk in range(2):
                        t0 = OWN0 + blk * 512
                        b = fm_group(wv, wreg, ol, lambda kc, t0=t0: h1T[:, kc, t0:t0 + 512], 512, t0)
                        S.op("dve", (lambda e, b=b, ol=ol, blk=blk: e.tensor_tensor(
                            out=cb[ol][:, 1 + blk * 512:1 + (blk + 1) * 512], in0=cb[ol][:, 1 + blk * 512:1 + (blk + 1) * 512],
                            in1=ps[:, b, :], op=ALU.mult)), writes=[R_CB[ol]], xs=[PS(b)])
                    halo_group(wv, wreg, ol, 8)
                    S.op("dve", (lambda e, ol=ol: e.tensor_tensor(out=cb[ol][:, 0:1026:1025], in0=cb[ol][:, 0:1026:1025],
                                                                 in1=ps[:, 6, 8:10], op=ALU.mult)),
                         writes=[R_CB[ol]], xs=[PS(6)])

            def ld_b(pj=pj):
                return w_load(w_in_v, 0, 16, 0 + pj * 256, 256)

            def cp_b(h, pj=pj):
                wv, wreg = h
                for ol in range(2):
                    ci = pj * 2 + ol
                    c_w0, c_w1, c_w2, c_bias, c_g = 8 + 3 * ci, 9 + 3 * ci, 10 + 3 * ci, 32 + ci, ci
                    S.op("dve", (lambda e, ol=ol, c_w1=c_w1, c_bias=c_bias: e.tensor_scalar(
                        out=yb, in0=cb[ol][:, 1:1025], scalar1=cols[:, c_w1:c_w1 + 1], scalar2=cols[:, c_bias:c_bias + 1],
                        op0=ALU.mult, op1=ALU.add)), reads=[R_CB[ol], "cols"], writes=[R_YB])
                    S.op("dve", (lambda e, ol=ol, c_w0=c_w0: e.scalar_tensor_tensor(
                        out=yb, in0=cb[ol][:, 0:1024], scalar=cols[:, c_w0:c_w0 + 1], in1=yb, op0=ALU.mult, op1=ALU.add)),
                        reads=[R_CB[ol], "cols"], writes=[R_YB])
                    S.op("dve", (lambda e, ol=ol, c_w2=c_w2: e.scalar_tensor_tensor(
                        out=yb, in0=cb[ol][:, 2:1026], scalar=cols[:, c_w2:c_w2 + 1], in1=yb, op0=ALU.mult, op1=ALU.add)),
                        reads=[R_CB[ol], "cols"], writes=[R_YB])
                    for blk in range(2):
                        t0 = OWN0 + blk * 512
                        b = fm_group(wv, wreg, ol, lambda kc, t0=t0: h1T[:, kc, t0:t0 + 512], 512, t0)
                        S.op("dve", (lambda e, b=b, blk=blk: e.tensor_tensor(
                            out=yb[:, blk * 512:(blk + 1) * 512], in0=yb[:, blk * 512:(blk + 1) * 512], in1=ps[:, b, :], op=ALU.mult)),
                            writes=[R_YB], xs=[PS(b)])
                    while ssq_pending:
                        ssq_pending.pop(0)()
                    S.op("act", (lambda e, ci=ci, c_g=c_g: e.activation(out=ycT[:, ci, :], in_=yb, func=AF.Copy,
                                                                         scale=cols[:, c_g:c_g + 1])),
                         reads=[R_YB, "cols"], writes=[R_YCT])
                    S.op("act", lambda e: e.activation(out=ysq, in_=yb, func=AF.Square), reads=[R_YB], writes=[R_YSQ])

                    def ssq_flush():
                        def mm(e):
                            ins = None
                            for tt in range(8):
                                ins = e.matmul(ps[:, 7, tt:tt + 1], lhsT=ysq[:, tt * 128:(tt + 1) * 128], rhs=ones[:, 0:1],
                                               start=True, stop=True)
                            return ins
                        S.op("pe", mm, reads=[R_YSQ, "ones"], writes=[PS(7)])
                        S.op("dve", lambda e: e.tensor_tensor(out=stat[:, C_SSQC:C_SSQC + 8], in0=stat[:, C_SSQC:C_SSQC + 8],
                                                              in1=ps[:, 7, 0:8], op=ALU.add),
                             writes=[("stat", C_SSQC, C_SSQC + 8)], xs=[PS(7)])
                    ssq_pending.append(ssq_flush)
            tasks.append((ld_c, cp_c))
            tasks.append((ld_u, cp_u))
            tasks.append((ld_b, cp_b))

        run_tasks(tasks)
        while ssq_pending:
            ssq_pending.pop(0)()
        rstd_ops(C_SSQC, C_RSC, 8, 1.0 / 1024)

        if dbg:
            for i, (nm, off, n) in enumerate((("d_qT", OFF_QT, 8192), ("d_kT", OFF_KT, 8 * TOK), ("d_V", OFF_V, NT * 1040),
                                              ("d_ycT", OFF_YCT, 8192))):
                S.dma("sp", (lambda e, nm=nm, off=off, n=n: e.dma_start(out=dbg_out[nm][:, :], in_=arena[:, off // 2:off // 2 + n])),
                      reads=[AR(off, n * 2)], writes=[f"dbg{i + 1}"], key=f"dbg{i + 1}")
                S.final_keys.append(f"dbg{i + 1}")

        tb = [view(OFF_TB + i * 6144, F32, [1536]) for i in range(4)]
        R_TB = [AR(OFF_TB + i * 6144, 6144) for i in range(4)]
        sb = [view(OFF_SB + i * 6144, F32, [1536]) for i in range(2)]
        R_SB = [AR(OFF_SB + i * 6144, 6144) for i in range(2)]
        PT = [view(OFF_PT + i * 3072, BF16, [1536]) for i in range(2)]
        R_PT = [AR(OFF_PT + i * 3072, 3072) for i in range(2)]
        ptm_offs = [OFF_PTM, OFF_PTM + 512, OFF_GAB + 4096]
        PTm = [view(o, BF16, [256]) for o in ptm_offs]
        R_PTM = [AR(o, 512) for o in ptm_offs]
        yatok = view(OFF_YATOK, F32, [2, 1024])
        R_YATOK = AR(OFF_YATOK, 8192)
        yan = view(OFF_YAN, BF16, [2, 1024])
        R_YAN = AR(OFF_YAN, 4096)
        gab = view(OFF_GAB, F32, [1024])
        R_GAB = AR(OFF_GAB, 4096)
        S.dma("sp", lambda e: e.dma_start(out=gab, in_=gab_d[:, :]), writes=[R_GAB], key="gab")

        def c_tab(n_):
            m, h = n_ // 16, n_ % 16
            S.dma("sp", (lambda e: e.dma_start(out=tb[n_ % 4], in_=tab[m, h])), writes=[R_TB[n_ % 4]], key=f"tb{n_ % 4}")

        def c_stage1(n_):
            m, h = n_ // 16, n_ % 16
            par = n_ % 2
            p3 = n_ % 3
            t4 = n_ % 4
            hc, hb = h // 2, 64 * (h % 2)

            def qk(e):
                ins = None
                rhs = qT[hb:hb + 64, hc, m * 256:(m + 1) * 256]
                ins = e.matmul(ps[0:16, 6, 0:256], lhsT=kT[hb:hb + 64, hc, 1536:1552], rhs=rhs, start=True, stop=True)
                for po in range(6):
                    kt = 2 * m + po
                    ins = e.matmul(ps[:, 3 * par + po // 2, (po % 2) * 256:(po % 2) * 256 + 256],
                                   lhsT=kT[hb:hb + 64, hc, kt * 128:(kt + 1) * 128], rhs=rhs, start=True, stop=True)
                return ins
            S.op("pe", qk, reads=[R_KT, R_QT], writes=[PS(3 * par, 3 * par + 3), PS(6)])
            S.op("act", (lambda e: e.activation(out=PTm[p3][0:16, :], in_=ps[0:16, 6, 0:256], func=AF.Exp)),
                 writes=[R_PTM[p3]], xs=[PS(6)])
            S.op("dve", (lambda e: e.tensor_tensor(out=ps[:, 3 * par:3 * par + 3, :], in0=ps[:, 3 * par:3 * par + 3, :],
                                                   in1=tb[t4].rearrange("p (a b) -> p a b", a=3), op=ALU.add)),
                 reads=[R_TB[t4]], xs=[PS(3 * par, 3 * par + 3)])
            if "P" not in opts:
                c_exp(n_)

        def c_exp(n_):
            par = n_ % 2
            S.op("act", (lambda e: e.activation(out=PT[par].rearrange("p (a b) -> p a b", a=3), in_=ps[:, 3 * par:3 * par + 3, :],
                                                func=AF.Exp)),
                 writes=[R_PT[par]], xs=[PS(3 * par, 3 * par + 3)])

        def c_stage1b(n_):
            par = n_ % 2
            S.op("act", (lambda e: e.activation(out=PT[par], in_=sb[par], func=AF.Exp)),
                 reads=[R_SB[par]], writes=[R_PT[par]])

        def c_stage2(n_):
            m, h = n_ // 16, n_ % 16
            par = n_ % 2
            p3 = n_ % 3

            def pv(e):
                ins = None
                for half in range(2):
                    o = ps[:, 7, half * 65:half * 65 + 65]
                    for po in range(6):
                        ins = e.matmul(o, lhsT=PT[par][:, po * 256 + half * 128:po * 256 + half * 128 + 128],
                                       rhs=V[:, 2 * m + po, h * 65:(h + 1) * 65], start=(po == 0), stop=False)
                    ins = e.matmul(o, lhsT=PTm[p3][0:16, half * 128:(half + 1) * 128], rhs=V[0:16, 12, h * 65:(h + 1) * 65],
                                   start=False, stop=True)
                return ins
            S.op("pe", pv, reads=[R_PT[par], R_PTM[p3], R_V], writes=[PS(7)])
            S.op("dve", lambda e: e.reciprocal(out=rec[:, 0:2], in_=ps[:, 7, 64:130:65]), writes=["rec"], xs=[PS(7)])
            S.op("dve", (lambda e: e.tensor_tensor(
                out=yatok[:, :, h * 64:(h + 1) * 64],
                in0=ps[:, 7, 0:130].rearrange("p (a b) -> p a b", a=2)[:, :, 0:64],
                in1=rec[:, 0:2].unsqueeze(2).to_broadcast([128, 2, 64]), op=ALU.mult)),
                reads=["rec"], writes=[R_YATOK], xs=[PS(7)])
            if h != 15:
                return
            for half in range(2):
                tt = 2 * m + half
                S.op("act", (lambda e, half=half, tt=tt: e.activation(out=yan[:, half, :], in_=yatok[:, half, :], func=AF.Square,
                                                                     accum_out=stat[:, C_SSQA + tt:C_SSQA + tt + 1])),
                     reads=[R_YATOK], writes=[R_YAN, ("stat", C_SSQA + tt)])
            rstd_ops(C_SSQA + 2 * m, C_RSA + 2 * m, 2, 1.0 / 1024)
            for half in range(2):
                tt = 2 * m + half
                S.op("dve", (lambda e, half=half, tt=tt: e.scalar_tensor_tensor(
                    out=yan[:, half, :], in0=yatok[:, half, :], scalar=stat[:, C_RSA + tt:C_RSA + tt + 1], in1=gab,
                    op0=ALU.mult, op1=ALU.mult)), reads=[R_YATOK, R_GAB, ("stat", C_RSA + tt)], writes=[R_YAN])
            for half in range(2):
                tt = 2 * m + half
                b = 6 + half
                pb = psbank_bf(b)

                def tr(e, half=half, pb=pb):
                    ins = None
                    for c in range(8):
                        ins = e.transpose(out=pb[:, c * 128:(c + 1) * 128], in_=yan[:, half, c * 128:(c + 1) * 128], identity=ident[:])
                    return ins
                S.op("pe", tr, reads=[R_YAN, "ident"], writes=[PS(b)])
                S.op("act", (lambda e, tt=tt, pb=pb: e.copy(out=yaT[:, 0:8, tt * 128:(tt + 1) * 128],
                                                           in_=pb.rearrange("p (a b) -> p a b", a=8))),
                     writes=[R_YAT], xs=[PS(b)])

        if "Q" in opts:
            c_stage1(0)
            c_stage1(1)
            c_stage1b(0)
            for n_ in range(64):
                if n_ + 2 < 64:
                    c_stage1(n_ + 2)
                if n_ + 1 < 64:
                    c_stage1b(n_ + 1)
                c_stage2(n_)
        elif "P" in opts:
            for n_ in range(4):
                c_tab(n_)
            c_stage1(0)
            c_exp(0)
            c_stage1(1)
            c_exp(1)
            for n_ in range(64):
                if n_ + 4 < 64:
                    c_tab(n_ + 4)
                if n_ + 2 < 64:
                    c_stage1(n_ + 2)
                c_stage2(n_)
                if n_ + 2 < 64:
                    c_exp(n_ + 2)
        else:
            for n_ in range(3):
                c_tab(n_)
            c_stage1(0)
            for n_ in range(64):
                if n_ + 3 < 64:
                    c_tab(n_ + 3)
                if n_ + 1 < 64:
                    c_stage1(n_ + 1)
                c_stage2(n_)

        if dbg:
            S.dma("sp", lambda e: e.dma_start(out=dbg_out["d_yaT"][:, :], in_=arena[:, OFF_YAT // 2:OFF_YAT // 2 + 8192]),
                  reads=[R_YAT], writes=["dbg5"], key="dbg5")
            S.final_keys.append("dbg5")

        for tt in range(8):
            S.dma("sp", (lambda e, tt=tt: e.dma_start(out=resid[:, tt, :], in_=xl[OWN0 + tt * 128:OWN0 + (tt + 1) * 128, :])),
                  writes=[R_RES(tt)], key=f"res{tt}")
        g2b = view(OFF_G2B, F32, [2048])
        R_G2B = AR(OFF_G2B, 8192)
        S.dma("sp", lambda e: e.dma_start(out=g2b, in_=g2b_d[:, :]), writes=[R_G2B], key="g2b")
        xn2 = [view(OFF_XN2 + i * 4096, BF16, [2048]) for i in range(2)]
        R_XN2 = [AR(OFF_XN2 + i * 4096, 4096) for i in range(2)]
        junk2 = view(OFF_JUNK2, BF16, [2048])
        R_JUNK2 = AR(OFF_JUNK2, 4096)

        def e_norm_a(tt):
            norm_a(resid[:, tt, :], R_RES(tt), junk2, R_JUNK2, C_SSQ2, C_RS2, tt)

        def e_norm_b(tt):
            norm_b(resid[:, tt, :], R_RES(tt), xn2[tt % 2], R_XN2[tt % 2], g2b, R_G2B, C_RS2, tt)

        def e_transpose(tt):
            regs = [("h2t", tt)] + [AR(OFF_H2T + (c * 1024 + tt * 128) * 2, 256) for c in range(16)]
            transpose_stage(xn2[tt % 2], R_XN2[tt % 2], h2T, regs, tt * 128)

        tasks = []
        for cbk in range(4):
            def ld(cbk=cbk):
                return w_load(w_out_v, 0, 16, cbk * 512, 512)

            def cp(h, cbk=cbk):
                wv, wreg = h
                for tt in range(8):
                    ba = next_bank(8)
                    bb = next_bank(8)

                    def mm(e, tt=tt, ba=ba, bb=bb):
                        ins = None
                        for kc in range(8):
                            ins = e.matmul(ps[:, ba, :], lhsT=ycT[:, kc, tt * 128:(tt + 1) * 128], rhs=wv[:, kc, :],
                                           start=(kc == 0), stop=(kc == 7))
                        for kc in range(8):
                            ins = e.matmul(ps[:, bb, :], lhsT=yaT[:, kc, tt * 128:(tt + 1) * 128], rhs=wv[:, 8 + kc, :],
                                           start=(kc == 0), stop=(kc == 7))
                        return ins
                    S.op("pe", mm, reads=[wreg, R_YCT, R_YAT], writes=[PS(ba), PS(bb)])
                    rs = resid[:, tt, cbk * 512:(cbk + 1) * 512]
                    S.op("dve", (lambda e, rs=rs, tt=tt, ba=ba: e.scalar_tensor_tensor(
                        out=rs, in0=ps[:, ba, :], scalar=stat[:, C_RSC + tt:C_RSC + tt + 1], in1=rs, op0=ALU.mult, op1=ALU.add)),
                        reads=[("stat", C_RSC + tt)], writes=[R_RES(tt)], xs=[PS(ba)])
                    S.op("dve", (lambda e, rs=rs, bb=bb: e.tensor_tensor(out=rs, in0=rs, in1=ps[:, bb, :], op=ALU.add)),
                         writes=[R_RES(tt)], xs=[PS(bb)])
                    if cbk == 3 and "E" in opts:
                        e_norm_a(tt)
                        if tt >= 1:
                            e_norm_b(tt - 1)
                        if tt >= 2:
                            e_transpose(tt - 2)
                if cbk == 3 and "E" in opts:
                    e_norm_b(7)
                    e_transpose(6)
                    e_transpose(7)
            tasks.append((ld, cp))
        run_tasks(tasks)
        if "E" not in opts:
            for tt in range(8):
                e_norm_a(tt)
                e_norm_b(tt)
                e_transpose(tt)

        gfb = view(OFF_GFB, F32, [2048])
        R_GFB = AR(OFF_GFB, 8192)
        S.dma("sp", lambda e: e.dma_start(out=gfb, in_=gfb_d[:, :]), writes=[R_GFB], key="gfb")

        if dbg:
            S.dma("sp", lambda e: e.dma_start(out=dbg_out["d_res"][:, :], in_=arena[:, 0:32768].bitcast(F32)),
                  reads=[AR(0, 65536)], writes=["dbg6"], key="dbg6")
            S.final_keys.append("dbg6")
            S.dma("sp", lambda e: e.dma_start(out=dbg_out["d_h2T"][:, :], in_=arena[:, OFF_H2T // 2:OFF_H2T // 2 + 16384]),
                  reads=[R_H2T, "h2t"], writes=["dbg7"], key="dbg7")
            S.final_keys.append("dbg7")

        rt = [view(OFF_RT + i * 2048, F32, [512]) for i in range(4)]
        R_RT = [AR(OFF_RT + i * 2048, 2048) for i in range(4)]
        rt_rr = {"i": 0}
        tasks = []
        for p in range(8):
            pp = p % 2
            for j in range(2):
                def ld(p=p, j=j):
                    return w_load(w_up_v, 0, 16, p * 1024 + j * 512, 512)

                def cp(h, p=p, j=j, pp=pp):
                    wv, wreg = h
                    for fl in range(4):
                        f = j * 4 + fl
                        for blk in range(2):
                            b = next_bank(8)

                            def mm(e, b=b, fl=fl, blk=blk):
                                ins = None
                                for kc in range(16):
                                    ins = e.matmul(ps[:, b, :], lhsT=wv[:, kc, fl * 128:(fl + 1) * 128],
                                                   rhs=h2T[:, kc, blk * 512:(blk + 1) * 512], start=(kc == 0), stop=(kc == 15))
                                return ins
                            S.op("pe", mm, reads=[wreg, ("h2t", 4 * blk, 4 * blk + 4)], writes=[PS(b)])
                            ri = rt_rr["i"] % 4
                            rt_rr["i"] += 1
                            S.op("act", (lambda e, b=b, ri=ri: e.activation(out=rt[ri], in_=ps[:, b, :], func=AF.Relu)),
                                 writes=[R_RT[ri]], xs=[PS(b)])
                            S.op("dve", (lambda e, ri=ri, f=f, blk=blk, pp=pp: e.tensor_tensor(
                                out=hid[pp][:, f, blk * 512:(blk + 1) * 512], in0=rt[ri], in1=rt[ri], op=ALU.mult)),
                                reads=[R_RT[ri]], writes=[R_HID[pp]])
                tasks.append((ld, cp))
            for j in range(2):
                def ld(p=p, j=j):
                    return w_load(w_down_v, p * 8, 8, j * 1024, 1024)

                def cp(h, p=p, j=j, pp=pp):
                    wv, wreg = h
                    for tt in range(8):
                        for cbl in range(2):
                            b = next_bank(8)

                            def mm(e, b=b, tt=tt, cbl=cbl):
                                ins = None
                                for kc in range(8):
                                    ins = e.matmul(ps[:, b, :], lhsT=hid[pp][:, kc, tt * 128:(tt + 1) * 128],
                                                   rhs=wv[:, kc, cbl * 512:(cbl + 1) * 512], start=(kc == 0), stop=(kc == 7))
                                return ins
                            S.op("pe", mm, reads=[wreg, R_HID[pp]], writes=[PS(b)])
                            c0 = j * 1024 + cbl * 512
                            rs = resid[:, tt, c0:c0 + 512]
                            S.op("dve", (lambda e, rs=rs, b=b: e.tensor_tensor(out=rs, in0=rs, in1=ps[:, b, :], op=ALU.add)),
                                 writes=[R_RES(tt)], xs=[PS(b)])
                        if p == 7 and j == 1 and "G" in opts:
                            g_norm_a(tt)
                            if tt >= 1:
                                g_norm_b(tt - 1)
                    if p == 7 and j == 1 and "G" in opts:
                        g_norm_b(7)
                tasks.append((ld, cp))

        junkg = view(OFF_JUNKG, BF16, [2048])
        R_JUNKG = AR(OFF_JUNKG, 4096)

        def g_norm_a(tt):
            norm_a(resid[:, tt, :], R_RES(tt), junkg, R_JUNKG, C_SSQF, C_RSF, tt)

        def g_norm_b(tt):
            norm_b(resid[:, tt, :], R_RES(tt), resid[:, tt, :], R_RES(tt), gfb, R_GFB, C_RSF, tt)
            S.dma("sp", (lambda e: e.dma_start(out=out_d[tt * 128:(tt + 1) * 128, :], in_=resid[:, tt, :])),
                  reads=[R_RES(tt)], writes=[("outd", tt)], key=f"out{tt % 2}")

        run_tasks(tasks)
        if "G" not in opts:
            for tt in range(8):
                g_norm_a(tt)
                g_norm_b(tt)
        S.final_keys += ["out0", "out1"]
        if dbg:
            S.dma("sp", lambda e: e.dma_start(out=dbg_out["d_stat"][:, :], in_=stat[:]), reads=["stat"], writes=["dbg8"], key="dbg8")
            S.final_keys.append("dbg8")
        S.emit(nc)
    return nc


def _bias_table(rpb, i):
    R0 = 16 * i
    m = np.arange(4)[:, None, None, None, None]
    kp = np.arange(128)[None, :, None, None, None]
    po = np.arange(6)[None, None, :, None, None]
    a = np.arange(4)[None, None, None, :, None]
    qc = np.arange(64)[None, None, None, None, :]
    kb, kc = kp // 64, kp % 64
    gk = R0 - 4 + 4 * m + 2 * po + kb
    r = R0 + 4 * m + a
    rs = np.clip(r - 4, 0, 56)
    c0 = np.clip(qc - 8, 0, 48)
    valid = (gk >= 0) & (gk < 64) & (gk >= rs) & (gk < rs + 8) & (kc >= c0) & (kc < c0 + 16)
    dr = np.clip(gk - r + 7, 0, 14)
    dc = np.clip(kc - qc + 15, 0, 30)
    shape = np.broadcast_shapes(valid.shape, dr.shape, dc.shape)
    valid = np.broadcast_to(valid, shape)
    dr = np.broadcast_to(dr, shape)
    dc = np.broadcast_to(dc, shape)
    t = rpb[:, dr, dc]
    t = np.where(valid[None], t, np.float32(NEG)).astype(np.float32)
    t = np.ascontiguousarray(t.transpose(1, 0, 2, 3, 4, 5)).reshape(4, 16, 128, 1536)
    return t


def _prep_inputs(x, meta_tokens, norm1_g, w_in, conv_w, conv_b, conv_norm_g, attn_rpb, attn_norm_g, w_out, norm2_g,
                 w_up, w_down, final_norm_g):
    import ml_dtypes
    f32 = np.float32
    x = np.asarray(x, f32)
    meta = np.asarray(meta_tokens, f32)
    shared = {
        "w_in": np.ascontiguousarray(np.asarray(w_in, f32)[0]),
        "w_out": np.ascontiguousarray(np.asarray(w_out, f32)[0]),
        "w_up": np.ascontiguousarray(np.asarray(w_up, f32)[0]),
        "w_down": np.ascontiguousarray(np.asarray(w_down, f32)[0]),
        "g1b": np.ascontiguousarray(np.broadcast_to(np.asarray(norm1_g, f32)[0][None, :], (128, 2048))),
        "g2b": np.ascontiguousarray(np.broadcast_to(np.asarray(norm2_g, f32)[0][None, :], (128, 2048))),
        "gfb": np.ascontiguousarray(np.broadcast_to(np.asarray(final_norm_g, f32)[None, :], (128, 2048))),
        "gab": np.ascontiguousarray(np.broadcast_to(np.asarray(attn_norm_g, f32)[0][None, :], (128, 1024))),
        "ident": np.eye(128, dtype=f32).astype(ml_dtypes.bfloat16),
    }
    cols = np.zeros((128, 48), f32)
    cols[:, 0:8] = np.asarray(conv_norm_g, f32)[0].reshape(8, 128).T
    cw = np.asarray(conv_w, f32)[0]
    cols[:, 8:32] = cw.reshape(3, 8, 128).transpose(2, 1, 0).reshape(128, 24)
    cols[:, 32:40] = np.asarray(conv_b, f32)[0].reshape(8, 128).T
    shared["cols"] = cols
    rpb = np.asarray(attn_rpb, f32)[0]
    tabs = [_bias_table(rpb, i) for i in range(4)]
    in_maps = []
    for c in range(8):
        b, i = c // 4, c % 4
        xl = np.zeros((TOK, D_MODEL), f32)
        g0 = 16 * i - 4
        lo, hi = max(g0, 0), min(g0 + 24, 64)
        xl[(lo - g0) * 64:(hi - g0) * 64] = x[b, lo * 64:hi * 64]
        if i == 0:
            xl[255] = meta[15]
        xl[1536:1552] = meta
        mp = dict(shared)
        mp["xl"] = xl
        mp["tab"] = tabs[i]
        in_maps.append(mp)
    return in_maps


_NC = {}


def kernel(**inputs):
    in_maps = _prep_inputs(**inputs)
    if "nc" not in _NC:
        _NC["nc"] = build_nc()
    res = run_bass_kernel_spmd(_NC["nc"], in_maps, core_ids=list(range(8)))
    out = np.empty((2, SEQ, D_MODEL), np.float32)
    for c in range(8):
        b, i = c // 4, c % 4
        out[b, i * 1024:(i + 1) * 1024] = np.asarray(res.results[c]["out"], np.float32)
    return out
```

```python
import contextlib
import numpy as np
import concourse.bass as bass
import concourse.mybir as mybir
from concourse.bass_utils import run_bass_kernel_spmd

F32 = mybir.dt.float32
BF16 = mybir.dt.bfloat16
ALU = mybir.AluOpType
AF = mybir.ActivationFunctionType

D_MODEL = 2048
SEQ = 4096
N_META = 16
D_FF = 8192
EPS = 1e-6
NEG = -1e30
NT = 13
TOK = NT * 128
OWN0 = 256
NOWN = 1024


class Op:
    __slots__ = ("eng", "fn", "dma", "key", "sig", "sigidx", "waits", "cnt", "deps")

    def __init__(self, eng, fn, dma, key):
        self.eng = eng
        self.fn = fn
        self.dma = dma
        self.key = key
        self.sig = False
        self.sigidx = 0
        self.waits = {}
        self.cnt = 0
        self.deps = []


class Sched:
    ENGS = ("pe", "act", "dve", "pool", "sp")

    def __init__(self):
        self.ops = {e: [] for e in self.ENGS}
        self.state = {}
        self.dma_cnt = {}
        self.allops = []
        self.final_keys = []

    @staticmethod
    def _norm(r):
        if isinstance(r, str):
            return (r, 0, 1 << 40)
        if len(r) == 2:
            return (r[0], r[1], r[1] + 1)
        return r

    def _add(self, eng, fn, reads, writes, xs, dma, key):
        op = Op(eng, fn, dma, key)
        deps = op.deps
        for mode, regs in (("r", reads), ("x", xs), ("w", writes)):
            for r in regs:
                n, lo, hi = self._norm(r)
                st = self.state.setdefault(n, [])
                for (a, b, o, pm) in st:
                    if not (a < hi and lo < b) or o is op:
                        continue
                    if pm == "r" and mode == "r":
                        continue
                    if pm == "x" and mode == "x" and o.eng == eng and not o.dma and not dma:
                        continue
                    deps.append(o)
                if mode != "r":
                    st[:] = [(a, b, o, pm) for (a, b, o, pm) in st if not (lo <= a and b <= hi) or o is op]
                elif not dma:
                    st[:] = [(a, b, o, pm) for (a, b, o, pm) in st
                             if not (a == lo and b == hi and pm == "r" and o.eng == eng and not o.dma)]
                st.append((lo, hi, op, mode))
        if dma:
            c = self.dma_cnt.get(key, 0) + 16
            self.dma_cnt[key] = c
            op.cnt = c
        self.ops[eng].append(op)
        self.allops.append(op)
        return op

    def op(self, eng, fn, reads=(), writes=(), xs=()):
        return self._add(eng, fn, reads, writes, xs, False, None)

    def dma(self, eng, fn, reads=(), writes=(), key=None):
        return self._add(eng, fn, reads, writes, (), True, key)

    def resolve(self):
        for op in self.allops:
            for d in op.deps:
                if not d.dma:
                    if d.eng == "pe" and op.eng == "pe" and not op.dma:
                        continue
                    d.sig = True
        for e in self.ENGS:
            c = 0
            for op in self.ops[e]:
                if op.sig and not op.dma:
                    c += 1
                    op.sigidx = c
        waited = {e: {} for e in self.ENGS}
        for e in self.ENGS:
            for op in self.ops[e]:
                need = {}
                for d in op.deps:
                    if d.dma:
                        k, v = ("dma", d.key), d.cnt
                    else:
                        if d.eng == "pe" and e == "pe" and not op.dma:
                            continue
                        k, v = ("eng", d.eng), d.sigidx
                    if v > need.get(k, 0):
                        need[k] = v
                for k, v in need.items():
                    if waited[e].get(k, 0) >= v:
                        continue
                    waited[e][k] = v
                    op.waits[k] = v

    def emit(self, nc):
        self.resolve()
        with contextlib.ExitStack() as es:
            sems = {}
            for e in self.ENGS:
                sems[("eng", e)] = es.enter_context(nc.semaphore("s_" + e))
            for k in self.dma_cnt:
                sems[("dma", k)] = es.enter_context(nc.semaphore("d_" + str(k)))
            block = es.enter_context(nc.Block())

            def run(engname, eng):
                for op in self.ops[engname]:
                    for k, v in op.waits.items():
                        eng.wait_ge(sems[k], v)
                    ins = op.fn(eng)
                    if op.dma:
                        ins.then_inc(sems[("dma", op.key)], 16)
                    elif op.sig:
                        ins.then_inc(sems[("eng", engname)], 1)

            @block.sync
            def _(eng):
                run("sp", eng)
                for k in self.final_keys:
                    eng.wait_ge(sems[("dma", k)], self.dma_cnt[k])

            @block.scalar
            def _(eng):
                run("act", eng)

            @block.vector
            def _(eng):
                run("dve", eng)

            @block.gpsimd
            def _(eng):
                run("pool", eng)

            @block.tensor
            def _(eng):
                run("pe", eng)


OFF_RESID = 0
OFF_H1T = 0
OFF_QT = 53248
OFF_KT = 69632
OFF_V = 96256
OFF_YCT = 123392
OFF_YAT = 139776
OFF_W = 156160
ARENA = 205312
OFF_H2T = 65536
OFF_HID = 98304
OFF_XS = 123392
OFF_XN = 139776
OFF_G1B = 147968
OFF_CB = 139776
OFF_YB = 139776 + 8256
OFF_YSQ = OFF_YB + 4096
OFF_TB = 0
OFF_SB = 12288
OFF_PT = 24576
OFF_PTM = 30720
OFF_YATOK = 31744
OFF_YAN = 39936
OFF_GAB = 44032
OFF_XN2 = 98304
OFF_G2B = 106496
OFF_JUNK2 = 114688
OFF_JUNKG = 139264
OFF_RT = 131072
OFF_GFB = 147456


OPTS = "ACEGP"


def build_nc(dbg=False, opts=None):
    import os
    opts = os.environ.get("KOPTS", OPTS) if opts is None else opts
    nc = bass.Bass("TRN2", target_bir_lowering=False)

    def din(name, shape, dt=F32):
        return nc.dram_tensor(name, shape, dt, kind="ExternalInput").ap()

    xl = din("xl", [TOK, D_MODEL])
    w_in = din("w_in", [2048, 6144])
    w_out = din("w_out", [2048, 2048])
    w_up = din("w_up", [2048, 8192])
    w_down = din("w_down", [8192, 2048])
    tab = din("tab", [4, 16, 128, 1536])
    g1b_d = din("g1b", [128, 2048])
    g2b_d = din("g2b", [128, 2048])
    gfb_d = din("gfb", [128, 2048])
    gab_d = din("gab", [128, 1024])
    cols_d = din("cols", [128, 48])
    ident_d = din("ident", [128, 128], BF16)
    out_d = nc.dram_tensor("out", [NOWN, D_MODEL], F32, kind="ExternalOutput").ap()
    dbg_out = {}
    if dbg:
        for nm, n, dt in (("d_h1T", 16 * TOK, BF16), ("d_qT", 8 * 1024, BF16), ("d_kT", 8 * TOK, BF16),
                          ("d_V", NT * 1040, BF16), ("d_ycT", 8 * 1024, BF16), ("d_yaT", 8 * 1024, BF16),
                          ("d_res", 8 * 2048, F32), ("d_stat", 128, F32), ("d_h2T", 16 * 1024, BF16)):
            dbg_out[nm] = nc.dram_tensor(nm, [128, n], dt, kind="ExternalOutput").ap()

    w_in_v = w_in.rearrange("(kc p) n -> p kc n", p=128)
    w_out_v = w_out.rearrange("(kc p) n -> p kc n", p=128)
    w_up_v = w_up.rearrange("(kc p) n -> p kc n", p=128)
    w_down_v = w_down.rearrange("(kc p) n -> p kc n", p=128)

    S = Sched()
    with contextlib.ExitStack() as es:
        arena = es.enter_context(nc.sbuf_tensor("arena", [128, ARENA // 2], BF16))
        ps = es.enter_context(nc.psum_tensor("ps", [128, 8, 512], F32))

        def small(name, shape, dt=F32):
            return es.enter_context(nc.sbuf_tensor(name, shape, dt))

        ident = small("identsb", [128, 128], BF16)
        cols = small("colsb", [128, 48])
        stat = small("stat", [128, 128])
        epsb = small("epsb", [128, 1])
        ones = small("ones", [128, 2], BF16)
        rec = small("rec", [128, 2])
        C_SSQ1, C_RS1, C_SSQC, C_RSC, C_SSQA, C_RSA, C_SSQ2, C_RS2, C_SSQF, C_RSF, C_TMP = 0, 13, 26, 34, 42, 50, 80, 88, 96, 104, 64

        def view(off, dt, dims):
            es_ = 4 if dt == F32 else 2
            n = int(np.prod(dims))
            a = arena[:, off // 2:(off + n * es_) // 2]
            if dt == F32:
                a = a.bitcast(F32)
            if len(dims) == 2:
                a = a.rearrange("p (a b) -> p a b", a=dims[0])
            elif len(dims) == 3:
                a = a.rearrange("p (a b c) -> p a b c", a=dims[0], b=dims[1])
            return a

        def AR(off, nbytes):
            return ("A", off, off + nbytes)

        def PS(b0, b1=None):
            return ("ps", b0, (b0 + 1) if b1 is None else b1)

        h1T = view(OFF_H1T, BF16, [16, TOK])
        R_H1T = AR(OFF_H1T, 16 * TOK * 2)
        qT = view(OFF_QT, BF16, [8, 1024])
        R_QT = AR(OFF_QT, 16384)
        kT = view(OFF_KT, BF16, [8, TOK])
        R_KT = AR(OFF_KT, 8 * TOK * 2)
        V = view(OFF_V, BF16, [NT, 1040])
        R_V = AR(OFF_V, NT * 1040 * 2)
        ycT = view(OFF_YCT, BF16, [8, 1024])
        R_YCT = AR(OFF_YCT, 16384)
        yaT = view(OFF_YAT, BF16, [8, 1024])
        R_YAT = AR(OFF_YAT, 16384)
        resid = view(OFF_RESID, F32, [8, 2048])

        def R_RES(tt):
            return AR(OFF_RESID + tt * 8192, 8192)

        h2T = view(OFF_H2T, BF16, [16, 1024])
        R_H2T = AR(OFF_H2T, 32768)
        hid = [view(OFF_HID + i * 16384, BF16, [8, 1024]) for i in range(2)]
        R_HID = [AR(OFF_HID + i * 16384, 16384) for i in range(2)]

        def psbank(b):
            return ps[:, b, :]

        def psbank_bf(b):
            return ps[:, b, :].bitcast(BF16)

        wstate = {"p": 0}

        def alloc_slots(ns):
            p = wstate["p"]
            if ns == 2 and p % 2 == 1:
                p = (p + 1) % 6
            if p + ns > 6:
                p = 0
            wstate["p"] = (p + ns) % 6
            return p

        def w_load(src, kc0, KC, c0, N):
            nbytes = KC * N * 2
            ns = nbytes // 8192
            assert ns in (1, 2) and nbytes == ns * 8192
            s0 = alloc_slots(ns)
            off = OFF_W + s0 * 8192
            wv = view(off, BF16, [KC, N])
            reg = AR(off, nbytes)
            step = max(1, (1 << 20) // (128 * N * 4))
            for k0 in range(0, KC, step):
                k1 = min(KC, k0 + step)
                S.dma("pool", (lambda e, k0=k0, k1=k1: e.dma_start(out=wv[:, k0:k1, :], in_=src[:, kc0 + k0:kc0 + k1, c0:c0 + N])),
                      writes=[reg], key=f"w{s0}")
            return wv, reg

        def run_tasks(tasks, look=2):
            handles = {}
            n = len(tasks)
            for j in range(min(look, n)):
                handles[j] = tasks[j][0]()
            for k in range(n):
                if k + look < n:
                    handles[k + look] = tasks[k + look][0]()
                tasks[k][1](handles.pop(k))

        bank_rr = {"i": 0}

        def next_bank(nb=6):
            b = bank_rr["i"] % nb
            bank_rr["i"] += 1
            return b

        evac_rr = {"i": 0}

        def evac_eng():
            evac_rr["i"] += 1
            return "act" if evac_rr["i"] % 2 == 0 else "dve"

        def copy_op(eng, out, in_, reads, writes, xs):
            if eng == "act":
                S.op("act", lambda e: e.copy(out=out, in_=in_), reads=reads, writes=writes, xs=xs)
            else:
                S.op("dve", lambda e: e.tensor_copy(out=out, in_=in_), reads=reads, writes=writes, xs=xs)

        def rstd_ops(c_ssq, c_rs, n, scale, C_TMP=112):
            S.op("act", lambda e: e.activation(out=stat[:, C_TMP:C_TMP + n], in_=stat[:, c_ssq:c_ssq + n], func=AF.Sqrt,
                                               scale=scale, bias=epsb[:, 0:1]),
                 reads=[("stat", c_ssq, c_ssq + n), "epsb"], writes=[("stat", C_TMP, C_TMP + n)])
            S.op("dve", lambda e: e.reciprocal(out=stat[:, c_rs:c_rs + n], in_=stat[:, C_TMP:C_TMP + n]),
                 reads=[("stat", C_TMP, C_TMP + n)], writes=[("stat", c_rs, c_rs + n)])

        S.dma("sp", lambda e: e.dma_start(out=ident[:], in_=ident_d[:, :]), writes=["ident"], key="ident")
        S.dma("sp", lambda e: e.dma_start(out=cols[:], in_=cols_d[:, :]), writes=["cols"], key="cols")
        S.op("dve", lambda e: e.memset(stat[:], 0.0), writes=["stat"])
        S.op("dve", lambda e: e.memset(epsb[:], EPS), writes=["epsb"])
        S.op("dve", lambda e: e.memset(ones[:], 1.0), writes=["ones"])

        g1b = view(OFF_G1B, F32, [2048])
        R_G1B = AR(OFF_G1B, 8192)
        S.dma("sp", lambda e: e.dma_start(out=g1b, in_=g1b_d[:, :]), writes=[R_G1B], key="g1b")
        xs_offs = [OFF_XS, OFF_XS + 8192, OFF_V, OFF_V + 8192]
        xs_b = [view(o, F32, [2048]) for o in xs_offs]
        R_XS = [AR(o, 8192) for o in xs_offs]
        xn_b = [view(OFF_XN + i * 4096, BF16, [2048]) for i in range(2)]
        R_XN = [AR(OFF_XN + i * 4096, 4096) for i in range(2)]

        def norm_a(src_ap, src_reg, junk, junk_reg, c_ssq, c_rs, t):
            S.op("act", lambda e: e.activation(out=junk, in_=src_ap, func=AF.Square, accum_out=stat[:, c_ssq + t:c_ssq + t + 1]),
                 reads=[src_reg], writes=[junk_reg, ("stat", c_ssq + t)])
            S.op("act", lambda e: e.activation(out=stat[:, C_TMP + t:C_TMP + t + 1], in_=stat[:, c_ssq + t:c_ssq + t + 1], func=AF.Sqrt,
                                               scale=1.0 / D_MODEL, bias=epsb[:, 0:1]),
                 reads=[("stat", c_ssq + t), "epsb"], writes=[("stat", C_TMP + t)])

        def norm_b(src_ap, src_reg, out_ap, out_reg, gb, gb_reg, c_rs, t):
            S.op("dve", lambda e: e.reciprocal(out=stat[:, c_rs + t:c_rs + t + 1], in_=stat[:, C_TMP + t:C_TMP + t + 1]),
                 reads=[("stat", C_TMP + t)], writes=[("stat", c_rs + t)])
            S.op("dve", lambda e: e.scalar_tensor_tensor(out=out_ap, in0=src_ap, scalar=stat[:, c_rs + t:c_rs + t + 1], in1=gb,
                                                         op0=ALU.mult, op1=ALU.mult),
                 reads=[src_reg, gb_reg, ("stat", c_rs + t)], writes=[out_reg])

        def norm_stage(src_ap, src_reg, xn, xn_reg, gb, gb_reg, c_ssq, c_rs, t):
            norm_a(src_ap, src_reg, xn, xn_reg, c_ssq, c_rs, t)
            norm_b(src_ap, src_reg, xn, xn_reg, gb, gb_reg, c_rs, t)

        def transpose_stage(xn, xn_reg, dstT, dst_regs, tok0):
            for half in range(2):
                b = next_bank(8)
                pb = psbank_bf(b)

                def tr(e, half=half, pb=pb):
                    ins = None
                    for c in range(8):
                        cc = half * 8 + c
                        ins = e.transpose(out=pb[:, c * 128:(c + 1) * 128], in_=xn[:, cc * 128:(cc + 1) * 128], identity=ident[:])
                    return ins
                S.op("pe", tr, reads=[xn_reg, "ident"], writes=[PS(b)])
                copy_op(evac_eng(), dstT[:, half * 8:half * 8 + 8, tok0:tok0 + 128],
                        pb.rearrange("p (a b) -> p a b", a=8), [], dst_regs, [PS(b)])

        a_order = [12] + list(range(12))

        def a_load(idx):
            t = a_order[idx]
            i = idx % 4
            S.dma("sp", (lambda e, t=t, i=i: e.dma_start(out=xs_b[i], in_=xl[t * 128:(t + 1) * 128, :])),
                  writes=[R_XS[i]], key=f"xs{i}")

        junka = view(OFF_V + 16384, BF16, [2048])
        R_JUNKA = AR(OFF_V + 16384, 4096)

        def a_norm_a(idx):
            norm_a(xs_b[idx % 4], R_XS[idx % 4], junka, R_JUNKA, C_SSQ1, C_RS1, a_order[idx])

        def a_norm_b(idx):
            norm_b(xs_b[idx % 4], R_XS[idx % 4], xn_b[idx % 2], R_XN[idx % 2], g1b, R_G1B, C_RS1, a_order[idx])

        def a_tr(idx):
            t = a_order[idx]
            transpose_stage(xn_b[idx % 2], R_XN[idx % 2], h1T, [("h1t", t)], t * 128)

        for idx in range(4):
            a_load(idx)
        a_norm_a(0)
        a_norm_a(1)
        a_norm_b(0)
        a_load(4)
        for idx in range(NT):
            if idx + 2 < NT:
                a_norm_a(idx + 2)
            if idx + 1 < NT:
                a_norm_b(idx + 1)
            if idx + 5 < NT:
                a_load(idx + 5)
            a_tr(idx)

        if dbg:
            S.dma("sp", lambda e: e.dma_start(out=dbg_out["d_h1T"][:, :], in_=arena[:, OFF_H1T // 2:OFF_H1T // 2 + 16 * TOK]),
                  reads=[R_H1T, "h1t"], writes=["dbg0"], key="dbg0")
            S.final_keys.append("dbg0")

        S.op("dve", lambda e: e.memset(arena[:, OFF_V // 2:OFF_V // 2 + NT * 1040], 1.0), writes=[R_V])
        Vh = V.rearrange("p t (h d) -> p t h d", h=16)

        def fm_group(wv, wreg, oc_local, rhs_ap, n, t0):
            b = next_bank(6)
            h1r = ("h1t", t0 // 128, (t0 + n - 1) // 128 + 1)

            def mm(e, b=b):
                ins = None
                for kc in range(16):
                    ins = e.matmul(ps[:, b, 0:n], lhsT=wv[:, kc, oc_local * 128:(oc_local + 1) * 128], rhs=rhs_ap(kc),
                                   start=(kc == 0), stop=(kc == 15))
                return ins
            S.op("pe", mm, reads=[wreg, R_H1T, h1r], writes=[PS(b)])
            return b

        tasks = []
        for pj in range(2):
            def ld(pj=pj):
                return w_load(w_in_v, 0, 16, 4096 + pj * 512, 512)

            def cp(h, pj=pj):
                wv, wreg = h
                for blk in (3, 0, 1, 2):
                    for ol in range(4):
                        oc = pj * 4 + ol
                        t0 = blk * 512
                        n = 512 if blk < 3 else 128
                        b = fm_group(wv, wreg, ol, lambda kc, t0=t0, n=n: h1T[:, kc, t0:t0 + n], n, t0)
                        copy_op(evac_eng(), kT[:, oc, t0:t0 + n], ps[:, b, 0:n], [], [R_KT], [PS(b)])
            tasks.append((ld, cp))
        for pj in range(2):
            def ld(pj=pj):
                return w_load(w_in_v, 0, 16, 3072 + pj * 512, 512)

            def cp(h, pj=pj):
                wv, wreg = h
                for blk in range(2):
                    for ol in range(4):
                        oc = pj * 4 + ol
                        t0 = OWN0 + blk * 512
                        b = fm_group(wv, wreg, ol, lambda kc, t0=t0: h1T[:, kc, t0:t0 + 512], 512, t0)
                        S.op("act", (lambda e, b=b, oc=oc, blk=blk: e.activation(out=qT[:, oc, blk * 512:(blk + 1) * 512], in_=ps[:, b, :],
                                                                               func=AF.Copy, scale=0.125)),
                             writes=[R_QT], xs=[PS(b)])
            tasks.append((ld, cp))
        for pj in range(2):
            def ld(pj=pj):
                return w_load(w_in_v, 0, 16, 5120 + pj * 512, 512)

            def cp(h, pj=pj):
                wv, wreg = h
                for t in range(NT):
                    b = next_bank(6)

                    def mm(e, b=b, t=t):
                        ins = None
                        for kc in range(16):
                            ins = e.matmul(ps[:, b, :], lhsT=h1T[:, kc, t * 128:(t + 1) * 128], rhs=wv[:, kc, :],
                                           start=(kc == 0), stop=(kc == 15))
                        return ins
                    S.op("pe", mm, reads=[wreg, R_H1T, ("h1t", t)], writes=[PS(b)])
                    copy_op(evac_eng(), Vh[:, t, pj * 8:pj * 8 + 8, 0:64], ps[:, b, :].rearrange("p (h d) -> p h d", h=8),
                            [], [R_V], [PS(b)])
            tasks.append((ld, cp))

        cb = [view(OFF_CB + i * 4128, F32, [1026]) for i in range(2)]
        R_CB = [AR(OFF_CB + i * 4128, 4104) for i in range(2)]
        yb = view(OFF_YB, F32, [1024])
        R_YB = AR(OFF_YB, 4096)
        ysq = view(OFF_YSQ, BF16, [1024])
        R_YSQ = AR(OFF_YSQ, 2048)

        ssq_pending = []

        def halo_group(wv, wreg, ol, col0):
            def mm(e):
                ins = None
                for kc in range(16):
                    ins = e.matmul(ps[:, 6, col0:col0 + 2], lhsT=wv[:, kc, ol * 128:(ol + 1) * 128],
                                   rhs=h1T[:, kc, 255:1281:1025], start=(kc == 0), stop=(kc == 15))
                return ins
            S.op("pe", mm, reads=[wreg, R_H1T, ("h1t", 1), ("h1t", 10)], writes=[PS(6)])

        for pj in range(4):
            def ld_c(pj=pj):
                return w_load(w_in_v, 0, 16, 1024 + pj * 256, 256)

            def cp_c(h, pj=pj):
                wv, wreg = h
                for ol in range(2):
                    for blk in range(2):
                        t0 = OWN0 + blk * 512
                        b = fm_group(wv, wreg, ol, lambda kc, t0=t0: h1T[:, kc, t0:t0 + 512], 512, t0)
                        S.op("act", (lambda e, b=b, ol=ol, blk=blk: e.copy(out=cb[ol][:, 1 + blk * 512:1 + (blk + 1) * 512], in_=ps[:, b, :])),
                             writes=[R_CB[ol]], xs=[PS(b)])
                    halo_group(wv, wreg, ol, 0)
                    S.op("act", (lambda e, ol=ol: e.copy(out=cb[ol][:, 0:1026:1025], in_=ps[:, 6, 0:2])),
                         writes=[R_CB[ol]], xs=[PS(6)])
                    while ssq_pending:
                        ssq_pending.pop(0)()

            def ld_u(pj=pj):
                return w_load(w_in_v, 0, 16, 2048 + pj * 256, 256)

            def cp_u(h, pj=pj):
                wv, wreg = h
                for ol in range(2):
                    for blk in range(2):
                        t0 = OWN0 + blk * 512
                        b = fm_group(wv, wreg, ol, lambda kc, t0=t0: h1T[:, kc, t0:t0 + 512], 512, t0)
                        S.op("dve", (lambda e, b=b, ol=ol, blk=blk: e.tensor_tensor(
                            out=cb[ol][:, 1 + blk * 512:1 + (blk + 1) * 512], in0=cb[ol][:, 1 + blk * 512:1 + (blk + 1) * 512],
                            in1=ps[:, b, :], op=ALU.mult)), writes=[R_CB[ol]], xs=[PS(b)])
                    halo_group(wv, wreg, ol, 8)
                    S.op("dve", (lambda e, ol=ol: e.tensor_tensor(out=cb[ol][:, 0:1026:1025], in0=cb[ol][:, 0:1026:1025],
                                                                 in1=ps[:, 6, 8:10], op=ALU.mult)),
                         writes=[R_CB[ol]], xs=[PS(6)])

            def ld_b(pj=pj):
                return w_load(w_in_v, 0, 16, 0 + pj * 256, 256)

            def cp_b(h, pj=pj):
                wv, wreg = h
                for ol in range(2):
                    ci = pj * 2 + ol
                    c_w0, c_w1, c_w2, c_bias, c_g = 8 + 3 * ci, 9 + 3 * ci, 10 + 3 * ci, 32 + ci, ci
                    S.op("dve", (lambda e, ol=ol, c_w1=c_w1, c_bias=c_bias: e.tensor_scalar(
                        out=yb, in0=cb[ol][:, 1:1025], scalar1=cols[:, c_w1:c_w1 + 1], scalar2=cols[:, c_bias:c_bias + 1],
                        op0=ALU.mult, op1=ALU.add)), reads=[R_CB[ol], "cols"], writes=[R_YB])
                    S.op("dve", (lambda e, ol=ol, c_w0=c_w0: e.scalar_tensor_tensor(
                        out=yb, in0=cb[ol][:, 0:1024], scalar=cols[:, c_w0:c_w0 + 1], in1=yb, op0=ALU.mult, op1=ALU.add)),
                        reads=[R_CB[ol], "cols"], writes=[R_YB])
                    S.op("dve", (lambda e, ol=ol, c_w2=c_w2: e.scalar_tensor_tensor(
                        out=yb, in0=cb[ol][:, 2:1026], scalar=cols[:, c_w2:c_w2 + 1], in1=yb, op0=ALU.mult, op1=ALU.add)),
                        reads=[R_CB[ol], "cols"], writes=[R_YB])
                    for blk in range(2):
                        t0 = OWN0 + blk * 512
                        b = fm_group(wv, wreg, ol, lambda kc, t0=t0: h1T[:, kc, t0:t0 + 512], 512, t0)
                        S.op("dve", (lambda e, b=b, blk=blk: e.tensor_tensor(
                            out=yb[:, blk * 512:(blk + 1) * 512], in0=yb[:, blk * 512:(blk + 1) * 512], in1=ps[:, b, :], op=ALU.mult)),
                            writes=[R_YB], xs=[PS(b)])
                    while ssq_pending:
                        ssq_pending.pop(0)()
                    S.op("act", (lambda e, ci=ci, c_g=c_g: e.activation(out=ycT[:, ci, :], in_=yb, func=AF.Copy,
                                                                         scale=cols[:, c_g:c_g + 1])),
                         reads=[R_YB, "cols"], writes=[R_YCT])
                    S.op("act", lambda e: e.activation(out=ysq, in_=yb, func=AF.Square), reads=[R_YB], writes=[R_YSQ])

                    def ssq_flush():
                        def mm(e):
                            ins = None
                            for tt in range(8):
                                ins = e.matmul(ps[:, 7, tt:tt + 1], lhsT=ysq[:, tt * 128:(tt + 1) * 128], rhs=ones[:, 0:1],
                                               start=True, stop=True)
                            return ins
                        S.op("pe", mm, reads=[R_YSQ, "ones"], writes=[PS(7)])
                        S.op("dve", lambda e: e.tensor_tensor(out=stat[:, C_SSQC:C_SSQC + 8], in0=stat[:, C_SSQC:C_SSQC + 8],
                                                              in1=ps[:, 7, 0:8], op=ALU.add),
                             writes=[("stat", C_SSQC, C_SSQC + 8)], xs=[PS(7)])
                    ssq_pending.append(ssq_flush)
            tasks.append((ld_c, cp_c))
            tasks.append((ld_u, cp_u))
            tasks.append((ld_b, cp_b))

        run_tasks(tasks)
        while ssq_pending:
            ssq_pending.pop(0)()
        rstd_ops(C_SSQC, C_RSC, 8, 1.0 / 1024)

        if dbg:
            for i, (nm, off, n) in enumerate((("d_qT", OFF_QT, 8192), ("d_kT", OFF_KT, 8 * TOK), ("d_V", OFF_V, NT * 1040),
                                              ("d_ycT", OFF_YCT, 8192))):
                S.dma("sp", (lambda e, nm=nm, off=off, n=n: e.dma_start(out=dbg_out[nm][:, :], in_=arena[:, off // 2:off // 2 + n])),
                      reads=[AR(off, n * 2)], writes=[f"dbg{i + 1}"], key=f"dbg{i + 1}")
                S.final_keys.append(f"dbg{i + 1}")

        tb = [view(OFF_TB + i * 6144, F32, [1536]) for i in range(4)]
        R_TB = [AR(OFF_TB + i * 6144, 6144) for i in range(4)]
        sb = [view(OFF_SB + i * 6144, F32, [1536]) for i in range(2)]
        R_SB = [AR(OFF_SB + i * 6144, 6144) for i in range(2)]
        PT = [view(OFF_PT + i * 3072, BF16, [1536]) for i in range(2)]
        R_PT = [AR(OFF_PT + i * 3072, 3072) for i in range(2)]
        ptm_offs = [OFF_PTM, OFF_PTM + 512, OFF_GAB + 4096]
        PTm = [view(o, BF16, [256]) for o in ptm_offs]
        R_PTM = [AR(o, 512) for o in ptm_offs]
        yatok = view(OFF_YATOK, F32, [2, 1024])
        R_YATOK = AR(OFF_YATOK, 8192)
        yan = view(OFF_YAN, BF16, [2, 1024])
        R_YAN = AR(OFF_YAN, 4096)
        gab = view(OFF_GAB, F32, [1024])
        R_GAB = AR(OFF_GAB, 4096)
        S.dma("sp", lambda e: e.dma_start(out=gab, in_=gab_d[:, :]), writes=[R_GAB], key="gab")

        def c_tab(n_):
            m, h = n_ // 16, n_ % 16
            S.dma("sp", (lambda e: e.dma_start(out=tb[n_ % 4], in_=tab[m, h])), writes=[R_TB[n_ % 4]], key=f"tb{n_ % 4}")

        def c_stage1(n_):
            m, h = n_ // 16, n_ % 16
            par = n_ % 2
            p3 = n_ % 3
            t4 = n_ % 4
            hc, hb = h // 2, 64 * (h % 2)

            def qk(e):
                ins = None
                rhs = qT[hb:hb + 64, hc, m * 256:(m + 1) * 256]
                ins = e.matmul(ps[0:16, 6, 0:256], lhsT=kT[hb:hb + 64, hc, 1536:1552], rhs=rhs, start=True, stop=True)
                for po in range(6):
                    kt = 2 * m + po
                    ins = e.matmul(ps[:, 3 * par + po // 2, (po % 2) * 256:(po % 2) * 256 + 256],
                                   lhsT=kT[hb:hb + 64, hc, kt * 128:(kt + 1) * 128], rhs=rhs, start=True, stop=True)
                return ins
            S.op("pe", qk, reads=[R_KT, R_QT], writes=[PS(3 * par, 3 * par + 3), PS(6)])
            S.op("act", (lambda e: e.activation(out=PTm[p3][0:16, :], in_=ps[0:16, 6, 0:256], func=AF.Exp)),
                 writes=[R_PTM[p3]], xs=[PS(6)])
            S.op("dve", (lambda e: e.tensor_tensor(out=ps[:, 3 * par:3 * par + 3, :], in0=ps[:, 3 * par:3 * par + 3, :],
                                                   in1=tb[t4].rearrange("p (a b) -> p a b", a=3), op=ALU.add)),
                 reads=[R_TB[t4]], xs=[PS(3 * par, 3 * par + 3)])
            if "P" not in opts:
                c_exp(n_)

        def c_exp(n_):
            par = n_ % 2
            S.op("act", (lambda e: e.activation(out=PT[par].rearrange("p (a b) -> p a b", a=3), in_=ps[:, 3 * par:3 * par + 3, :],
                                                func=AF.Exp)),
                 writes=[R_PT[par]], xs=[PS(3 * par, 3 * par + 3)])

        def c_stage1b(n_):
            par = n_ % 2
            S.op("act", (lambda e: e.activation(out=PT[par], in_=sb[par], func=AF.Exp)),
                 reads=[R_SB[par]], writes=[R_PT[par]])

        def c_stage2(n_):
            m, h = n_ // 16, n_ % 16
            par = n_ % 2
            p3 = n_ % 3

            def pv(e):
                ins = None
                for half in range(2):
                    o = ps[:, 7, half * 65:half * 65 + 65]
                    for po in range(6):
                        ins = e.matmul(o, lhsT=PT[par][:, po * 256 + half * 128:po * 256 + half * 128 + 128],
                                       rhs=V[:, 2 * m + po, h * 65:(h + 1) * 65], start=(po == 0), stop=False)
                    ins = e.matmul(o, lhsT=PTm[p3][0:16, half * 128:(half + 1) * 128], rhs=V[0:16, 12, h * 65:(h + 1) * 65],
                                   start=False, stop=True)
                return ins
            S.op("pe", pv, reads=[R_PT[par], R_PTM[p3], R_V], writes=[PS(7)])
            S.op("dve", lambda e: e.reciprocal(out=rec[:, 0:2], in_=ps[:, 7, 64:130:65]), writes=["rec"], xs=[PS(7)])
            S.op("dve", (lambda e: e.tensor_tensor(
                out=yatok[:, :, h * 64:(h + 1) * 64],
                in0=ps[:, 7, 0:130].rearrange("p (a b) -> p a b", a=2)[:, :, 0:64],
                in1=rec[:, 0:2].unsqueeze(2).to_broadcast([128, 2, 64]), op=ALU.mult)),
                reads=["rec"], writes=[R_YATOK], xs=[PS(7)])
            if h != 15:
                return
            for half in range(2):
                tt = 2 * m + half
                S.op("act", (lambda e, half=half, tt=tt: e.activation(out=yan[:, half, :], in_=yatok[:, half, :], func=AF.Square,
                                                                     accum_out=stat[:, C_SSQA + tt:C_SSQA + tt + 1])),
                     reads=[R_YATOK], writes=[R_YAN, ("stat", C_SSQA + tt)])
            rstd_ops(C_SSQA + 2 * m, C_RSA + 2 * m, 2, 1.0 / 1024)
            for half in range(2):
                tt = 2 * m + half
                S.op("dve", (lambda e, half=half, tt=tt: e.scalar_tensor_tensor(
                    out=yan[:, half, :], in0=yatok[:, half, :], scalar=stat[:, C_RSA + tt:C_RSA + tt + 1], in1=gab,
                    op0=ALU.mult, op1=ALU.mult)), reads=[R_YATOK, R_GAB, ("stat", C_RSA + tt)], writes=[R_YAN])
            for half in range(2):
                tt = 2 * m + half
                b = 6 + half
                pb = psbank_bf(b)

                def tr(e, half=half, pb=pb):
                    ins = None
                    for c in range(8):
                        ins = e.transpose(out=pb[:, c * 128:(c + 1) * 128], in_=yan[:, half, c * 128:(c + 1) * 128], identity=ident[:])
                    return ins
                S.op("pe", tr, reads=[R_YAN, "ident"], writes=[PS(b)])
                S.op("act", (lambda e, tt=tt, pb=pb: e.copy(out=yaT[:, 0:8, tt * 128:(tt + 1) * 128],
                                                           in_=pb.rearrange("p (a b) -> p a b", a=8))),
                     writes=[R_YAT], xs=[PS(b)])

        if "Q" in opts:
            c_stage1(0)
            c_stage1(1)
            c_stage1b(0)
            for n_ in range(64):
                if n_ + 2 < 64:
                    c_stage1(n_ + 2)
                if n_ + 1 < 64:
                    c_stage1b(n_ + 1)
                c_stage2(n_)
        elif "P" in opts:
            for n_ in range(4):
                c_tab(n_)
            c_stage1(0)
            c_exp(0)
            c_stage1(1)
            c_exp(1)
            for n_ in range(64):
                if n_ + 4 < 64:
                    c_tab(n_ + 4)
                if n_ + 2 < 64:
                    c_stage1(n_ + 2)
                c_stage2(n_)
                if n_ + 2 < 64:
                    c_exp(n_ + 2)
        else:
            for n_ in range(3):
                c_tab(n_)
            c_stage1(0)
            for n_ in range(64):
                if n_ + 3 < 64:
                    c_tab(n_ + 3)
                if n_ + 1 < 64:
                    c_stage1(n_ + 1)
                c_stage2(n_)

        if dbg:
            S.dma("sp", lambda e: e.dma_start(out=dbg_out["d_yaT"][:, :], in_=arena[:, OFF_YAT // 2:OFF_YAT // 2 + 8192]),
                  reads=[R_YAT], writes=["dbg5"], key="dbg5")
            S.final_keys.append("dbg5")

        for tt in range(8):
            S.dma("sp", (lambda e, tt=tt: e.dma_start(out=resid[:, tt, :], in_=xl[OWN0 + tt * 128:OWN0 + (tt + 1) * 128, :])),
                  writes=[R_RES(tt)], key=f"res{tt}")
        g2b = view(OFF_G2B, F32, [2048])
        R_G2B = AR(OFF_G2B, 8192)
        S.dma("sp", lambda e: e.dma_start(out=g2b, in_=g2b_d[:, :]), writes=[R_G2B], key="g2b")
        xn2 = [view(OFF_XN2 + i * 4096, BF16, [2048]) for i in range(2)]
        R_XN2 = [AR(OFF_XN2 + i * 4096, 4096) for i in range(2)]
        junk2 = view(OFF_JUNK2, BF16, [2048])
        R_JUNK2 = AR(OFF_JUNK2, 4096)

        def e_norm_a(tt):
            norm_a(resid[:, tt, :], R_RES(tt), junk2, R_JUNK2, C_SSQ2, C_RS2, tt)

        def e_norm_b(tt):
            norm_b(resid[:, tt, :], R_RES(tt), xn2[tt % 2], R_XN2[tt % 2], g2b, R_G2B, C_RS2, tt)

        def e_transpose(tt):
            regs = [("h2t", tt)] + [AR(OFF_H2T + (c * 1024 + tt * 128) * 2, 256) for c in range(16)]
            transpose_stage(xn2[tt % 2], R_XN2[tt % 2], h2T, regs, tt * 128)

        tasks = []
        for cbk in range(4):
            def ld(cbk=cbk):
                return w_load(w_out_v, 0, 16, cbk * 512, 512)

            def cp(h, cbk=cbk):
                wv, wreg = h
                for tt in range(8):
                    ba = next_bank(8)
                    bb = next_bank(8)

                    def mm(e, tt=tt, ba=ba, bb=bb):
                        ins = None
                        for kc in range(8):
                            ins = e.matmul(ps[:, ba, :], lhsT=ycT[:, kc, tt * 128:(tt + 1) * 128], rhs=wv[:, kc, :],
                                           start=(kc == 0), stop=(kc == 7))
                        for kc in range(8):
                            ins = e.matmul(ps[:, bb, :], lhsT=yaT[:, kc, tt * 128:(tt + 1) * 128], rhs=wv[:, 8 + kc, :],
                                           start=(kc == 0), stop=(kc == 7))
                        return ins
                    S.op("pe", mm, reads=[wreg, R_YCT, R_YAT], writes=[PS(ba), PS(bb)])
                    rs = resid[:, tt, cbk * 512:(cbk + 1) * 512]
                    S.op("dve", (lambda e, rs=rs, tt=tt, ba=ba: e.scalar_tensor_tensor(
                        out=rs, in0=ps[:, ba, :], scalar=stat[:, C_RSC + tt:C_RSC + tt + 1], in1=rs, op0=ALU.mult, op1=ALU.add)),
                        reads=[("stat", C_RSC + tt)], writes=[R_RES(tt)], xs=[PS(ba)])
                    S.op("dve", (lambda e, rs=rs, bb=bb: e.tensor_tensor(out=rs, in0=rs, in1=ps[:, bb, :], op=ALU.add)),
                         writes=[R_RES(tt)], xs=[PS(bb)])
                    if cbk == 3 and "E" in opts:
                        e_norm_a(tt)
                        if tt >= 1:
                            e_norm_b(tt - 1)
                        if tt >= 2:
                            e_transpose(tt - 2)
                if cbk == 3 and "E" in opts:
                    e_norm_b(7)
                    e_transpose(6)
                    e_transpose(7)
            tasks.append((ld, cp))
        run_tasks(tasks)
        if "E" not in opts:
            for tt in range(8):
                e_norm_a(tt)
                e_norm_b(tt)
                e_transpose(tt)

        gfb = view(OFF_GFB, F32, [2048])
        R_GFB = AR(OFF_GFB, 8192)
        S.dma("sp", lambda e: e.dma_start(out=gfb, in_=gfb_d[:, :]), writes=[R_GFB], key="gfb")

        if dbg:
            S.dma("sp", lambda e: e.dma_start(out=dbg_out["d_res"][:, :], in_=arena[:, 0:32768].bitcast(F32)),
                  reads=[AR(0, 65536)], writes=["dbg6"], key="dbg6")
            S.final_keys.append("dbg6")
            S.dma("sp", lambda e: e.dma_start(out=dbg_out["d_h2T"][:, :], in_=arena[:, OFF_H2T // 2:OFF_H2T // 2 + 16384]),
                  reads=[R_H2T, "h2t"], writes=["dbg7"], key="dbg7")
            S.final_keys.append("dbg7")

        rt = [view(OFF_RT + i * 2048, F32, [512]) for i in range(4)]
        R_RT = [AR(OFF_RT + i * 2048, 2048) for i in range(4)]
        rt_rr = {"i": 0}
        up_tasks, dn_tasks = [], []
        for p in range(8):
            pp = p % 2
            for j in range(2):
                def ld(p=p, j=j):
                    return w_load(w_up_v, 0, 16, p * 1024 + j * 512, 512)

                def cp(h, p=p, j=j, pp=pp):
                    wv, wreg = h
                    for fl in range(4):
                        f = j * 4 + fl
                        for blk in range(2):
                            b = next_bank(8)

                            def mm(e, b=b, fl=fl, blk=blk):
                                ins = None
                                for kc in range(16):
                                    ins = e.matmul(ps[:, b, :], lhsT=wv[:, kc, fl * 128:(fl + 1) * 128],
                                                   rhs=h2T[:, kc, blk * 512:(blk + 1) * 512], start=(kc == 0), stop=(kc == 15))
                                return ins
                            S.op("pe", mm, reads=[wreg, ("h2t", 4 * blk, 4 * blk + 4)], writes=[PS(b)])
                            ri = rt_rr["i"] % 4
                            rt_rr["i"] += 1
                            S.op("act", (lambda e, b=b, ri=ri: e.activation(out=rt[ri], in_=ps[:, b, :], func=AF.Relu)),
                                 writes=[R_RT[ri]], xs=[PS(b)])
                            S.op("dve", (lambda e, ri=ri, f=f, blk=blk, pp=pp: e.tensor_tensor(
                                out=hid[pp][:, f, blk * 512:(blk + 1) * 512], in0=rt[ri], in1=rt[ri], op=ALU.mult)),
                                reads=[R_RT[ri]], writes=[R_HID[pp]])
                up_tasks.append((ld, cp))
            for j in range(2):
                def ld(p=p, j=j):
                    return w_load(w_down_v, p * 8, 8, j * 1024, 1024)

                def cp(h, p=p, j=j, pp=pp):
                    wv, wreg = h
                    for tt in range(8):
                        for cbl in range(2):
                            b = next_bank(8)

                            def mm(e, b=b, tt=tt, cbl=cbl):
                                ins = None
                                for kc in range(8):
                                    ins = e.matmul(ps[:, b, :], lhsT=hid[pp][:, kc, tt * 128:(tt + 1) * 128],
                                                   rhs=wv[:, kc, cbl * 512:(cbl + 1) * 512], start=(kc == 0), stop=(kc == 7))
                                return ins
                            S.op("pe", mm, reads=[wreg, R_HID[pp]], writes=[PS(b)])
                            c0 = j * 1024 + cbl * 512
                            rs = resid[:, tt, c0:c0 + 512]
                            S.op("dve", (lambda e, rs=rs, b=b: e.tensor_tensor(out=rs, in0=rs, in1=ps[:, b, :], op=ALU.add)),
                                 writes=[R_RES(tt)], xs=[PS(b)])
                        if p == 7 and j == 1 and "G" in opts:
                            g_norm_a(tt)
                            if tt >= 1:
                                g_norm_b(tt - 1)
                    if p == 7 and j == 1 and "G" in opts:
                        g_norm_b(7)
                dn_tasks.append((ld, cp))

        junkg = view(OFF_JUNKG, BF16, [2048])
        R_JUNKG = AR(OFF_JUNKG, 4096)

        def g_norm_a(tt):
            norm_a(resid[:, tt, :], R_RES(tt), junkg, R_JUNKG, C_SSQF, C_RSF, tt)

        def g_norm_b(tt):
            norm_b(resid[:, tt, :], R_RES(tt), resid[:, tt, :], R_RES(tt), gfb, R_GFB, C_RSF, tt)
            S.dma("sp", (lambda e: e.dma_start(out=out_d[tt * 128:(tt + 1) * 128, :], in_=resid[:, tt, :])),
                  reads=[R_RES(tt)], writes=[("outd", tt)], key=f"out{tt % 2}")

        tasks = up_tasks[0:2]
        for p in range(8):
            if p + 1 < 8:
                tasks += up_tasks[2 * (p + 1):2 * (p + 1) + 2]
            tasks += dn_tasks[2 * p:2 * p + 2]
        run_tasks(tasks)
        if "G" not in opts:
            for tt in range(8):
                g_norm_a(tt)
                g_norm_b(tt)
        S.final_keys += ["out0", "out1"]
        if dbg:
            S.dma("sp", lambda e: e.dma_start(out=dbg_out["d_stat"][:, :], in_=stat[:]), reads=["stat"], writes=["dbg8"], key="dbg8")
            S.final_keys.append("dbg8")
        S.emit(nc)
    return nc


def _bias_table(rpb, i):
    R0 = 16 * i
    m = np.arange(4)[:, None, None, None, None]
    kp = np.arange(128)[None, :, None, None, None]
    po = np.arange(6)[None, None, :, None, None]
    a = np.arange(4)[None, None, None, :, None]
    qc = np.arange(64)[None, None, None, None, :]
    kb, kc = kp // 64, kp % 64
    gk = R0 - 4 + 4 * m + 2 * po + kb
    r = R0 + 4 * m + a
    rs = np.clip(r - 4, 0, 56)
    c0 = np.clip(qc - 8, 0, 48)
    valid = (gk >= 0) & (gk < 64) & (gk >= rs) & (gk < rs + 8) & (kc >= c0) & (kc < c0 + 16)
    dr = np.clip(gk - r + 7, 0, 14)
    dc = np.clip(kc - qc + 15, 0, 30)
    shape = np.broadcast_shapes(valid.shape, dr.shape, dc.shape)
    valid = np.broadcast_to(valid, shape)
    dr = np.broadcast_to(dr, shape)
    dc = np.broadcast_to(dc, shape)
    t = rpb[:, dr, dc]
    t = np.where(valid[None], t, np.float32(NEG)).astype(np.float32)
    t = np.ascontiguousarray(t.transpose(1, 0, 2, 3, 4, 5)).reshape(4, 16, 128, 1536)
    return t


def _prep_inputs(x, meta_tokens, norm1_g, w_in, conv_w, conv_b, conv_norm_g, attn_rpb, attn_norm_g, w_out, norm2_g,
                 w_up, w_down, final_norm_g):
    import ml_dtypes
    f32 = np.float32
    x = np.asarray(x, f32)
    meta = np.asarray(meta_tokens, f32)
    shared = {
        "w_in": np.ascontiguousarray(np.asarray(w_in, f32)[0]),
        "w_out": np.ascontiguousarray(np.asarray(w_out, f32)[0]),
        "w_up": np.ascontiguousarray(np.asarray(w_up, f32)[0]),
        "w_down": np.ascontiguousarray(np.asarray(w_down, f32)[0]),
        "g1b": np.ascontiguousarray(np.broadcast_to(np.asarray(norm1_g, f32)[0][None, :], (128, 2048))),
        "g2b": np.ascontiguousarray(np.broadcast_to(np.asarray(norm2_g, f32)[0][None, :], (128, 2048))),
        "gfb": np.ascontiguousarray(np.broadcast_to(np.asarray(final_norm_g, f32)[None, :], (128, 2048))),
        "gab": np.ascontiguousarray(np.broadcast_to(np.asarray(attn_norm_g, f32)[0][None, :], (128, 1024))),
        "ident": np.eye(128, dtype=f32).astype(ml_dtypes.bfloat16),
    }
    cols = np.zeros((128, 48), f32)
    cols[:, 0:8] = np.asarray(conv_norm_g, f32)[0].reshape(8, 128).T
    cw = np.asarray(conv_w, f32)[0]
    cols[:, 8:32] = cw.reshape(3, 8, 128).transpose(2, 1, 0).reshape(128, 24)
    cols[:, 32:40] = np.asarray(conv_b, f32)[0].reshape(8, 128).T
    shared["cols"] = cols
    rpb = np.asarray(attn_rpb, f32)[0]
    tabs = [_bias_table(rpb, i) for i in range(4)]
    in_maps = []
    for c in range(8):
        b, i = c // 4, c % 4
        xl = np.zeros((TOK, D_MODEL), f32)
        g0 = 16 * i - 4
        lo, hi = max(g0, 0), min(g0 + 24, 64)
        xl[(lo - g0) * 64:(hi - g0) * 64] = x[b, lo * 64:hi * 64]
        if i == 0:
            xl[255] = meta[15]
        xl[1536:1552] = meta
        mp = dict(shared)
        mp["xl"] = xl
        mp["tab"] = tabs[i]
        in_maps.append(mp)
    return in_maps


_NC = {}


def kernel(**inputs):
    in_maps = _prep_inputs(**inputs)
    if "nc" not in _NC:
        _NC["nc"] = build_nc()
    res = run_bass_kernel_spmd(_NC["nc"], in_maps, core_ids=list(range(8)))
    out = np.empty((2, SEQ, D_MODEL), np.float32)
    for c in range(8):
        b, i = c // 4, c % 4
        out[b, i * 1024:(i + 1) * 1024] = np.asarray(res.results[c]["out"], np.float32)
    return out
```

```python
import contextlib
import numpy as np
import concourse.bass as bass
import concourse.mybir as mybir
from concourse.bass_utils import run_bass_kernel_spmd

F32 = mybir.dt.float32
BF16 = mybir.dt.bfloat16
ALU = mybir.AluOpType
AF = mybir.ActivationFunctionType

D_MODEL = 2048
SEQ = 4096
N_META = 16
D_FF = 8192
EPS = 1e-6
NEG = -1e30
NT = 13
TOK = NT * 128
OWN0 = 256
NOWN = 1024


class Op:
    __slots__ = ("eng", "fn", "dma", "key", "sig", "sigidx", "waits", "cnt", "deps")

    def __init__(self, eng, fn, dma, key):
        self.eng = eng
        self.fn = fn
        self.dma = dma
        self.key = key
        self.sig = False
        self.sigidx = 0
        self.waits = {}
        self.cnt = 0
        self.deps = []


class Sched:
    ENGS = ("pe", "act", "dve", "pool", "sp")

    def __init__(self):
        self.ops = {e: [] for e in self.ENGS}
        self.state = {}
        self.dma_cnt = {}
        self.allops = []
        self.final_keys = []

    @staticmethod
    def _norm(r):
        if isinstance(r, str):
            return (r, 0, 1 << 40)
        if len(r) == 2:
            return (r[0], r[1], r[1] + 1)
        return r

    def _add(self, eng, fn, reads, writes, xs, dma, key):
        op = Op(eng, fn, dma, key)
        deps = op.deps
        for mode, regs in (("r", reads), ("x", xs), ("w", writes)):
            for r in regs:
                n, lo, hi = self._norm(r)
                st = self.state.setdefault(n, [])
                for (a, b, o, pm) in st:
                    if not (a < hi and lo < b) or o is op:
                        continue
                    if pm == "r" and mode == "r":
                        continue
                    if pm == "x" and mode == "x" and o.eng == eng and not o.dma and not dma:
                        continue
                    deps.append(o)
                if mode != "r":
                    st[:] = [(a, b, o, pm) for (a, b, o, pm) in st if not (lo <= a and b <= hi) or o is op]
                elif not dma:
                    st[:] = [(a, b, o, pm) for (a, b, o, pm) in st
                             if not (a == lo and b == hi and pm == "r" and o.eng == eng and not o.dma)]
                st.append((lo, hi, op, mode))
        if dma:
            c = self.dma_cnt.get(key, 0) + 16
            self.dma_cnt[key] = c
            op.cnt = c
        self.ops[eng].append(op)
        self.allops.append(op)
        return op

    def op(self, eng, fn, reads=(), writes=(), xs=()):
        return self._add(eng, fn, reads, writes, xs, False, None)

    def dma(self, eng, fn, reads=(), writes=(), key=None):
        return self._add(eng, fn, reads, writes, (), True, key)

    def resolve(self):
        for op in self.allops:
            for d in op.deps:
                if not d.dma:
                    if d.eng == "pe" and op.eng == "pe" and not op.dma:
                        continue
                    d.sig = True
        for e in self.ENGS:
            c = 0
            for op in self.ops[e]:
                if op.sig and not op.dma:
                    c += 1
                    op.sigidx = c
        waited = {e: {} for e in self.ENGS}
        for e in self.ENGS:
            for op in self.ops[e]:
                need = {}
                for d in op.deps:
                    if d.dma:
                        k, v = ("dma", d.key), d.cnt
                    else:
                        if d.eng == "pe" and e == "pe" and not op.dma:
                            continue
                        k, v = ("eng", d.eng), d.sigidx
                    if v > need.get(k, 0):
                        need[k] = v
                for k, v in need.items():
                    if waited[e].get(k, 0) >= v:
                        continue
                    waited[e][k] = v
                    op.waits[k] = v

    def emit(self, nc):
        self.resolve()
        with contextlib.ExitStack() as es:
            sems = {}
            for e in self.ENGS:
                sems[("eng", e)] = es.enter_context(nc.semaphore("s_" + e))
            for k in self.dma_cnt:
                sems[("dma", k)] = es.enter_context(nc.semaphore("d_" + str(k)))
            block = es.enter_context(nc.Block())

            def run(engname, eng):
                for op in self.ops[engname]:
                    for k, v in op.waits.items():
                        eng.wait_ge(sems[k], v)
                    ins = op.fn(eng)
                    if op.dma:
                        ins.then_inc(sems[("dma", op.key)], 16)
                    elif op.sig:
                        ins.then_inc(sems[("eng", engname)], 1)

            @block.sync
            def _(eng):
                run("sp", eng)
                for k in self.final_keys:
                    eng.wait_ge(sems[("dma", k)], self.dma_cnt[k])

            @block.scalar
            def _(eng):
                run("act", eng)

            @block.vector
            def _(eng):
                run("dve", eng)

            @block.gpsimd
            def _(eng):
                run("pool", eng)

            @block.tensor
            def _(eng):
                run("pe", eng)


OFF_RESID = 0
OFF_H1T = 0
OFF_QT = 53248
OFF_KT = 69632
OFF_V = 96256
OFF_YCT = 123392
OFF_YAT = 139776
OFF_W = 156160
ARENA = 205312
OFF_H2T = 65536
OFF_HID = 98304
OFF_XS = 123392
OFF_XN = 139776
OFF_G1B = 147968
OFF_CB = 139776
OFF_YB = 139776 + 8256
OFF_YSQ = OFF_YB + 4096
OFF_TB = 0
OFF_SB = 12288
OFF_PT = 24576
OFF_PTM = 30720
OFF_YATOK = 31744
OFF_YAN = 39936
OFF_GAB = 44032
OFF_XN2 = 98304
OFF_G2B = 106496
OFF_JUNK2 = 114688
OFF_JUNKG = 139264
OFF_RT = 131072
OFF_GFB = 147456


OPTS = "ACEGP"


def build_nc(dbg=False, opts=None):
    import os
    opts = os.environ.get("KOPTS", OPTS) if opts is None else opts
    nc = bass.Bass("TRN2", target_bir_lowering=False)

    def din(name, shape, dt=F32):
        return nc.dram_tensor(name, shape, dt, kind="ExternalInput").ap()

    xl = din("xl", [TOK, D_MODEL])
    w_in = din("w_in", [2048, 6144])
    w_out = din("w_out", [2048, 2048])
    w_up = din("w_up", [2048, 8192])
    w_down = din("w_down", [8192, 2048])
    tab = din("tab", [4, 16, 128, 1536])
    g1b_d = din("g1b", [128, 2048])
    g2b_d = din("g2b", [128, 2048])
    gfb_d = din("gfb", [128, 2048])
    gab_d = din("gab", [128, 1024])
    cols_d = din("cols", [128, 48])
    ident_d = din("ident", [128, 128], BF16)
    out_d = nc.dram_tensor("out", [NOWN, D_MODEL], F32, kind="ExternalOutput").ap()
    dbg_out = {}
    if dbg:
        for nm, n, dt in (("d_h1T", 16 * TOK, BF16), ("d_qT", 8 * 1024, BF16), ("d_kT", 8 * TOK, BF16),
                          ("d_V", NT * 1040, BF16), ("d_ycT", 8 * 1024, BF16), ("d_yaT", 8 * 1024, BF16),
                          ("d_res", 8 * 2048, F32), ("d_stat", 128, F32), ("d_h2T", 16 * 1024, BF16)):
            dbg_out[nm] = nc.dram_tensor(nm, [128, n], dt, kind="ExternalOutput").ap()

    w_in_v = w_in.rearrange("(kc p) n -> p kc n", p=128)
    w_out_v = w_out.rearrange("(kc p) n -> p kc n", p=128)
    w_up_v = w_up.rearrange("(kc p) n -> p kc n", p=128)
    w_down_v = w_down.rearrange("(kc p) n -> p kc n", p=128)

    S = Sched()
    with contextlib.ExitStack() as es:
        arena = es.enter_context(nc.sbuf_tensor("arena", [128, ARENA // 2], BF16))
        ps = es.enter_context(nc.psum_tensor("ps", [128, 8, 512], F32))

        def small(name, shape, dt=F32):
            return es.enter_context(nc.sbuf_tensor(name, shape, dt))

        ident = small("identsb", [128, 128], BF16)
        cols = small("colsb", [128, 48])
        stat = small("stat", [128, 128])
        epsb = small("epsb", [128, 1])
        ones = small("ones", [128, 2], BF16)
        rec = small("rec", [128, 2])
        C_SSQ1, C_RS1, C_SSQC, C_RSC, C_SSQA, C_RSA, C_SSQ2, C_RS2, C_SSQF, C_RSF, C_TMP = 0, 13, 26, 34, 42, 50, 80, 88, 96, 104, 64

        def view(off, dt, dims):
            es_ = 4 if dt == F32 else 2
            n = int(np.prod(dims))
            a = arena[:, off // 2:(off + n * es_) // 2]
            if dt == F32:
                a = a.bitcast(F32)
            if len(dims) == 2:
                a = a.rearrange("p (a b) -> p a b", a=dims[0])
            elif len(dims) == 3:
                a = a.rearrange("p (a b c) -> p a b c", a=dims[0], b=dims[1])
            return a

        def AR(off, nbytes):
            return ("A", off, off + nbytes)

        def PS(b0, b1=None):
            return ("ps", b0, (b0 + 1) if b1 is None else b1)

        h1T = view(OFF_H1T, BF16, [16, TOK])
        R_H1T = AR(OFF_H1T, 16 * TOK * 2)
        qT = view(OFF_QT, BF16, [8, 1024])
        R_QT = AR(OFF_QT, 16384)
        kT = view(OFF_KT, BF16, [8, TOK])
        R_KT = AR(OFF_KT, 8 * TOK * 2)
        V = view(OFF_V, BF16, [NT, 1040])
        R_V = AR(OFF_V, NT * 1040 * 2)
        ycT = view(OFF_YCT, BF16, [8, 1024])
        R_YCT = AR(OFF_YCT, 16384)
        yaT = view(OFF_YAT, BF16, [8, 1024])
        R_YAT = AR(OFF_YAT, 16384)
        resid = view(OFF_RESID, F32, [8, 2048])

        def R_RES(tt):
            return AR(OFF_RESID + tt * 8192, 8192)

        h2T = view(OFF_H2T, BF16, [16, 1024])
        R_H2T = AR(OFF_H2T, 32768)
        hid = [view(OFF_HID + i * 16384, BF16, [8, 1024]) for i in range(2)]
        R_HID = [AR(OFF_HID + i * 16384, 16384) for i in range(2)]

        def psbank(b):
            return ps[:, b, :]

        def psbank_bf(b):
            return ps[:, b, :].bitcast(BF16)

        wstate = {"p": 0}

        def alloc_slots(ns):
            p = wstate["p"]
            if ns == 2 and p % 2 == 1:
                p = (p + 1) % 6
            if p + ns > 6:
                p = 0
            wstate["p"] = (p + ns) % 6
            return p

        def w_load(src, kc0, KC, c0, N):
            nbytes = KC * N * 2
            ns = nbytes // 8192
            assert ns in (1, 2) and nbytes == ns * 8192
            s0 = alloc_slots(ns)
            off = OFF_W + s0 * 8192
            wv = view(off, BF16, [KC, N])
            reg = AR(off, nbytes)
            step = max(1, (1 << 20) // (128 * N * 4))
            for k0 in range(0, KC, step):
                k1 = min(KC, k0 + step)
                S.dma("pool", (lambda e, k0=k0, k1=k1: e.dma_start(out=wv[:, k0:k1, :], in_=src[:, kc0 + k0:kc0 + k1, c0:c0 + N])),
                      writes=[reg], key=f"w{s0}")
            return wv, reg

        def run_tasks(tasks, look=2):
            handles = {}
            n = len(tasks)
            for j in range(min(look, n)):
                handles[j] = tasks[j][0]()
            for k in range(n):
                if k + look < n:
                    handles[k + look] = tasks[k + look][0]()
                tasks[k][1](handles.pop(k))

        bank_rr = {"i": 0}

        def next_bank(nb=6):
            b = bank_rr["i"] % nb
            bank_rr["i"] += 1
            return b

        evac_rr = {"i": 0}

        def evac_eng():
            evac_rr["i"] += 1
            return "act" if evac_rr["i"] % 2 == 0 else "dve"

        def copy_op(eng, out, in_, reads, writes, xs):
            if eng == "act":
                S.op("act", lambda e: e.copy(out=out, in_=in_), reads=reads, writes=writes, xs=xs)
            else:
                S.op("dve", lambda e: e.tensor_copy(out=out, in_=in_), reads=reads, writes=writes, xs=xs)

        def rstd_ops(c_ssq, c_rs, n, scale, C_TMP=112):
            S.op("act", lambda e: e.activation(out=stat[:, C_TMP:C_TMP + n], in_=stat[:, c_ssq:c_ssq + n], func=AF.Sqrt,
                                               scale=scale, bias=epsb[:, 0:1]),
                 reads=[("stat", c_ssq, c_ssq + n), "epsb"], writes=[("stat", C_TMP, C_TMP + n)])
            S.op("dve", lambda e: e.reciprocal(out=stat[:, c_rs:c_rs + n], in_=stat[:, C_TMP:C_TMP + n]),
                 reads=[("stat", C_TMP, C_TMP + n)], writes=[("stat", c_rs, c_rs + n)])

        S.dma("sp", lambda e: e.dma_start(out=ident[:], in_=ident_d[:, :]), writes=["ident"], key="ident")
        S.dma("sp", lambda e: e.dma_start(out=cols[:], in_=cols_d[:, :]), writes=["cols"], key="cols")
        S.op("dve", lambda e: e.memset(stat[:], 0.0), writes=["stat"])
        S.op("dve", lambda e: e.memset(epsb[:], EPS), writes=["epsb"])
        S.op("dve", lambda e: e.memset(ones[:], 1.0), writes=["ones"])

        g1b = view(OFF_G1B, F32, [2048])
        R_G1B = AR(OFF_G1B, 8192)
        S.dma("sp", lambda e: e.dma_start(out=g1b, in_=g1b_d[:, :]), writes=[R_G1B], key="g1b")
        xs_offs = [OFF_XS, OFF_XS + 8192, OFF_V, OFF_V + 8192]
        xs_b = [view(o, F32, [2048]) for o in xs_offs]
        R_XS = [AR(o, 8192) for o in xs_offs]
        xn_b = [view(OFF_XN + i * 4096, BF16, [2048]) for i in range(2)]
        R_XN = [AR(OFF_XN + i * 4096, 4096) for i in range(2)]

        def norm_a(src_ap, src_reg, junk, junk_reg, c_ssq, c_rs, t):
            S.op("act", lambda e: e.activation(out=junk, in_=src_ap, func=AF.Square, accum_out=stat[:, c_ssq + t:c_ssq + t + 1]),
                 reads=[src_reg], writes=[junk_reg, ("stat", c_ssq + t)])
            S.op("act", lambda e: e.activation(out=stat[:, C_TMP + t:C_TMP + t + 1], in_=stat[:, c_ssq + t:c_ssq + t + 1], func=AF.Sqrt,
                                               scale=1.0 / D_MODEL, bias=epsb[:, 0:1]),
                 reads=[("stat", c_ssq + t), "epsb"], writes=[("stat", C_TMP + t)])

        def norm_b(src_ap, src_reg, out_ap, out_reg, gb, gb_reg, c_rs, t):
            S.op("dve", lambda e: e.reciprocal(out=stat[:, c_rs + t:c_rs + t + 1], in_=stat[:, C_TMP + t:C_TMP + t + 1]),
                 reads=[("stat", C_TMP + t)], writes=[("stat", c_rs + t)])
            S.op("dve", lambda e: e.scalar_tensor_tensor(out=out_ap, in0=src_ap, scalar=stat[:, c_rs + t:c_rs + t + 1], in1=gb,
                                                         op0=ALU.mult, op1=ALU.mult),
                 reads=[src_reg, gb_reg, ("stat", c_rs + t)], writes=[out_reg])

        def norm_stage(src_ap, src_reg, xn, xn_reg, gb, gb_reg, c_ssq, c_rs, t):
            norm_a(src_ap, src_reg, xn, xn_reg, c_ssq, c_rs, t)
            norm_b(src_ap, src_reg, xn, xn_reg, gb, gb_reg, c_rs, t)

        def transpose_stage(xn, xn_reg, dstT, dst_regs, tok0):
            for half in range(2):
                b = next_bank(8)
                pb = psbank_bf(b)

                def tr(e, half=half, pb=pb):
                    ins = None
                    for c in range(8):
                        cc = half * 8 + c
                        ins = e.transpose(out=pb[:, c * 128:(c + 1) * 128], in_=xn[:, cc * 128:(cc + 1) * 128], identity=ident[:])
                    return ins
                S.op("pe", tr, reads=[xn_reg, "ident"], writes=[PS(b)])
                copy_op(evac_eng(), dstT[:, half * 8:half * 8 + 8, tok0:tok0 + 128],
                        pb.rearrange("p (a b) -> p a b", a=8), [], dst_regs, [PS(b)])

        a_order = [12] + list(range(12))

        def a_load(idx):
            t = a_order[idx]
            i = idx % 4
            S.dma("sp", (lambda e, t=t, i=i: e.dma_start(out=xs_b[i], in_=xl[t * 128:(t + 1) * 128, :])),
                  writes=[R_XS[i]], key=f"xs{i}")

        junka = view(OFF_V + 16384, BF16, [2048])
        R_JUNKA = AR(OFF_V + 16384, 4096)

        def a_norm_a(idx):
            norm_a(xs_b[idx % 4], R_XS[idx % 4], junka, R_JUNKA, C_SSQ1, C_RS1, a_order[idx])

        def a_norm_b(idx):
            norm_b(xs_b[idx % 4], R_XS[idx % 4], xn_b[idx % 2], R_XN[idx % 2], g1b, R_G1B, C_RS1, a_order[idx])

        def a_tr(idx):
            t = a_order[idx]
            transpose_stage(xn_b[idx % 2], R_XN[idx % 2], h1T, [("h1t", t)], t * 128)

        for idx in range(4):
            a_load(idx)
        a_norm_a(0)
        a_norm_a(1)
        a_norm_b(0)
        a_load(4)
        for idx in range(NT):
            if idx + 2 < NT:
                a_norm_a(idx + 2)
            if idx + 1 < NT:
                a_norm_b(idx + 1)
            if idx + 5 < NT:
                a_load(idx + 5)
            a_tr(idx)

        if dbg:
            S.dma("sp", lambda e: e.dma_start(out=dbg_out["d_h1T"][:, :], in_=arena[:, OFF_H1T // 2:OFF_H1T // 2 + 16 * TOK]),
                  reads=[R_H1T, "h1t"], writes=["dbg0"], key="dbg0")
            S.final_keys.append("dbg0")

        S.op("dve", lambda e: e.memset(arena[:, OFF_V // 2:OFF_V // 2 + NT * 1040], 1.0), writes=[R_V])
        Vh = V.rearrange("p t (h d) -> p t h d", h=16)

        def fm_group(wv, wreg, oc_local, rhs_ap, n, t0):
            b = next_bank(6)
            h1r = ("h1t", t0 // 128, (t0 + n - 1) // 128 + 1)

            def mm(e, b=b):
                ins = None
                for kc in range(16):
                    ins = e.matmul(ps[:, b, 0:n], lhsT=wv[:, kc, oc_local * 128:(oc_local + 1) * 128], rhs=rhs_ap(kc),
                                   start=(kc == 0), stop=(kc == 15))
                return ins
            S.op("pe", mm, reads=[wreg, R_H1T, h1r], writes=[PS(b)])
            return b

        tasks = []
        for pj in range(2):
            def ld(pj=pj):
                return w_load(w_in_v, 0, 16, 4096 + pj * 512, 512)

            def cp(h, pj=pj):
                wv, wreg = h
                for blk in (3, 0, 1, 2):
                    for ol in range(4):
                        oc = pj * 4 + ol
                        t0 = blk * 512
                        n = 512 if blk < 3 else 128
                        b = fm_group(wv, wreg, ol, lambda kc, t0=t0, n=n: h1T[:, kc, t0:t0 + n], n, t0)
                        copy_op(evac_eng(), kT[:, oc, t0:t0 + n], ps[:, b, 0:n], [], [R_KT], [PS(b)])
            tasks.append((ld, cp))
        for pj in range(2):
            def ld(pj=pj):
                return w_load(w_in_v, 0, 16, 3072 + pj * 512, 512)

            def cp(h, pj=pj):
                wv, wreg = h
                for blk in range(2):
                    for ol in range(4):
                        oc = pj * 4 + ol
                        t0 = OWN0 + blk * 512
                        b = fm_group(wv, wreg, ol, lambda kc, t0=t0: h1T[:, kc, t0:t0 + 512], 512, t0)
                        S.op("act", (lambda e, b=b, oc=oc, blk=blk: e.activation(out=qT[:, oc, blk * 512:(blk + 1) * 512], in_=ps[:, b, :],
                                                                               func=AF.Copy, scale=0.125)),
                             writes=[R_QT], xs=[PS(b)])
            tasks.append((ld, cp))
        for pj in range(2):
            def ld(pj=pj):
                return w_load(w_in_v, 0, 16, 5120 + pj * 512, 512)

            def cp(h, pj=pj):
                wv, wreg = h
                for t in range(NT):
                    b = next_bank(6)

                    def mm(e, b=b, t=t):
                        ins = None
                        for kc in range(16):
                            ins = e.matmul(ps[:, b, :], lhsT=h1T[:, kc, t * 128:(t + 1) * 128], rhs=wv[:, kc, :],
                                           start=(kc == 0), stop=(kc == 15))
                        return ins
                    S.op("pe", mm, reads=[wreg, R_H1T, ("h1t", t)], writes=[PS(b)])
                    copy_op(evac_eng(), Vh[:, t, pj * 8:pj * 8 + 8, 0:64], ps[:, b, :].rearrange("p (h d) -> p h d", h=8),
                            [], [R_V], [PS(b)])
            tasks.append((ld, cp))

        cb = [view(OFF_CB + i * 4128, F32, [1026]) for i in range(2)]
        R_CB = [AR(OFF_CB + i * 4128, 4104) for i in range(2)]
        yb = view(OFF_YB, F32, [1024])
        R_YB = AR(OFF_YB, 4096)
        ysq = view(OFF_YSQ, BF16, [1024])
        R_YSQ = AR(OFF_YSQ, 2048)

        ssq_pending = []

        def halo_group(wv, wreg, ol, col0):
            def mm(e):
                ins = None
                for kc in range(16):
                    ins = e.matmul(ps[:, 6, col0:col0 + 2], lhsT=wv[:, kc, ol * 128:(ol + 1) * 128],
                                   rhs=h1T[:, kc, 255:1281:1025], start=(kc == 0), stop=(kc == 15))
                return ins
            S.op("pe", mm, reads=[wreg, R_H1T, ("h1t", 1), ("h1t", 10)], writes=[PS(6)])

        for pj in range(4):
            def ld_c(pj=pj):
                return w_load(w_in_v, 0, 16, 1024 + pj * 256, 256)

            def cp_c(h, pj=pj):
                wv, wreg = h
                for ol in range(2):
                    for blk in range(2):
                        t0 = OWN0 + blk * 512
                        b = fm_group(wv, wreg, ol, lambda kc, t0=t0: h1T[:, kc, t0:t0 + 512], 512, t0)
                        S.op("act", (lambda e, b=b, ol=ol, blk=blk: e.copy(out=cb[ol][:, 1 + blk * 512:1 + (blk + 1) * 512], in_=ps[:, b, :])),
                             writes=[R_CB[ol]], xs=[PS(b)])
                    halo_group(wv, wreg, ol, 0)
                    S.op("act", (lambda e, ol=ol: e.copy(out=cb[ol][:, 0:1026:1025], in_=ps[:, 6, 0:2])),
                         writes=[R_CB[ol]], xs=[PS(6)])
                    while ssq_pending:
                        ssq_pending.pop(0)()

            def ld_u(pj=pj):
                return w_load(w_in_v, 0, 16, 2048 + pj * 256, 256)

            def cp_u(h, pj=pj):
                wv, wreg = h
                for ol in range(2):
                    for blk in range(2):
                        t0 = OWN0 + blk * 512
                        b = fm_group(wv, wreg, ol, lambda kc, t0=t0: h1T[:, kc, t0:t0 + 512], 512, t0)
                        S.op("dve", (lambda e, b=b, ol=ol, blk=blk: e.tensor_tensor(
                            out=cb[ol][:, 1 + blk * 512:1 + (blk + 1) * 512], in0=cb[ol][:, 1 + blk * 512:1 + (blk + 1) * 512],
                            in1=ps[:, b, :], op=ALU.mult)), writes=[R_CB[ol]], xs=[PS(b)])
                    halo_group(wv, wreg, ol, 8)
                    S.op("dve", (lambda e, ol=ol: e.tensor_tensor(out=cb[ol][:, 0:1026:1025], in0=cb[ol][:, 0:1026:1025],
                                                                 in1=ps[:, 6, 8:10], op=ALU.mult)),
                         writes=[R_CB[ol]], xs=[PS(6)])

            def ld_b(pj=pj):
                return w_load(w_in_v, 0, 16, 0 + pj * 256, 256)

            def cp_b(h, pj=pj):
                wv, wreg = h
                for ol in range(2):
                    ci = pj * 2 + ol
                    c_w0, c_w1, c_w2, c_bias, c_g = 8 + 3 * ci, 9 + 3 * ci, 10 + 3 * ci, 32 + ci, ci
                    S.op("dve", (lambda e, ol=ol, c_w1=c_w1, c_bias=c_bias: e.tensor_scalar(
                        out=yb, in0=cb[ol][:, 1:1025], scalar1=cols[:, c_w1:c_w1 + 1], scalar2=cols[:, c_bias:c_bias + 1],
                        op0=ALU.mult, op1=ALU.add)), reads=[R_CB[ol], "cols"], writes=[R_YB])
                    S.op("dve", (lambda e, ol=ol, c_w0=c_w0: e.scalar_tensor_tensor(
                        out=yb, in0=cb[ol][:, 0:1024], scalar=cols[:, c_w0:c_w0 + 1], in1=yb, op0=ALU.mult, op1=ALU.add)),
                        reads=[R_CB[ol], "cols"], writes=[R_YB])
                    S.op("dve", (lambda e, ol=ol, c_w2=c_w2: e.scalar_tensor_tensor(
                        out=yb, in0=cb[ol][:, 2:1026], scalar=cols[:, c_w2:c_w2 + 1], in1=yb, op0=ALU.mult, op1=ALU.add)),
                        reads=[R_CB[ol], "cols"], writes=[R_YB])
                    for blk in range(2):
                        t0 = OWN0 + blk * 512
                        b = fm_group(wv, wreg, ol, lambda kc, t0=t0: h1T[:, kc, t0:t0 + 512], 512, t0)
                        S.op("dve", (lambda e, b=b, blk=blk: e.tensor_tensor(
                            out=yb[:, blk * 512:(blk + 1) * 512], in0=yb[:, blk * 512:(blk + 1) * 512], in1=ps[:, b, :], op=ALU.mult)),
                            writes=[R_YB], xs=[PS(b)])
                    while ssq_pending:
                        ssq_pending.pop(0)()
                    S.op("act", (lambda e, ci=ci, c_g=c_g: e.activation(out=ycT[:, ci, :], in_=yb, func=AF.Copy,
                                                                         scale=cols[:, c_g:c_g + 1])),
                         reads=[R_YB, "cols"], writes=[R_YCT])
                    S.op("act", lambda e: e.activation(out=ysq, in_=yb, func=AF.Square), reads=[R_YB], writes=[R_YSQ])

                    def ssq_flush():
                        def mm(e):
                            ins = None
                            for tt in range(8):
                                ins = e.matmul(ps[:, 7, tt:tt + 1], lhsT=ysq[:, tt * 128:(tt + 1) * 128], rhs=ones[:, 0:1],
                                               start=True, stop=True)
                            return ins
                        S.op("pe", mm, reads=[R_YSQ, "ones"], writes=[PS(7)])
                        S.op("dve", lambda e: e.tensor_tensor(out=stat[:, C_SSQC:C_SSQC + 8], in0=stat[:, C_SSQC:C_SSQC + 8],
                                                              in1=ps[:, 7, 0:8], op=ALU.add),
                             writes=[("stat", C_SSQC, C_SSQC + 8)], xs=[PS(7)])
                    ssq_pending.append(ssq_flush)
            tasks.append((ld_c, cp_c))
            tasks.append((ld_u, cp_u))
            tasks.append((ld_b, cp_b))

        run_tasks(tasks)
        while ssq_pending:
            ssq_pending.pop(0)()
        rstd_ops(C_SSQC, C_RSC, 8, 1.0 / 1024)

        if dbg:
            for i, (nm, off, n) in enumerate((("d_qT", OFF_QT, 8192), ("d_kT", OFF_KT, 8 * TOK), ("d_V", OFF_V, NT * 1040),
                                              ("d_ycT", OFF_YCT, 8192))):
                S.dma("sp", (lambda e, nm=nm, off=off, n=n: e.dma_start(out=dbg_out[nm][:, :], in_=arena[:, off // 2:off // 2 + n])),
                      reads=[AR(off, n * 2)], writes=[f"dbg{i + 1}"], key=f"dbg{i + 1}")
                S.final_keys.append(f"dbg{i + 1}")

        tb = [view(OFF_TB + i * 6144, F32, [1536]) for i in range(4)]
        R_TB = [AR(OFF_TB + i * 6144, 6144) for i in range(4)]
        sb = [view(OFF_SB + i * 6144, F32, [1536]) for i in range(2)]
        R_SB = [AR(OFF_SB + i * 6144, 6144) for i in range(2)]
        PT = [view(OFF_PT + i * 3072, BF16, [1536]) for i in range(2)]
        R_PT = [AR(OFF_PT + i * 3072, 3072) for i in range(2)]
        ptm_offs = [OFF_PTM, OFF_PTM + 512, OFF_GAB + 4096]
        PTm = [view(o, BF16, [256]) for o in ptm_offs]
        R_PTM = [AR(o, 512) for o in ptm_offs]
        yatok = view(OFF_YATOK, F32, [2, 1024])
        R_YATOK = AR(OFF_YATOK, 8192)
        yan = view(OFF_YAN, BF16, [2, 1024])
        R_YAN = AR(OFF_YAN, 4096)
        gab = view(OFF_GAB, F32, [1024])
        R_GAB = AR(OFF_GAB, 4096)
        S.dma("sp", lambda e: e.dma_start(out=gab, in_=gab_d[:, :]), writes=[R_GAB], key="gab")

        def c_tab(n_):
            m, h = n_ // 16, n_ % 16
            S.dma("sp", (lambda e: e.dma_start(out=tb[n_ % 4], in_=tab[m, h])), writes=[R_TB[n_ % 4]], key=f"tb{n_ % 4}")

        def c_stage1(n_):
            m, h = n_ // 16, n_ % 16
            par = n_ % 2
            p3 = n_ % 3
            t4 = n_ % 4
            hc, hb = h // 2, 64 * (h % 2)

            def qk(e):
                ins = None
                rhs = qT[hb:hb + 64, hc, m * 256:(m + 1) * 256]
                ins = e.matmul(ps[0:16, 6, 0:256], lhsT=kT[hb:hb + 64, hc, 1536:1552], rhs=rhs, start=True, stop=True)
                for po in range(6):
                    kt = 2 * m + po
                    ins = e.matmul(ps[:, 3 * par + po // 2, (po % 2) * 256:(po % 2) * 256 + 256],
                                   lhsT=kT[hb:hb + 64, hc, kt * 128:(kt + 1) * 128], rhs=rhs, start=True, stop=True)
                return ins
            S.op("pe", qk, reads=[R_KT, R_QT], writes=[PS(3 * par, 3 * par + 3), PS(6)])
            S.op("act", (lambda e: e.activation(out=PTm[p3][0:16, :], in_=ps[0:16, 6, 0:256], func=AF.Exp)),
                 writes=[R_PTM[p3]], xs=[PS(6)])
            S.op("dve", (lambda e: e.tensor_tensor(out=ps[:, 3 * par:3 * par + 3, :], in0=ps[:, 3 * par:3 * par + 3, :],
                                                   in1=tb[t4].rearrange("p (a b) -> p a b", a=3), op=ALU.add)),
                 reads=[R_TB[t4]], xs=[PS(3 * par, 3 * par + 3)])
            if "P" not in opts:
                c_exp(n_)

        def c_exp(n_):
            par = n_ % 2
            S.op("act", (lambda e: e.activation(out=PT[par].rearrange("p (a b) -> p a b", a=3), in_=ps[:, 3 * par:3 * par + 3, :],
                                                func=AF.Exp)),
                 writes=[R_PT[par]], xs=[PS(3 * par, 3 * par + 3)])

        def c_stage1b(n_):
            par = n_ % 2
            S.op("act", (lambda e: e.activation(out=PT[par], in_=sb[par], func=AF.Exp)),
                 reads=[R_SB[par]], writes=[R_PT[par]])

        epi_pending = []

        def c_stage2(n_):
            m, h = n_ // 16, n_ % 16
            par = n_ % 2
            p3 = n_ % 3
            if h == 3 and epi_pending:
                epi_pending.pop(0)()

            def pv(e):
                ins = None
                for half in range(2):
                    o = ps[:, 7, half * 65:half * 65 + 65]
                    for po in range(6):
                        ins = e.matmul(o, lhsT=PT[par][:, po * 256 + half * 128:po * 256 + half * 128 + 128],
                                       rhs=V[:, 2 * m + po, h * 65:(h + 1) * 65], start=(po == 0), stop=False)
                    ins = e.matmul(o, lhsT=PTm[p3][0:16, half * 128:(half + 1) * 128], rhs=V[0:16, 12, h * 65:(h + 1) * 65],
                                   start=False, stop=True)
                return ins
            S.op("pe", pv, reads=[R_PT[par], R_PTM[p3], R_V], writes=[PS(7)])
            S.op("dve", lambda e: e.reciprocal(out=rec[:, 0:2], in_=ps[:, 7, 64:130:65]), writes=["rec"], xs=[PS(7)])
            S.op("dve", (lambda e: e.tensor_tensor(
                out=yatok[:, :, h * 64:(h + 1) * 64],
                in0=ps[:, 7, 0:130].rearrange("p (a b) -> p a b", a=2)[:, :, 0:64],
                in1=rec[:, 0:2].unsqueeze(2).to_broadcast([128, 2, 64]), op=ALU.mult)),
                reads=["rec"], writes=[R_YATOK], xs=[PS(7)])
            if h != 15:
                return
            for half in range(2):
                tt = 2 * m + half
                S.op("act", (lambda e, half=half, tt=tt: e.activation(out=yan[:, half, :], in_=yatok[:, half, :], func=AF.Square,
                                                                     accum_out=stat[:, C_SSQA + tt:C_SSQA + tt + 1])),
                     reads=[R_YATOK], writes=[R_YAN, ("stat", C_SSQA + tt)])
            rstd_ops(C_SSQA + 2 * m, C_RSA + 2 * m, 2, 1.0 / 1024)
            for half in range(2):
                tt = 2 * m + half
                S.op("dve", (lambda e, half=half, tt=tt: e.scalar_tensor_tensor(
                    out=yan[:, half, :], in0=yatok[:, half, :], scalar=stat[:, C_RSA + tt:C_RSA + tt + 1], in1=gab,
                    op0=ALU.mult, op1=ALU.mult)), reads=[R_YATOK, R_GAB, ("stat", C_RSA + tt)], writes=[R_YAN])
            def epi2(m=m):
                for half in range(2):
                    tt = 2 * m + half
                    b = 6 + half
                    pb = psbank_bf(b)

                    def tr(e, half=half, pb=pb):
                        ins = None
                        for c in range(8):
                            ins = e.transpose(out=pb[:, c * 128:(c + 1) * 128], in_=yan[:, half, c * 128:(c + 1) * 128],
                                              identity=ident[:])
                        return ins
                    S.op("pe", tr, reads=[R_YAN, "ident"], writes=[PS(b)])
                    S.op("act", (lambda e, tt=tt, pb=pb: e.copy(out=yaT[:, 0:8, tt * 128:(tt + 1) * 128],
                                                               in_=pb.rearrange("p (a b) -> p a b", a=8))),
                         writes=[R_YAT], xs=[PS(b)])
            epi_pending.append(epi2)

        if "Q" in opts:
            c_stage1(0)
            c_stage1(1)
            c_stage1b(0)
            for n_ in range(64):
                if n_ + 2 < 64:
                    c_stage1(n_ + 2)
                if n_ + 1 < 64:
                    c_stage1b(n_ + 1)
                c_stage2(n_)
        elif "P" in opts:
            for n_ in range(4):
                c_tab(n_)
            c_stage1(0)
            c_exp(0)
            c_stage1(1)
            c_exp(1)
            for n_ in range(64):
                if n_ + 4 < 64:
                    c_tab(n_ + 4)
                if n_ + 2 < 64:
                    c_stage1(n_ + 2)
                c_stage2(n_)
                if n_ + 2 < 64:
                    c_exp(n_ + 2)
            while epi_pending:
                epi_pending.pop(0)()
        else:
            for n_ in range(3):
                c_tab(n_)
            c_stage1(0)
            for n_ in range(64):
                if n_ + 3 < 64:
                    c_tab(n_ + 3)
                if n_ + 1 < 64:
                    c_stage1(n_ + 1)
                c_stage2(n_)

        if dbg:
            S.dma("sp", lambda e: e.dma_start(out=dbg_out["d_yaT"][:, :], in_=arena[:, OFF_YAT // 2:OFF_YAT // 2 + 8192]),
                  reads=[R_YAT], writes=["dbg5"], key="dbg5")
            S.final_keys.append("dbg5")

        for tt in range(8):
            S.dma("sp", (lambda e, tt=tt: e.dma_start(out=resid[:, tt, :], in_=xl[OWN0 + tt * 128:OWN0 + (tt + 1) * 128, :])),
                  writes=[R_RES(tt)], key=f"res{tt}")
        g2b = view(OFF_G2B, F32, [2048])
        R_G2B = AR(OFF_G2B, 8192)
        S.dma("sp", lambda e: e.dma_start(out=g2b, in_=g2b_d[:, :]), writes=[R_G2B], key="g2b")
        xn2 = [view(OFF_XN2 + i * 4096, BF16, [2048]) for i in range(2)]
        R_XN2 = [AR(OFF_XN2 + i * 4096, 4096) for i in range(2)]
        junk2 = view(OFF_JUNK2, BF16, [2048])
        R_JUNK2 = AR(OFF_JUNK2, 4096)

        def e_norm_a(tt):
            norm_a(resid[:, tt, :], R_RES(tt), junk2, R_JUNK2, C_SSQ2, C_RS2, tt)

        def e_norm_b(tt):
            norm_b(resid[:, tt, :], R_RES(tt), xn2[tt % 2], R_XN2[tt % 2], g2b, R_G2B, C_RS2, tt)

        def e_transpose(tt):
            regs = [("h2t", tt)] + [AR(OFF_H2T + (c * 1024 + tt * 128) * 2, 256) for c in range(16)]
            transpose_stage(xn2[tt % 2], R_XN2[tt % 2], h2T, regs, tt * 128)

        tasks = []
        for cbk in range(4):
            def ld(cbk=cbk):
                return w_load(w_out_v, 0, 16, cbk * 512, 512)

            def cp(h, cbk=cbk):
                wv, wreg = h
                for tt in range(8):
                    ba = next_bank(8)
                    bb = next_bank(8)

                    def mm(e, tt=tt, ba=ba, bb=bb):
                        ins = None
                        for kc in range(8):
                            ins = e.matmul(ps[:, ba, :], lhsT=ycT[:, kc, tt * 128:(tt + 1) * 128], rhs=wv[:, kc, :],
                                           start=(kc == 0), stop=(kc == 7))
                        for kc in range(8):
                            ins = e.matmul(ps[:, bb, :], lhsT=yaT[:, kc, tt * 128:(tt + 1) * 128], rhs=wv[:, 8 + kc, :],
                                           start=(kc == 0), stop=(kc == 7))
                        return ins
                    S.op("pe", mm, reads=[wreg, R_YCT, R_YAT], writes=[PS(ba), PS(bb)])
                    rs = resid[:, tt, cbk * 512:(cbk + 1) * 512]
                    S.op("dve", (lambda e, rs=rs, tt=tt, ba=ba: e.scalar_tensor_tensor(
                        out=rs, in0=ps[:, ba, :], scalar=stat[:, C_RSC + tt:C_RSC + tt + 1], in1=rs, op0=ALU.mult, op1=ALU.add)),
                        reads=[("stat", C_RSC + tt)], writes=[R_RES(tt)], xs=[PS(ba)])
                    S.op("dve", (lambda e, rs=rs, bb=bb: e.tensor_tensor(out=rs, in0=rs, in1=ps[:, bb, :], op=ALU.add)),
                         writes=[R_RES(tt)], xs=[PS(bb)])
                    if cbk == 3 and "E" in opts:
                        e_norm_a(tt)
                        if tt >= 1:
                            e_norm_b(tt - 1)
                        if tt >= 2:
                            e_transpose(tt - 2)
                if cbk == 3 and "E" in opts:
                    e_norm_b(7)
                    e_transpose(6)
                    e_transpose(7)
            tasks.append((ld, cp))
        run_tasks(tasks)
        if "E" not in opts:
            for tt in range(8):
                e_norm_a(tt)
                e_norm_b(tt)
                e_transpose(tt)

        gfb = view(OFF_GFB, F32, [2048])
        R_GFB = AR(OFF_GFB, 8192)
        S.dma("sp", lambda e: e.dma_start(out=gfb, in_=gfb_d[:, :]), writes=[R_GFB], key="gfb")

        if dbg:
            S.dma("sp", lambda e: e.dma_start(out=dbg_out["d_res"][:, :], in_=arena[:, 0:32768].bitcast(F32)),
                  reads=[AR(0, 65536)], writes=["dbg6"], key="dbg6")
            S.final_keys.append("dbg6")
            S.dma("sp", lambda e: e.dma_start(out=dbg_out["d_h2T"][:, :], in_=arena[:, OFF_H2T // 2:OFF_H2T // 2 + 16384]),
                  reads=[R_H2T, "h2t"], writes=["dbg7"], key="dbg7")
            S.final_keys.append("dbg7")

        rt = [view(OFF_RT + i * 2048, F32, [512]) for i in range(4)]
        R_RT = [AR(OFF_RT + i * 2048, 2048) for i in range(4)]
        rt_rr = {"i": 0}
        up_tasks, dn_tasks = [], []
        for p in range(8):
            pp = p % 2
            for j in range(2):
                def ld(p=p, j=j):
                    return w_load(w_up_v, 0, 16, p * 1024 + j * 512, 512)

                def cp(h, p=p, j=j, pp=pp):
                    wv, wreg = h
                    for fl in range(4):
                        f = j * 4 + fl
                        for blk in range(2):
                            b = next_bank(8)

                            def mm(e, b=b, fl=fl, blk=blk):
                                ins = None
                                for kc in range(16):
                                    ins = e.matmul(ps[:, b, :], lhsT=wv[:, kc, fl * 128:(fl + 1) * 128],
                                                   rhs=h2T[:, kc, blk * 512:(blk + 1) * 512], start=(kc == 0), stop=(kc == 15))
                                return ins
                            S.op("pe", mm, reads=[wreg, ("h2t", 4 * blk, 4 * blk + 4)], writes=[PS(b)])
                            ri = rt_rr["i"] % 4
                            rt_rr["i"] += 1
                            S.op("act", (lambda e, b=b, ri=ri: e.activation(out=rt[ri], in_=ps[:, b, :], func=AF.Relu)),
                                 writes=[R_RT[ri]], xs=[PS(b)])
                            S.op("dve", (lambda e, ri=ri, f=f, blk=blk, pp=pp: e.tensor_tensor(
                                out=hid[pp][:, f, blk * 512:(blk + 1) * 512], in0=rt[ri], in1=rt[ri], op=ALU.mult)),
                                reads=[R_RT[ri]], writes=[R_HID[pp]])
                up_tasks.append((ld, cp))
            for j in range(2):
                def ld(p=p, j=j):
                    return w_load(w_down_v, p * 8, 8, j * 1024, 1024)

                def cp(h, p=p, j=j, pp=pp):
                    wv, wreg = h
                    for tt in range(8):
                        for cbl in range(2):
                            b = next_bank(8)

                            def mm(e, b=b, tt=tt, cbl=cbl):
                                ins = None
                                for kc in range(8):
                                    ins = e.matmul(ps[:, b, :], lhsT=hid[pp][:, kc, tt * 128:(tt + 1) * 128],
                                                   rhs=wv[:, kc, cbl * 512:(cbl + 1) * 512], start=(kc == 0), stop=(kc == 7))
                                return ins
                            S.op("pe", mm, reads=[wreg, R_HID[pp]], writes=[PS(b)])
                            c0 = j * 1024 + cbl * 512
                            rs = resid[:, tt, c0:c0 + 512]
                            S.op("dve", (lambda e, rs=rs, b=b: e.tensor_tensor(out=rs, in0=rs, in1=ps[:, b, :], op=ALU.add)),
                                 writes=[R_RES(tt)], xs=[PS(b)])
                        if p == 7 and j == 1 and "G" in opts:
                            g_norm_a(tt)
                            if tt >= 1:
                                g_norm_b(tt - 1)
                    if p == 7 and j == 1 and "G" in opts:
                        g_norm_b(7)
                dn_tasks.append((ld, cp))

        junkg = view(OFF_JUNKG, BF16, [2048])
        R_JUNKG = AR(OFF_JUNKG, 4096)

        def g_norm_a(tt):
            norm_a(resid[:, tt, :], R_RES(tt), junkg, R_JUNKG, C_SSQF, C_RSF, tt)

        def g_norm_b(tt):
            norm_b(resid[:, tt, :], R_RES(tt), resid[:, tt, :], R_RES(tt), gfb, R_GFB, C_RSF, tt)
            S.dma("sp", (lambda e: e.dma_start(out=out_d[tt * 128:(tt + 1) * 128, :], in_=resid[:, tt, :])),
                  reads=[R_RES(tt)], writes=[("outd", tt)], key=f"out{tt % 2}")

        tasks = up_tasks[0:2]
        for p in range(8):
            if p + 1 < 8:
                tasks += up_tasks[2 * (p + 1):2 * (p + 1) + 2]
            tasks += dn_tasks[2 * p:2 * p + 2]
        run_tasks(tasks)
        if "G" not in opts:
            for tt in range(8):
                g_norm_a(tt)
                g_norm_b(tt)
        S.final_keys += ["out0", "out1"]
        if dbg:
            S.dma("sp", lambda e: e.dma_start(out=dbg_out["d_stat"][:, :], in_=stat[:]), reads=["stat"], writes=["dbg8"], key="dbg8")
            S.final_keys.append("dbg8")
        S.emit(nc)
    return nc


def _bias_table(rpb, i):
    R0 = 16 * i
    m = np.arange(4)[:, None, None, None, None]
    kp = np.arange(128)[None, :, None, None, None]
    po = np.arange(6)[None, None, :, None, None]
    a = np.arange(4)[None, None, None, :, None]
    qc = np.arange(64)[None, None, None, None, :]
    kb, kc = kp // 64, kp % 64
    gk = R0 - 4 + 4 * m + 2 * po + kb
    r = R0 + 4 * m + a
    rs = np.clip(r - 4, 0, 56)
    c0 = np.clip(qc - 8, 0, 48)
    valid = (gk >= 0) & (gk < 64) & (gk >= rs) & (gk < rs + 8) & (kc >= c0) & (kc < c0 + 16)
    dr = np.clip(gk - r + 7, 0, 14)
    dc = np.clip(kc - qc + 15, 0, 30)
    shape = np.broadcast_shapes(valid.shape, dr.shape, dc.shape)
    valid = np.broadcast_to(valid, shape)
    dr = np.broadcast_to(dr, shape)
    dc = np.broadcast_to(dc, shape)
    t = rpb[:, dr, dc]
    t = np.where(valid[None], t, np.float32(NEG)).astype(np.float32)
    t = np.ascontiguousarray(t.transpose(1, 0, 2, 3, 4, 5)).reshape(4, 16, 128, 1536)
    return t


def _prep_inputs(x, meta_tokens, norm1_g, w_in, conv_w, conv_b, conv_norm_g, attn_rpb, attn_norm_g, w_out, norm2_g,
                 w_up, w_down, final_norm_g):
    import ml_dtypes
    f32 = np.float32
    x = np.asarray(x, f32)
    meta = np.asarray(meta_tokens, f32)
    shared = {
        "w_in": np.ascontiguousarray(np.asarray(w_in, f32)[0]),
        "w_out": np.ascontiguousarray(np.asarray(w_out, f32)[0]),
        "w_up": np.ascontiguousarray(np.asarray(w_up, f32)[0]),
        "w_down": np.ascontiguousarray(np.asarray(w_down, f32)[0]),
        "g1b": np.ascontiguousarray(np.broadcast_to(np.asarray(norm1_g, f32)[0][None, :], (128, 2048))),
        "g2b": np.ascontiguousarray(np.broadcast_to(np.asarray(norm2_g, f32)[0][None, :], (128, 2048))),
        "gfb": np.ascontiguousarray(np.broadcast_to(np.asarray(final_norm_g, f32)[None, :], (128, 2048))),
        "gab": np.ascontiguousarray(np.broadcast_to(np.asarray(attn_norm_g, f32)[0][None, :], (128, 1024))),
        "ident": np.eye(128, dtype=f32).astype(ml_dtypes.bfloat16),
    }
    cols = np.zeros((128, 48), f32)
    cols[:, 0:8] = np.asarray(conv_norm_g, f32)[0].reshape(8, 128).T
    cw = np.asarray(conv_w, f32)[0]
    cols[:, 8:32] = cw.reshape(3, 8, 128).transpose(2, 1, 0).reshape(128, 24)
    cols[:, 32:40] = np.asarray(conv_b, f32)[0].reshape(8, 128).T
    shared["cols"] = cols
    rpb = np.asarray(attn_rpb, f32)[0]
    tabs = [_bias_table(rpb, i) for i in range(4)]
    in_maps = []
    for c in range(8):
        b, i = c // 4, c % 4
        xl = np.zeros((TOK, D_MODEL), f32)
        g0 = 16 * i - 4
        lo, hi = max(g0, 0), min(g0 + 24, 64)
        xl[(lo - g0) * 64:(hi - g0) * 64] = x[b, lo * 64:hi * 64]
        if i == 0:
            xl[255] = meta[15]
        xl[1536:1552] = meta
        mp = dict(shared)
        mp["xl"] = xl
        mp["tab"] = tabs[i]
        in_maps.append(mp)
    return in_maps


_NC = {}


def kernel(**inputs):
    in_maps = _prep_inputs(**inputs)
    if "nc" not in _NC:
        _NC["nc"] = build_nc()
    res = run_bass_kernel_spmd(_NC["nc"], in_maps, core_ids=list(range(8)))
    out = np.empty((2, SEQ, D_MODEL), np.float32)
    for c in range(8):
        b, i = c // 4, c % 4
        out[b, i * 1024:(i + 1) * 1024] = np.asarray(res.results[c]["out"], np.float32)
    return out
```
